# Optimizing a Trainium2 kernel written in Bass

```python
import math, functools
import jax, jax.numpy as jnp
from jax import lax
import numpy as np

D_MODEL = 2048
BATCH = 4
SEQ = 2048
DEPTH = 1
DEC_BATCH = 128
DEC_SEQ = 8
PAST_LEN = 16384
PAGE_SIZE = 128

GLA_HEADS = 4
GLA_DK = D_MODEL // 2 // GLA_HEADS
GLA_DV = D_MODEL // GLA_HEADS
GLA_RANK = 16
GLA_TAU = 16.0
GDN_HEADS = 16
GDN_DK = D_MODEL // GDN_HEADS
GDN_DV = D_MODEL // GDN_HEADS
CONV_W = 4
CHUNK = 64
EPS = 1e-6

GLA_QK = GLA_HEADS * GLA_DK
GLA_V = GLA_HEADS * GLA_DV
GDN_QK = GDN_HEADS * GDN_DK
GDN_V = GDN_HEADS * GDN_DV
CONV_CH = 2 * GDN_QK + GDN_V
SPLIT_SIZES = (GLA_QK, GLA_QK, GLA_V, GLA_RANK, GLA_V,
               CONV_CH, GDN_HEADS, GDN_HEADS, GDN_V,
               D_MODEL, D_MODEL)
SPLIT_POINTS = tuple(int(s) for s in np.cumsum(SPLIT_SIZES)[:-1])
D_IN = int(sum(SPLIT_SIZES))

kernel_name = "hybrid_gla_gdn_parallel_gated_step"


def _rmsnorm(x, g):
    xf = x.astype(jnp.float32)
    inv = lax.rsqrt(jnp.mean(xf * xf, axis=-1, keepdims=True) + EPS)
    return (xf * inv * g.astype(jnp.float32)).astype(x.dtype)


def _l2norm(x):
    xf = x.astype(jnp.float32)
    return xf * lax.rsqrt(jnp.sum(xf * xf, axis=-1, keepdims=True) + EPS)


def _to_chunks(a, c, n):
    t = a.shape[1]
    a = jnp.pad(a, [(0, 0), (0, n * c - t)] + [(0, 0)] * (a.ndim - 2))
    a = a.reshape((a.shape[0], n, c) + a.shape[2:])
    return jnp.moveaxis(a, 1, 0)


def _from_chunks(o, t):
    o = jnp.moveaxis(o, 0, 1)
    o = o.reshape((o.shape[0], o.shape[1] * o.shape[2]) + o.shape[3:])
    return o[:, :t]


def _gla_chunked(q, k, v, log_a, s0):
    t = q.shape[1]
    c = min(CHUNK, t)
    n = -(-t // c)
    xs = tuple(_to_chunks(a, c, n) for a in (q, k, v, log_a))
    mask = jnp.tril(jnp.ones((c, c), dtype=bool))

    def step(s, inp):
        qc, kc, vc, lac = inp
        b = jnp.cumsum(lac, axis=1)
        b_last = b[:, -1]
        q_d = qc * jnp.exp(b)
        k_d = kc * jnp.exp(-b)
        att = jnp.where(mask, jnp.einsum('bihd,bjhd->bhij', q_d, k_d), 0.0)
        o = (jnp.einsum('bhij,bjhv->bihv', att, vc)
             + jnp.einsum('bihd,bhdv->bihv', q_d, s))
        k_end = kc * jnp.exp(b_last[:, None] - b)
        s = s * jnp.exp(b_last)[..., None] + jnp.einsum('bjhd,bjhv->bhdv', k_end, vc)
        return s, o

    s, o = lax.scan(step, s0, xs)
    return _from_chunks(o, t), s


def _gdn_chunked(q, k, v, g, beta, s0):
    t = q.shape[1]
    c = min(CHUNK, t)
    n = -(-t // c)
    xs = tuple(_to_chunks(a, c, n) for a in (q, k, v, g, beta))
    incl = jnp.tril(jnp.ones((c, c), dtype=bool))
    strict = jnp.tril(jnp.ones((c, c), dtype=bool), k=-1)
    eye = jnp.eye(c, dtype=jnp.float32)

    def step(s, inp):
        qc, kc, vc, gc, bc = inp
        qh = jnp.moveaxis(qc, 1, 2)
        kh = jnp.moveaxis(kc, 1, 2)
        vh = jnp.moveaxis(vc, 1, 2)
        gh = jnp.cumsum(jnp.moveaxis(gc, 1, 2), axis=-1)
        bh = jnp.moveaxis(bc, 1, 2)
        diff = gh[..., :, None] - gh[..., None, :]
        decay = jnp.exp(jnp.where(incl, diff, -jnp.inf))
        kk = jnp.einsum('bhid,bhjd->bhij', kh, kh)
        m = jnp.where(strict, bh[..., :, None] * kk * decay, 0.0)
        rhs = jnp.concatenate([vh * bh[..., None],
                               kh * (bh * jnp.exp(gh))[..., None]], axis=-1)
        sol = lax.linalg.triangular_solve(eye + m, rhs, left_side=True, lower=True,
                                          unit_diagonal=True)
        u, w = sol[..., :GDN_DV], sol[..., GDN_DV:]
        v_new = u - jnp.einsum('bhcd,bhdv->bhcv', w, s)
        qk = jnp.einsum('bhid,bhjd->bhij', qh, kh) * decay
        o = (jnp.einsum('bhid,bhdv->bhiv', qh * jnp.exp(gh)[..., None], s)
             + jnp.einsum('bhij,bhjv->bhiv', qk, v_new))
        g_last = gh[..., -1]
        k_end = kh * jnp.exp(g_last[..., None] - gh)[..., None]
        s = s * jnp.exp(g_last)[..., None, None] + jnp.einsum('bhcd,bhcv->bhdv', k_end, v_new)
        return s, jnp.moveaxis(o, 1, 2)

    s, o = lax.scan(step, s0, xs)
    return _from_chunks(o, t), s


def _causal_conv(u, buf, w):
    t = u.shape[1]
    ext = jnp.concatenate([buf.astype(u.dtype), u], axis=1)
    out = sum(ext[:, i:i + t] * w[i] for i in range(CONV_W))
    return out, ext[:, -(CONV_W - 1):]


def _layer(x, s_gla, s_gdn, conv_buf, ln_g, w_in, w_alpha2, b_alpha, conv_w, a_log,
           dt_bias, gla_norm_g, gdn_norm_g, w_br_a, w_br_b, w_out):
    bsz, t = x.shape[0], x.shape[1]
    f32 = jnp.float32
    h = _rmsnorm(x, ln_g)
    proj = h @ w_in
    (q_a, k_a, v_a, lr_a, gate_a, qkv_b, beta_b, dec_b, gate_b, m_a, m_b) = jnp.split(
        proj, SPLIT_POINTS, axis=-1)

    q_a = q_a.reshape(bsz, t, GLA_HEADS, GLA_DK).astype(f32) * (GLA_DK ** -0.5)
    k_a = k_a.reshape(bsz, t, GLA_HEADS, GLA_DK).astype(f32)
    v_a = v_a.reshape(bsz, t, GLA_HEADS, GLA_DV).astype(f32)
    log_alpha = jax.nn.log_sigmoid((lr_a @ w_alpha2 + b_alpha).astype(f32)) / GLA_TAU
    log_alpha = log_alpha.reshape(bsz, t, GLA_HEADS, GLA_DK)
    o_a, s_gla_new = _gla_chunked(q_a, k_a, v_a, log_alpha, s_gla.astype(f32))
    o_a = _rmsnorm(o_a, gla_norm_g).reshape(bsz, t, GLA_V).astype(x.dtype) * jax.nn.silu(gate_a)
    y_a = o_a @ w_br_a

    qkv, conv_new = _causal_conv(qkv_b, conv_buf, conv_w)
    qkv = jax.nn.silu(qkv)
    q_b, k_b, v_b = jnp.split(qkv, [GDN_QK, 2 * GDN_QK], axis=-1)
    q_b = _l2norm(q_b.reshape(bsz, t, GDN_HEADS, GDN_DK)) * (GDN_DK ** -0.5)
    k_b = _l2norm(k_b.reshape(bsz, t, GDN_HEADS, GDN_DK))
    v_b = v_b.reshape(bsz, t, GDN_HEADS, GDN_DV).astype(f32)
    beta = jax.nn.sigmoid(beta_b.astype(f32))
    g = -jnp.exp(a_log.astype(f32)) * jax.nn.softplus(dec_b.astype(f32) + dt_bias.astype(f32))
    o_b, s_gdn_new = _gdn_chunked(q_b, k_b, v_b, g, beta, s_gdn.astype(f32))
    o_b = _rmsnorm(o_b, gdn_norm_g).reshape(bsz, t, GDN_V).astype(x.dtype) * jax.nn.silu(gate_b)
    y_b = o_b @ w_br_b

    merged = jax.nn.sigmoid(m_a) * y_a + jax.nn.sigmoid(m_b) * y_b
    out = x + merged @ w_out
    return out, s_gla_new.astype(x.dtype), s_gdn_new.astype(x.dtype), conv_new.astype(x.dtype)


def setup_inputs(seed: int = 0) -> dict:
    key = jax.random.key(seed)
    ks = jax.random.split(key, 20)
    f = jnp.float32
    x_prompt = jax.random.normal(ks[0], (BATCH, SEQ, D_MODEL), f)
    x_sample = jax.random.normal(ks[1], (DEC_BATCH, DEC_SEQ, D_MODEL), f)
    state_gla = jax.random.normal(ks[2], (DEPTH, DEC_BATCH, GLA_HEADS, GLA_DK, GLA_DV), f)
    state_gdn = 0.5 * jax.random.normal(ks[3], (DEPTH, DEC_BATCH, GDN_HEADS, GDN_DK, GDN_DV), f)
    state_conv = jax.random.normal(ks[4], (DEPTH, DEC_BATCH, CONV_W - 1, CONV_CH), f)
    ln_in_g = 1.0 + 0.02 * jax.random.normal(ks[5], (DEPTH, D_MODEL), f)
    w_in = jax.random.normal(ks[6], (DEPTH, D_MODEL, D_IN), f) * D_MODEL ** -0.5
    w_alpha2 = jax.random.normal(ks[7], (DEPTH, GLA_RANK, GLA_QK), f) * GLA_RANK ** -0.5
    b_alpha = 0.1 * jax.random.normal(ks[8], (DEPTH, GLA_QK), f)
    conv_w = jax.random.normal(ks[9], (DEPTH, CONV_W, CONV_CH), f) * CONV_W ** -0.5
    a_log = jnp.log(jax.random.uniform(ks[10], (DEPTH, GDN_HEADS), f, 1.0, 16.0))
    dt = jnp.exp(jax.random.uniform(ks[11], (DEPTH, GDN_HEADS), f,
                                    math.log(1e-3), math.log(1e-1)))
    dt_bias = dt + jnp.log(-jnp.expm1(-dt))
    gla_norm_g = 1.0 + 0.02 * jax.random.normal(ks[12], (DEPTH, GLA_DV), f)
    gdn_norm_g = 1.0 + 0.02 * jax.random.normal(ks[13], (DEPTH, GDN_DV), f)
    w_br_a = jax.random.normal(ks[14], (DEPTH, GLA_V, D_MODEL), f) * GLA_V ** -0.5
    w_br_b = jax.random.normal(ks[15], (DEPTH, GDN_V, D_MODEL), f) * GDN_V ** -0.5
    w_out = jax.random.normal(ks[16], (DEPTH, D_MODEL, D_MODEL), f) * D_MODEL ** -0.5
    final_norm_g = 1.0 + 0.02 * jax.random.normal(ks[17], (D_MODEL,), f)
    return {"x_prompt": x_prompt, "x_sample": x_sample,
            "state_gla": state_gla, "state_gdn": state_gdn, "state_conv": state_conv,
            "ln_in_g": ln_in_g, "w_in": w_in, "w_alpha2": w_alpha2, "b_alpha": b_alpha,
            "conv_w": conv_w, "a_log": a_log, "dt_bias": dt_bias,
            "gla_norm_g": gla_norm_g, "gdn_norm_g": gdn_norm_g,
            "w_br_a": w_br_a, "w_br_b": w_br_b, "w_out": w_out,
            "final_norm_g": final_norm_g}


def reference(x_prompt, x_sample, state_gla, state_gdn, state_conv, ln_in_g, w_in, w_alpha2,
              b_alpha, conv_w, a_log, dt_bias, gla_norm_g, gdn_norm_g, w_br_a, w_br_b, w_out,
              final_norm_g):
    bp = x_prompt.shape[0]
    dt_ = x_prompt.dtype
    hp, hs = x_prompt, x_sample
    gla_p, gdn_p, conv_p, gla_s, gdn_s, conv_s = [], [], [], [], [], []
    for l in range(DEPTH):
        params = (ln_in_g[l], w_in[l], w_alpha2[l], b_alpha[l], conv_w[l], a_log[l], dt_bias[l],
                  gla_norm_g[l], gdn_norm_g[l], w_br_a[l], w_br_b[l], w_out[l])
        z_gla = jnp.zeros((bp, GLA_HEADS, GLA_DK, GLA_DV), dt_)
        z_gdn = jnp.zeros((bp, GDN_HEADS, GDN_DK, GDN_DV), dt_)
        z_conv = jnp.zeros((bp, CONV_W - 1, CONV_CH), dt_)
        hp, sg, sd, sc = _layer(hp, z_gla, z_gdn, z_conv, *params)
        gla_p.append(sg); gdn_p.append(sd); conv_p.append(sc)
        hs, sg, sd, sc = _layer(hs, state_gla[l], state_gdn[l], state_conv[l], *params)
        gla_s.append(sg); gdn_s.append(sd); conv_s.append(sc)
    y_prompt = _rmsnorm(hp, final_norm_g)
    y_sample = _rmsnorm(hs, final_norm_g)
    new_gla_prompt = jnp.stack(gla_p)
    new_gdn_prompt = jnp.stack(gdn_p)
    new_conv_prompt = jnp.stack(conv_p)
    new_gla_sample = jnp.stack(gla_s)
    new_gdn_sample = jnp.stack(gdn_s)
    new_conv_sample = jnp.stack(conv_s)
    return (y_prompt, y_sample, new_gla_prompt, new_gdn_prompt, new_conv_prompt,
            new_gla_sample, new_gdn_sample, new_conv_sample)
```

```python
import contextlib
import os
import numpy as np
import concourse.bass as bass
import concourse.mybir as mybir
from concourse.bass_utils import run_bass_kernel_spmd

F32 = mybir.dt.float32
BF16 = mybir.dt.bfloat16
AF = mybir.ActivationFunctionType
ALU = mybir.AluOpType

D = 2048
KT = 16
NTP = 16
NT = 17
TOK = NT * 128
NSEQ = 16
DIN = 18480
C_QA, C_KA, C_VA, C_LR, C_GA, C_QKV, C_BD, C_GB, C_MA, C_MB = 0, 1024, 2048, 4096, 4112, 6160, 12304, 12336, 14384, 16432
EPS = 1e-6
BIG = 3.0e4
NLEV = 7


class Bank:
    def __init__(self):
        self.reads = {}


class Buf:
    def __init__(self, t, name, bank=None):
        self.t = t
        self.name = name
        self.last_write = None
        self.reads = []
        self.bank = bank

    def __getitem__(self, idx):
        return V(self.t[idx], [self])

    def ap(self):
        return V(self.t.ap() if hasattr(self.t, "ap") and callable(getattr(self.t, "ap")) else self.t, [self])


class SubBuf:
    def __init__(self, parent, ap, name):
        self.parent = parent
        self.t = ap
        self.name = name

    def __getitem__(self, idx):
        return V(self.t[idx], [self.parent])


class V:
    def __init__(self, ap, keys):
        self.ap = ap
        self.keys = keys

    def __getitem__(self, idx):
        return V(self.ap[idx], self.keys)

    def re(self, pattern, **kw):
        return V(self.ap.rearrange(pattern, **kw), self.keys)

    def bc(self, shape):
        return V(self.ap.to_broadcast(list(shape)), self.keys)

    def un(self, axis):
        return V(self.ap.unsqueeze(axis), self.keys)

    def bitcast(self, dt):
        return V(self.ap.bitcast(dt), self.keys)


def _ap(x):
    return x.ap if isinstance(x, V) else x


class Prog:
    def __init__(self, nc):
        self.nc = nc
        self.stack = contextlib.ExitStack()
        self.engs = ["tensor", "vector", "scalar", "gpsimd", "sync"]
        self.ops = {e: [] for e in self.engs}
        self.sem = {}
        self.cnt = {e: 0 for e in self.engs}
        self.seen = {e: {} for e in self.engs}
        for e in self.engs:
            self.sem[e] = self.stack.enter_context(nc.semaphore("s_" + e))
        self.dma_sems = {}
        self.n_wait = 0
        self.n_ops = 0
        self.scopes = [self.stack]
        self.all_ops = {e: [] for e in self.engs}

    def push(self):
        self.scopes.append(contextlib.ExitStack())

    def pop(self):
        self.flush()
        self.scopes.pop().close()

    def sb(self, name, shape, dt):
        t = self.scopes[-1].enter_context(self.nc.sbuf_tensor(name, list(shape), dt))
        return Buf(t, name)

    def ps(self, name, shape, dt=F32):
        t = self.scopes[-1].enter_context(self.nc.psum_tensor(name, list(shape), dt))
        return Buf(t, name, bank=Bank())

    def dram(self, name, shape, dt, kind="Internal"):
        t = self.nc.dram_tensor(name, list(shape), dt, kind=kind)
        return Buf(t, name)

    def _collect(self, eng, ins, outs, same_ok):
        need = {}

        def add(sig):
            if sig is None:
                return
            sem, val, seng = sig
            if seng == eng and same_ok:
                return
            k = id(sem)
            if k not in need or need[k][1] < val:
                need[k] = (sem, val)
        for v in ins:
            if isinstance(v, V):
                for key in v.keys:
                    add(key.last_write)
                    if key.bank is not None:
                        for oe, sig in key.bank.reads.items():
                            if oe != eng:
                                add(sig)
        for v in outs:
            if isinstance(v, V):
                for key in v.keys:
                    add(key.last_write)
                    for r in key.reads:
                        add(r)
        waits = []
        for k, (sem, val) in need.items():
            if self.seen[eng].get(k, 0) >= val:
                continue
            self.seen[eng][k] = val
            waits.append((sem, val))
        return waits

    def _record(self, sig, ins, outs):
        for v in ins:
            if isinstance(v, V):
                for key in v.keys:
                    if key.bank is not None and sig[2] != "tensor":
                        key.bank.reads[sig[2]] = sig
                    key.reads.append(sig)
                    if len(key.reads) > 64:
                        best = {}
                        for s in key.reads:
                            if id(s[0]) not in best or best[id(s[0])][1] < s[1]:
                                best[id(s[0])] = s
                        key.reads = list(best.values())
        for v in outs:
            if isinstance(v, V):
                for key in v.keys:
                    key.last_write = sig
                    key.reads = []

    def op(self, eng, method, outs, ins, *args, same_ok=False, **kw):
        waits = self._collect(eng, ins, outs, same_ok)
        self.cnt[eng] += 1
        sig = (self.sem[eng], self.cnt[eng], eng)
        a2 = [_ap(a) for a in args]
        k2 = {k: _ap(v) for k, v in kw.items()}
        self.ops[eng].append((waits, method, a2, k2, (self.sem[eng], 1)))
        self.n_wait += len(waits)
        self.n_ops += 1
        self._record(sig, ins, outs)

    def dma(self, q, out, in_, semkey=None, **kw):
        waits = self._collect(q, [in_], [out], False)
        if semkey is None:
            semkey = out.keys[0]
        k = id(semkey)
        if k not in self.dma_sems:
            s = self.stack.enter_context(self.nc.semaphore("d_" + semkey.name))
            self.dma_sems[k] = [s, 0]
        ent = self.dma_sems[k]
        ent[1] += 16
        sig = (ent[0], ent[1], "dma")
        self.ops[q].append((waits, "dma_start", [], dict(out=_ap(out), in_=_ap(in_), **kw), (ent[0], 16)))
        self.n_wait += len(waits)
        self.n_ops += 1
        self._record(sig, [in_], [out])

    def final_wait(self, eng, bufs):
        need = {}
        for b in bufs:
            for sig in [b.last_write] + b.reads:
                if sig is None:
                    continue
                sem, val, _ = sig
                k = id(sem)
                if k not in need or need[k][1] < val:
                    need[k] = (sem, val)
        self.ops[eng].append((list(need.values()), None, None, None, None))

    def audit(self):
        final = {id(self.sem[e]): (self.cnt[e], "eng:" + e) for e in self.engs}
        for k, (sm, c) in self.dma_sems.items():
            final[id(sm)] = (c, "dma:" + sm.name)
        bad = 0
        mx = 0
        for e in self.engs:
            for (waits, method, a, k, inc) in self.all_ops[e]:
                for (sm, val) in waits:
                    mx = max(mx, val)
                    if id(sm) not in final or final[id(sm)][0] < val:
                        bad += 1
                        if bad < 10:
                            print("AUDIT: unsatisfiable wait on", e, getattr(sm, "name", sm), val, final.get(id(sm)))
        print("AUDIT: bad waits", bad, "max wait value", mx, "engine counts", dict(self.cnt),
              "max dma sem", max([c for (_, c) in self.dma_sems.values()] + [0]), "n dma sems", len(self.dma_sems))
        return bad

    def flush(self):
        for e in self.engs:
            self.all_ops[e].extend(self.ops[e])
        if any(self.ops[e] for e in self.engs):
            self.emit()
        self.ops = {e: [] for e in self.engs}

    def emit(self):
        nc = self.nc
        with nc.Block() as block:
            def mk(ename):
                def body(e):
                    for (waits, method, a, k, inc) in self.ops[ename]:
                        for (sem, val) in waits:
                            e.wait_ge(sem, val)
                        if method is None:
                            continue
                        ins = getattr(e, method)(*a, **k)
                        ins.then_inc(inc[0], inc[1])
                return body
            block.tensor(mk("tensor"))
            block.vector(mk("vector"))
            block.scalar(mk("scalar"))
            block.gpsimd(mk("gpsimd"))
            block.sync(mk("sync"))

    def close(self):
        self.stack.close()


def _const_table():
    idx = np.arange(128)
    P_, F_ = np.meshgrid(idx, idx, indexing="ij")
    segp = np.zeros_like(P_)
    segs_p, segs_f = P_ // 8, F_ // 8
    c = {}
    c["ident"] = (P_ == F_).astype(np.float32)
    c["ones"] = np.ones((128, 128), np.float32)
    c["notI"] = (P_ != F_).astype(np.float32)
    for x, same in (("p", np.ones_like(P_, bool)), ("s", segs_p == segs_f)):
        c["triT_" + x] = ((P_ <= F_) & same).astype(np.float32)
        c["sufT_" + x] = ((P_ > F_) & same).astype(np.float32)
        c["bigU_" + x] = np.where((F_ <= P_) & same, 0.0, BIG).astype(np.float32)
        c["nbigL_" + x] = np.where((F_ >= P_) & same, 0.0, -BIG).astype(np.float32)
        c["bigUs_" + x] = np.where((F_ < P_) & same, 0.0, BIG).astype(np.float32)
    lm = np.zeros((128, NLEV, 128), np.float32)
    for l in range(NLEV):
        s = 1 << l
        lm[:, l, :] = ((P_ // (2 * s) == F_ // (2 * s)) & (P_ % (2 * s) >= s) & (F_ % (2 * s) < s))
    c["lmask"] = lm.reshape(128, NLEV * 128)
    segF = np.zeros((128, NSEQ, 128), np.float32)
    for s in range(NSEQ):
        segF[:, s, 8 * s:8 * s + 8] = 1.0
    c["segF"] = segF.reshape(128, NSEQ * 128)
    c["segP"] = (idx[:, None] // 8 == np.arange(NSEQ)[None, :]).astype(np.float32)
    c["selL_s"] = (idx[:, None] == (8 * np.arange(NSEQ)[None, :] + 7)).astype(np.float32)
    c["selL_p"] = (idx[:, None] == 127).astype(np.float32)
    offs = {}
    cols = []
    o = 0
    for k, v in c.items():
        offs[k] = (o, v.shape[1])
        cols.append(v)
        o += v.shape[1]
    return np.ascontiguousarray(np.concatenate(cols, axis=1)), offs


_CONST, _COFF = _const_table()
NCONST = _CONST.shape[1]


def build_nc(upto=99, debug=False):
    nc = bass.Bass("TRN2", target_bir_lowering=False)
    P = Prog(nc)

    def din(name, shape, dt=F32):
        return Buf(nc.dram_tensor(name, list(shape), dt, kind="ExternalInput"), name)

    def dout(name, shape, dt=F32):
        return Buf(nc.dram_tensor(name, list(shape), dt, kind="ExternalOutput"), name)

    def dscr(name, shape, dt=F32):
        if debug:
            return dout(name, shape, dt)
        return P.dram(name, shape, dt)

    x_d = din("x", [TOK, D])
    w_in_d = din("w_in", [D, DIN])
    w2e_d = din("w2ext", [17, 1024])
    cwT_d = din("cwT", [128, 48, 4])
    alog_d = din("a_log", [1, 16])
    dtb_d = din("dt_bias", [1, 16])
    lng_d = din("ln_in_g", [1, D])
    fng_d = din("final_norm_g", [1, D])
    gna_d = din("gla_norm_g", [1, 512])
    gnb_d = din("gdn_norm_g", [1, 128])
    wba_d = din("w_br_a", [D, D])
    wbb_d = din("w_br_b", [D, D])
    wo_d = din("w_out", [D, D])
    sgla_d = din("state_gla", [NSEQ, 4, 256, 512])
    sgdn_d = din("state_gdn", [NSEQ, 16, 128, 128])
    sconv_d = din("state_conv", [NSEQ, 3, 6144])
    const_d = din("consts", [128, NCONST])

    y_d = dout("y", [TOK, D])
    glap_d = dout("gla_p", [4, 256, 512])
    gdnp_d = dout("gdn_p", [16, 128, 128])
    convp_d = dout("conv_p", [3, 6144])
    glas_d = dout("gla_s", [NSEQ, 4, 256, 512])
    gdns_d = dout("gdn_s", [NSEQ, 16, 128, 128])
    convs_d = dout("conv_s", [NSEQ, 3, 6144])
    outs_all = [y_d, glap_d, gdnp_d, convp_d, glas_d, gdns_d, convs_d]

    s_ka_tm = dscr("s_ka_tm", [TOK, 1024])
    s_va = dscr("s_va", [TOK, 2048], BF16)
    s_ga = dscr("s_ga", [TOK, 2048])
    s_bd = dscr("s_bd", [TOK, 32])
    s_gb = dscr("s_gb", [TOK, 2048])
    s_ma = dscr("s_ma", [TOK, 2048])
    s_mb = dscr("s_mb", [TOK, 2048])
    s_qkv_tm = dscr("s_qkv_tm", [256, 6144])
    s_qa_fm = dscr("s_qa_fm", [1024, TOK])
    s_ka_fm = dscr("s_ka_fm", [1024, TOK])
    s_lr_fm = dscr("s_lr_fm", [16, TOK])
    s_qkv_fm = dscr("s_qkv_fm", [6144, TOK])
    s_oaT = dscr("s_oaT", [2048, TOK], BF16)
    s_obT = dscr("s_obT", [2048, TOK], BF16)
    dbg = [s_ka_tm, s_va, s_ga, s_bd, s_gb, s_ma, s_mb, s_qkv_tm, s_qa_fm, s_ka_fm, s_lr_fm, s_qkv_fm, s_oaT, s_obT]

    cst = P.sb("cst", [128, NCONST], F32)
    P.dma("sync", cst[:], const_d[:])

    def C(name):
        o, n = _COFF[name]
        return cst[:, o:o + n]

    ident_bf = P.sb("ident_bf", [128, 128], BF16)
    P.op("vector", "tensor_copy", [ident_bf[:]], [cst[:]], out=ident_bf[:], in_=C("ident"))

    P.push()
    gB = P.sb("gB", [128, D], F32)
    def bload(dst, src, n):
        P.dma("sync", dst, V(src.t.ap()[0:1, 0:n].partition_broadcast(128)[:, 0, :], [src]))

    bload(gB[:], lng_d, D)

    psA = [P.ps("psA%d" % i, [128, 512], F32) for i in range(4)]
    psT = [P.ps("psT%d" % i, [128, 1024], BF16) for i in range(2)]
    psM = [P.ps("psM%d" % i, [128, 512], F32) for i in range(2)]

    hT = P.sb("hT", [128, KT, TOK], BF16)
    xt = [P.sb("xt%d" % i, [128, D], F32) for i in range(2)]
    hb = [P.sb("hb%d" % i, [128, D], BF16) for i in range(2)]
    junk = P.sb("junk", [128, D], BF16)
    ss = [P.sb("ss%d" % i, [128, 1], F32) for i in range(2)]
    rs = [P.sb("rs%d" % i, [128, 1], F32) for i in range(2)]
    eps_t = P.sb("eps_t", [128, 1], F32)
    P.op("vector", "memset", [eps_t[:]], [], eps_t[:], EPS)
    for t in range(NT):
        b = t % 2
        P.dma("sync", xt[b][:], x_d[t * 128:(t + 1) * 128, :])
        P.op("scalar", "activation", [junk[:], ss[b][:]], [xt[b][:]], out=junk[:], in_=xt[b][:], func=AF.Square,
             accum_out=ss[b][:])
        P.op("scalar", "activation", [rs[b][:]], [ss[b][:], eps_t[:]], out=rs[b][:], in_=ss[b][:], func=AF.Sqrt,
             bias=eps_t[:], scale=1.0 / D)
        P.op("vector", "reciprocal", [rs[b][:]], [rs[b][:]], out=rs[b][:], in_=rs[b][:])
        P.op("vector", "scalar_tensor_tensor", [hb[b][:]], [xt[b][:], rs[b][:], gB[:]], out=hb[b][:], in0=xt[b][:],
             scalar=rs[b][:, 0:1], in1=gB[:], op0=ALU.mult, op1=ALU.mult)
        for g in range(2):
            pt = psT[g]
            for j in range(8):
                k = g * 8 + j
                P.op("tensor", "transpose", [pt[:]], [hb[b][:], ident_bf[:]], out=pt[:, j * 128:(j + 1) * 128],
                     in_=hb[b][:, k * 128:(k + 1) * 128], identity=ident_bf[:], same_ok=True)
            dst = hT[:, g * 8:(g + 1) * 8, t * 128:(t + 1) * 128]
            src = pt[:].re("p (a b) -> p a b", a=8)
            if g == 0:
                P.op("vector", "tensor_copy", [dst], [pt[:]], out=dst, in_=src)
            else:
                P.op("scalar", "copy", [dst], [pt[:]], out=dst, in_=src)

    wblk = [P.sb("wblk%d" % i, [128, KT, 512], BF16) for i in range(2)]
    stg = [P.sb("stg%d" % i, [128, 512], F32) for i in range(4)]
    stgb = [P.sb("stgb%d" % i, [128, 512], BF16) for i in range(2)]
    all_tiles = list(range(NT))
    groups = [
        (C_QA, 1024, s_qa_fm, None, None, False),
        (C_KA, 1024, s_ka_fm, s_ka_tm, all_tiles, False),
        (C_VA, 2048, None, s_va, all_tiles, True),
        (C_LR, 16, s_lr_fm, None, None, False),
        (C_GA, 2048, None, s_ga, all_tiles, False),
        (C_QKV, 6144, s_qkv_fm, s_qkv_tm, [NTP - 1, NT - 1], False),
        (C_BD, 32, None, s_bd, all_tiles, False),
        (C_GB, 2048, None, s_gb, all_tiles, False),
        (C_MA, 2048, None, s_ma, all_tiles, False),
        (C_MB, 2048, None, s_mb, all_tiles, False),
    ]
    blocks = []
    for (c0, n, fm, tm, tiles, tbf) in groups:
        for o in range(0, n, 512):
            blocks.append((c0 + o, min(512, n - o), o, fm, tm, tiles, tbf))
    w_view = w_in_d.t.ap().rearrange("(ko ki) c -> ki ko c", ki=128)
    cnt = {"ps": 0, "stg": 0, "stgb": 0, "ev": 0}

    def evac(dst, src):
        cnt["ev"] += 1
        if cnt["ev"] % 2:
            P.op("vector", "tensor_copy", [dst], [src], out=dst, in_=src)
        else:
            P.op("scalar", "copy", [dst], [src], out=dst, in_=src)

    tok_groups = [(0, 512), (512, 512), (1024, 512), (1536, 512), (2048, 128)]
    if upto >= 2:
        for bi, (cg, n, off, fm, tm, tiles, tbf) in enumerate(blocks):
            wb = wblk[bi % 2]
            for q4 in range(4):
                P.dma("gpsimd", wb[:, q4 * 4:(q4 + 1) * 4, 0:n], V(w_view[:, q4 * 4:(q4 + 1) * 4, cg:cg + n], [w_in_d]))
            if fm is not None:
                for sbo in range(0, n, 128):
                    m = min(128, n - sbo)
                    for (t0, tn) in tok_groups:
                        ps = psA[cnt["ps"] % 4]
                        cnt["ps"] += 1
                        for k in range(KT):
                            P.op("tensor", "matmul", [ps[:]], [wb[:], hT[:]], ps[0:m, 0:tn], wb[:, k, sbo:sbo + m],
                                 hT[:, k, t0:t0 + tn], start=(k == 0), stop=(k == KT - 1), same_ok=True)
                        st = stg[cnt["stg"] % 4]
                        cnt["stg"] += 1
                        evac(st[0:m, 0:tn], ps[0:m, 0:tn])
                        P.dma("sync", fm[off + sbo:off + sbo + m, t0:t0 + tn], st[0:m, 0:tn], semkey=st)
            if tm is not None:
                for t in tiles:
                    ps = psA[cnt["ps"] % 4]
                    cnt["ps"] += 1
                    for k in range(KT):
                        P.op("tensor", "matmul", [ps[:]], [wb[:], hT[:]], ps[:, 0:n], hT[:, k, t * 128:(t + 1) * 128],
                             wb[:, k, 0:n], start=(k == 0), stop=(k == KT - 1), same_ok=True)
                    if tbf:
                        st = stgb[cnt["stgb"] % 2]
                        cnt["stgb"] += 1
                    else:
                        st = stg[cnt["stg"] % 4]
                        cnt["stg"] += 1
                    evac(st[:, 0:n], ps[:, 0:n])
                    if tm is s_qkv_tm:
                        r0 = 0 if t == NTP - 1 else 128
                    else:
                        r0 = t * 128
                    P.dma("sync", tm[r0:r0 + 128, off:off + n], st[:, 0:n], semkey=st)

    P.pop()

    if upto >= 3:
        _phase_gla(P, C, locals())
    if upto >= 4:
        _phase_gdn(P, C, locals())
    if upto >= 5:
        _phase_post(P, C, locals())
    final = list(outs_all) + (dbg if debug else [])
    P.final_wait("sync", final + dbg)
    P.flush()
    P.close()
    print("ops", P.n_ops, "waits", P.n_wait)
    P.audit()
    return nc


def _phase_gla(P, C, L):
    d = L
    s_qa_fm, s_ka_fm, s_ka_tm, s_va, s_ga, s_lr_fm, s_oaT = (d[k] for k in
        ["s_qa_fm", "s_ka_fm", "s_ka_tm", "s_va", "s_ga", "s_lr_fm", "s_oaT"])
    w2e_d, gna_d, sgla_d, glap_d, glas_d, ident_bf = (d[k] for k in
        ["w2e_d", "gna_d", "sgla_d", "glap_d", "glas_d", "ident_bf"])
    P.push()
    w2e = P.sb("w2e", [17, 1024], F32)
    P.dma("sync", w2e[:], w2e_d[:])
    gnA = P.sb("gnA", [128, 512], F32)
    P.dma("sync", gnA[:], V(gna_d.t.ap()[0:1, :].partition_broadcast(128)[:, 0, :], [gna_d]))
    eps_t = P.sb("eps3", [128, 1], F32)
    P.op("vector", "memset", [eps_t[:]], [], eps_t[:], EPS)
    S = [P.sb("S%d" % h, [128, 2, 512], F32) for h in range(4)]
    Sbf = [P.sb("Sbf%d" % h, [128, 2, 512], BF16) for h in range(4)]
    for h in range(4):
        P.op("gpsimd", "memset", [S[h][:]], [], S[h][:], 0.0)
        P.op("gpsimd", "memset", [Sbf[h][:]], [], Sbf[h][:], 0.0)
    NB = 2
    lrT = [P.sb("lrT%d" % i, [17, 128], F32) for i in range(NB)]
    for i in range(NB):
        P.op("vector", "memset", [lrT[i][:]], [], lrT[i][:], 1.0)

    def mk(name, shape, dt):
        return [P.sb("%s%d" % (name, i), shape, dt) for i in range(NB)]
    qT = mk("qT", [128, 2, 128], F32); kT = mk("kT", [128, 2, 128], F32)
    ktok = mk("ktok", [128, 256], F32); vv = mk("vv", [128, 512], BF16); gate = mk("gate", [128, 512], F32)
    ee = mk("ee", [128, 256], F32); sp = mk("sp", [128, 256], F32)
    ebT = mk("ebT", [128, 2, 128], F32); enbT = mk("enbT", [128, 2, 128], F32); esuf = mk("esuf", [128, 256], F32)
    qd = mk("qd", [128, 2, 128], BF16); kd = mk("kd", [128, 2, 128], BF16); kend = mk("kend", [128, 256], BF16)
    attT = mk("attT", [128, 128], BF16)
    on = mk("on", [128, 512], F32); gs = mk("gs", [128, 512], F32); og = mk("og", [128, 512], BF16)
    oT = mk("oT", [128, 4, 128], BF16)
    junk = P.sb("junk3", [128, 512], BF16)
    ssq = mk("ssq", [128, 1], F32); rstd = mk("rstd", [128, 1], F32)
    qm = P.sb("qm", [128, 2, NSEQ, 128], BF16)
    kendm = P.sb("kendm", [128, NSEQ, 256], BF16)
    S0 = [P.sb("S0_%d" % i, [128, 2, 512], F32) for i in range(3)]
    S0b = [P.sb("S0b_%d" % i, [128, 2, 512], BF16) for i in range(3)]
    S1 = [P.sb("S1_%d" % i, [128, 2, 512], F32) for i in range(3)]
    z_ps = P.ps("z_ps", [128, 256], F32)
    bT_ps = P.ps("bT_ps", [128, 2, 128], F32)
    bsuf_ps = P.ps("bsuf_ps", [128, 256], F32)
    att_ps = P.ps("att_ps", [128, 128], F32)
    o_ps = P.ps("o_ps", [128, 512], F32)
    sn_ps = [P.ps("sn_ps%d" % i, [128, 512], F32) for i in range(2)]
    tp_ps = P.ps("tp_ps", [128, 4, 128], BF16)
    it = 0
    nsn = 0
    for t in range(NT):
        samp = (t == NT - 1)
        x = "s" if samp else "p"
        triT, sufT = C("triT_" + x), C("sufT_" + x)
        tok = slice(t * 128, (t + 1) * 128)
        for h in range(4):
            b = it % NB
            it += 1
            P.dma("sync", lrT[b][0:16, :], s_lr_fm[:, tok])
            P.dma("sync", qT[b][:], s_qa_fm[h * 256:(h + 1) * 256, tok].re("(c p) t -> p c t", p=128))
            P.dma("sync", kT[b][:], s_ka_fm[h * 256:(h + 1) * 256, tok].re("(c p) t -> p c t", p=128))
            P.dma("sync", ktok[b][:], s_ka_tm[tok, h * 256:(h + 1) * 256])
            P.dma("sync", vv[b][:], s_va[tok, h * 512:(h + 1) * 512])
            P.dma("sync", gate[b][:], s_ga[tok, h * 512:(h + 1) * 512])
            P.op("tensor", "matmul", [z_ps[:]], [lrT[b][:], w2e[:]], z_ps[:], lrT[b][:], w2e[:, h * 256:(h + 1) * 256],
                 start=True, stop=True)
            P.op("scalar", "activation", [ee[b][:]], [z_ps[:]], out=ee[b][:], in_=z_ps[:], func=AF.Exp, scale=-1.0)
            P.op("scalar", "activation", [sp[b][:]], [ee[b][:]], out=sp[b][:], in_=ee[b][:], func=AF.Ln, bias=1.0, scale=1.0)
            for c in range(2):
                P.op("tensor", "matmul", [bT_ps[:]], [sp[b][:], triT], bT_ps[:, c, :], sp[b][:, c * 128:(c + 1) * 128], triT,
                     start=True, stop=True, same_ok=True)
            P.op("tensor", "matmul", [bsuf_ps[:]], [sp[b][:], sufT], bsuf_ps[:], sufT, sp[b][:], start=True, stop=True)
            P.op("scalar", "activation", [ebT[b][:]], [bT_ps[:]], out=ebT[b][:], in_=bT_ps[:], func=AF.Exp, scale=-1.0 / 16)
            P.op("scalar", "activation", [enbT[b][:]], [bT_ps[:]], out=enbT[b][:], in_=bT_ps[:], func=AF.Exp, scale=1.0 / 16)
            P.op("scalar", "activation", [esuf[b][:]], [bsuf_ps[:]], out=esuf[b][:], in_=bsuf_ps[:], func=AF.Exp, scale=-1.0 / 16)
            P.op("vector", "scalar_tensor_tensor", [qd[b][:]], [qT[b][:], ebT[b][:]], out=qd[b][:], in0=qT[b][:], scalar=0.0625,
                 in1=ebT[b][:], op0=ALU.mult, op1=ALU.mult)
            P.op("vector", "tensor_tensor", [kd[b][:]], [kT[b][:], enbT[b][:]], out=kd[b][:], in0=kT[b][:], in1=enbT[b][:], op=ALU.mult)
            P.op("gpsimd", "tensor_tensor", [kend[b][:]], [ktok[b][:], esuf[b][:]], out=kend[b][:], in0=ktok[b][:], in1=esuf[b][:],
                 op=ALU.mult)
            for c in range(2):
                P.op("tensor", "matmul", [att_ps[:]], [kd[b][:], qd[b][:]], att_ps[:], kd[b][:, c, :], qd[b][:, c, :],
                     start=(c == 0), stop=(c == 1), same_ok=True)
            P.op("vector", "tensor_tensor", [attT[b][:]], [att_ps[:], triT], out=attT[b][:], in0=att_ps[:], in1=triT, op=ALU.mult)
            P.op("tensor", "matmul", [o_ps[:]], [attT[b][:], vv[b][:]], o_ps[:], attT[b][:], vv[b][:], start=True, stop=False)
            if not samp:
                for c in range(2):
                    P.op("tensor", "matmul", [o_ps[:]], [qd[b][:], Sbf[h][:]], o_ps[:], qd[b][:, c, :], Sbf[h][:, c, :],
                         start=False, stop=(c == 1), same_ok=True)
                for c in range(2):
                    sn = sn_ps[nsn % 2]; nsn += 1
                    P.op("tensor", "matmul", [sn[:]], [kend[b][:], vv[b][:]], sn[:], kend[b][:, c * 128:(c + 1) * 128], vv[b][:],
                         start=True, stop=True)
                    P.op("vector", "scalar_tensor_tensor", [S[h][:]], [S[h][:], ebT[b][:], sn[:]], out=S[h][:, c, :],
                         in0=S[h][:, c, :], scalar=ebT[b][:, c, 127:128], in1=sn[:], op0=ALU.mult, op1=ALU.add)
                P.op("scalar", "copy", [Sbf[h][:]], [S[h][:]], out=Sbf[h][:], in_=S[h][:])
                if t == NTP - 1:
                    P.dma("sync", glap_d[h].re("(c p) v -> p c v", p=128), S[h][:], semkey=S[h])
            else:
                for c in range(2):
                    P.op("vector", "tensor_tensor", [qm[:]], [qd[b][:], C("segF")], out=qm[:, c, :, :],
                         in0=qd[b][:, c, :].un(1).bc([128, NSEQ, 128]), in1=C("segF").re("p (s t) -> p s t", s=NSEQ), op=ALU.mult)
                P.op("gpsimd", "tensor_tensor", [kendm[:]], [kend[b][:], C("segP")], out=kendm[:],
                     in0=kend[b][:].un(1).bc([128, NSEQ, 256]), in1=C("segP").un(2).bc([128, NSEQ, 256]), op=ALU.mult)
                for s in range(NSEQ):
                    r = s % 3
                    P.dma("sync", S0[r][:], sgla_d[s, h].re("(c p) v -> p c v", p=128))
                    P.op("scalar", "copy", [S0b[r][:]], [S0[r][:]], out=S0b[r][:], in_=S0[r][:])
                    for c in range(2):
                        P.op("tensor", "matmul", [o_ps[:]], [qm[:], S0b[r][:]], o_ps[:], qm[:, c, s, :], S0b[r][:, c, :],
                             start=False, stop=(s == NSEQ - 1 and c == 1), same_ok=True)
                    for c in range(2):
                        sn = sn_ps[nsn % 2]; nsn += 1
                        P.op("tensor", "matmul", [sn[:]], [kendm[:], vv[b][:]], sn[:], kendm[:, s, c * 128:(c + 1) * 128], vv[b][:],
                             start=True, stop=True)
                        P.op("vector", "scalar_tensor_tensor", [S1[r][:]], [S0[r][:], ebT[b][:], sn[:]], out=S1[r][:, c, :],
                             in0=S0[r][:, c, :], scalar=ebT[b][:, c, 8 * s + 7:8 * s + 8], in1=sn[:], op0=ALU.mult, op1=ALU.add)
                    P.dma("sync", glas_d[s, h].re("(c p) v -> p c v", p=128), S1[r][:], semkey=S1[r])
            P.op("scalar", "activation", [junk[:], ssq[b][:]], [o_ps[:]], out=junk[:], in_=o_ps[:], func=AF.Square, accum_out=ssq[b][:])
            P.op("scalar", "activation", [rstd[b][:]], [ssq[b][:], eps_t[:]], out=rstd[b][:], in_=ssq[b][:], func=AF.Sqrt,
                 bias=eps_t[:], scale=1.0 / 512)
            P.op("vector", "reciprocal", [rstd[b][:]], [rstd[b][:]], out=rstd[b][:], in_=rstd[b][:])
            P.op("vector", "scalar_tensor_tensor", [on[b][:]], [o_ps[:], rstd[b][:], gnA[:]], out=on[b][:], in0=o_ps[:],
                 scalar=rstd[b][:, 0:1], in1=gnA[:], op0=ALU.mult, op1=ALU.mult)
            P.op("scalar", "activation", [gs[b][:]], [gate[b][:]], out=gs[b][:], in_=gate[b][:], func=AF.Silu)
            P.op("gpsimd", "tensor_tensor", [og[b][:]], [on[b][:], gs[b][:]], out=og[b][:], in0=on[b][:], in1=gs[b][:], op=ALU.mult)
            for j in range(4):
                P.op("tensor", "transpose", [tp_ps[:]], [og[b][:], ident_bf[:]], out=tp_ps[:, j, :], in_=og[b][:, j * 128:(j + 1) * 128],
                     identity=ident_bf[:], same_ok=True)
            P.op("vector", "tensor_copy", [oT[b][:]], [tp_ps[:]], out=oT[b][:], in_=tp_ps[:])
            P.dma("sync", s_oaT[h * 512:(h + 1) * 512, tok].re("(j p) t -> p j t", p=128), oT[b][:], semkey=oT[b])
    P.pop()


def _phase_gdn(P, C, L):
    d = L
    s_qkv_fm, s_qkv_tm, s_bd, s_gb, s_obT = (d[k] for k in ["s_qkv_fm", "s_qkv_tm", "s_bd", "s_gb", "s_obT"])
    cwT_d, alog_d, dtb_d, gnb_d, sgdn_d, sconv_d, gdnp_d, gdns_d, convp_d, convs_d, ident_bf = (d[k] for k in
        ["cwT_d", "alog_d", "dtb_d", "gnb_d", "sgdn_d", "sconv_d", "gdnp_d", "gdns_d", "convp_d", "convs_d", "ident_bf"])
    P.push()

    def bl(name, src, n):
        t = P.sb(name, [128, n], F32)
        P.dma("sync", t[:], V(src.t.ap()[0:1, 0:n].partition_broadcast(128)[:, 0, :], [src]))
        return t
    gnB = bl("gnB", gnb_d, 128)
    alog_b = bl("alog_b", alog_d, 16)
    dtb_b = bl("dtb_b", dtb_d, 16)
    negA = P.sb("negA", [128, 16], F32)
    P.op("scalar", "activation", [negA[:]], [alog_b[:]], out=negA[:], in_=alog_b[:], func=AF.Exp)
    P.op("vector", "tensor_scalar", [negA[:]], [negA[:]], out=negA[:], in0=negA[:], scalar1=-1.0, scalar2=None, op0=ALU.mult)
    cw = P.sb("cw", [128, 48, 4], F32)
    P.dma("sync", cw[:], cwT_d[:])
    eps_t = P.sb("eps4", [128, 1], F32)
    P.op("vector", "memset", [eps_t[:]], [], eps_t[:], EPS)
    P.push()
    cvs = P.sb("cvs", [51, 6144], F32)
    P.dma("sync", cvs[48:51, :], s_qkv_tm[125:128, :])
    P.dma("sync", convp_d[:], cvs[48:51, :], semkey=cvs)
    for s in range(NSEQ):
        P.dma("sync", cvs[3 * s:3 * s + 3, :], s_qkv_tm[128 + 8 * s + 5:128 + 8 * s + 8, :])
    P.dma("sync", convs_d.ap().re("s r c -> (s r) c"), cvs[0:48, :], semkey=cvs)
    P.pop()
    sct = P.sb("sct", [48, 6144], F32)
    P.dma("sync", sct[:], sconv_d.ap().re("s r c -> (s r) c"))
    S = P.sb("Sg", [128, 16, 128], F32)
    Sbf = P.sb("Sgbf", [128, 16, 128], BF16)
    P.op("gpsimd", "memset", [S[:]], [], S[:], 0.0)
    P.op("gpsimd", "memset", [Sbf[:]], [], Sbf[:], 0.0)
    prev3 = P.sb("prev3", [128, 16, 3, 3], F32)
    P.op("gpsimd", "memset", [prev3[:]], [], prev3[:], 0.0)
    bd = P.sb("bd", [128, 32], F32)
    beta = P.sb("beta", [128, 16], F32); tmp16 = P.sb("tmp16", [128, 16], F32); g = P.sb("g", [128, 16], F32)
    gh = P.sb("gh", [128, 16], F32); eg = P.sb("eg", [128, 16], F32); egsuf = P.sb("egsuf", [128, 16], F32)
    bq = P.sb("bq", [128, 16], F32)
    Rl = P.sb("Rl", [128, NSEQ, 16], F32); egl = P.sb("egl", [128, NSEQ, 16], F32)
    diag = P.sb("diag", [128, 16, 128], F32)
    E1 = P.sb("E1", [128, 16, 128], F32); M1 = diag
    Ds = P.sb("Ds", [128, 16, 128], F32); DT = P.sb("DT", [128, 16, 128], F32); egB = P.sb("egB", [128, 16, 128], F32)
    gbt = P.sb("gbt", [128, 2048], F32); gsb = gbt
    ogall = P.sb("ogall", [128, 2048], BF16)
    oTs = [P.sb("oTs%d" % i, [128, 8, 128], BF16) for i in range(2)]
    NB = 2

    def mk(name, shape, dt):
        return [P.sb("%s%d" % (name, i), shape, dt) for i in range(NB)]
    ext = mk("ext", [128, 3, 176], F32)
    cacc = mk("cacc", [128, 3, 128], F32); act = mk("act", [128, 3, 128], F32)
    sq = mk("sq", [128, 2, 128], F32); rn = mk("rn", [128, 2, 128], F32)
    qn = mk("qn", [128, 128], BF16); qg = mk("qg", [128, 128], BF16); knf = mk("knf", [128, 128], F32); kn = mk("kn", [128, 128], BF16)
    rhsv = mk("rhsv", [128, 128], BF16); rhsw = mk("rhsw", [128, 128], BF16); kend = mk("kendg", [128, 128], BF16)
    mm = mk("mm", [128, 128], F32); Cl = mk("Cl", [128, NLEV, 128], F32)
    U = mk("U", [128, 128], F32); X = mk("X", [128, 128], F32); Psb = mk("Psb", [128, 128], F32); Ubf = mk("Ubf", [128, 128], BF16)
    nwT = mk("nwT", [128, 128], BF16); vnew = mk("vnew", [128, 128], BF16); qkT = mk("qkT", [128, 128], BF16)
    on = mk("ong", [128, 128], F32); ssq = mk("ssqg", [128, 1], F32); rstd = mk("rstdg", [128, 1], F32)
    junk = P.sb("junk4", [128, 128], BF16)
    nwTm = P.sb("nwTm", [128, NSEQ, 128], BF16); qgm = P.sb("qgm", [128, NSEQ, 128], BF16); kendm = P.sb("kendmg", [128, NSEQ, 128], BF16)
    S0h = [P.sb("S0h", [128, NSEQ, 128], F32)] * NB; S0hb = [P.sb("S0hb", [128, NSEQ, 128], BF16)] * NB
    S1h = [P.sb("S1h", [128, NSEQ, 128], F32)] * NB
    banks = [P.ps("gbank%d" % i, [128, 512], F32) for i in range(6)]

    def sub(bk, c0, n, name, pat=None, **kw):
        ap = banks[bk].t[:, c0:c0 + n]
        if pat:
            ap = ap.rearrange(pat, **kw)
        return SubBuf(banks[bk], ap, name)
    ghB_half = [sub(0, 0, 256, "ghB_ps0", "p (a b) -> p a b", a=2), sub(0, 256, 256, "ghB_ps1", "p (a b) -> p a b", a=2)]
    sm_ps = sub(1, 0, 32, "sm_ps")
    glB_ps = sub(1, 64, 256, "glB_ps", "p (a b) -> p a b", b=16)
    tpc_ps = sub(1, 320, 48, "tpc_ps")
    ss_ps = sub(2, 0, 256, "ss_ps")
    tk_ps = sub(2, 256, 256, "tk_ps", "p (a b) -> p a b", a=2)
    kk_ps = sub(3, 0, 128, "kk_ps"); lp_ps = sub(3, 128, 128, "lp_ps"); lu_ps = sub(3, 256, 128, "lu_ps"); lx_ps = sub(3, 384, 128, "lx_ps")
    vn_ps = sub(4, 0, 128, "vn_ps"); wT_ps = sub(4, 128, 128, "wT_ps"); qk_ps = sub(4, 256, 128, "qk_ps"); o_ps = sub(4, 384, 128, "og_ps")
    sn_ps = sub(5, 0, 512, "sng_ps", "p (a b) -> p a b", a=4)
    tp_ps = P.ps("tpg_ps", [128, 8, 128], BF16)
    ident, ones = C("ident"), C("ones")
    it = 0
    G_T = [int(v) for v in os.environ.get("GDN_T", ",".join(str(i) for i in range(NT))).split(",") if v != ""]
    G_H = int(os.environ.get("GDN_H", "16"))
    G_A = int(os.environ.get("GDN_A", "1000"))
    for t in G_T:
        samp = (t == NT - 1)
        x = "s" if samp else "p"
        nseg = NSEQ if samp else 1
        nlev = 3 if samp else NLEV
        triT, sufT = C("triT_" + x), C("sufT_" + x)
        tok = slice(t * 128, (t + 1) * 128)
        _ac = [0]

        def _A():
            _ac[0] += 1
            return _ac[0] <= G_A
        if _A():
            P.dma("sync", bd[:], s_bd[tok, :])
        if _A():
            P.dma("sync", gbt[:], s_gb[tok, :])
        if _A():
            P.op("scalar", "activation", [gsb[:]], [gbt[:]], out=gsb[:], in_=gbt[:], func=AF.Silu)
        if _A():
            P.op("scalar", "activation", [beta[:]], [bd[:]], out=beta[:], in_=bd[:, 0:16], func=AF.Sigmoid)
        if _A():
            P.op("vector", "tensor_tensor", [tmp16[:]], [bd[:], dtb_b[:]], out=tmp16[:], in0=bd[:, 16:32], in1=dtb_b[:], op=ALU.add)
        if _A():
            P.op("scalar", "activation", [tmp16[:]], [tmp16[:]], out=tmp16[:], in_=tmp16[:], func=AF.Exp)
        if _A():
            P.op("scalar", "activation", [tmp16[:]], [tmp16[:]], out=tmp16[:], in_=tmp16[:], func=AF.Ln, bias=1.0, scale=1.0)
        if _A():
            P.op("vector", "tensor_tensor", [g[:]], [tmp16[:], negA[:]], out=g[:], in0=tmp16[:], in1=negA[:], op=ALU.mult)
        if _A():
            P.op("tensor", "matmul", [sm_ps[:]], [g[:], triT], sm_ps[:, 0:16], triT, g[:], start=True, stop=True)
        if _A():
            P.op("tensor", "matmul", [sm_ps[:]], [g[:], sufT], sm_ps[:, 16:32], sufT, g[:], start=True, stop=True, same_ok=True)
        if _A():
            P.op("vector", "tensor_copy", [gh[:]], [sm_ps[:]], out=gh[:], in_=sm_ps[:, 0:16])
        if _A():
            P.op("scalar", "activation", [eg[:]], [sm_ps[:]], out=eg[:], in_=sm_ps[:, 0:16], func=AF.Exp)
        if _A():
            P.op("scalar", "activation", [egsuf[:]], [sm_ps[:]], out=egsuf[:], in_=sm_ps[:, 16:32], func=AF.Exp)
        if _A():
            P.op("vector", "tensor_tensor", [bq[:]], [beta[:], eg[:]], out=bq[:], in0=beta[:], in1=eg[:], op=ALU.mult)
        selL = C("selL_" + x)
        if _A():
            P.op("vector", "tensor_tensor", [Rl[:]], [gh[:], selL], out=Rl[:, 0:nseg, :], in0=gh[:].un(1).bc([128, nseg, 16]),
                 in1=selL.un(2).bc([128, nseg, 16]), op=ALU.mult)
        if _A():
            P.op("tensor", "matmul", [glB_ps[:]], [Rl[:], ones], glB_ps[:, 0:nseg, :], ones, Rl[:, 0:nseg, :], start=True, stop=True)
        if _A():
            P.op("scalar", "activation", [egl[:]], [glB_ps[:]], out=egl[:, 0:nseg, :], in_=glB_ps[:, 0:nseg, :], func=AF.Exp)
        if _A():
            P.op("gpsimd", "tensor_tensor", [diag[:]], [gh[:], ident], out=diag[:], in0=ident.un(1).bc([128, 16, 128]),
                 in1=gh[:].un(2).bc([128, 16, 128]), op=ALU.mult)
        for q8 in range(8):
            hs = slice(q8 * 2, q8 * 2 + 2)
            gps = ghB_half[q8 % 2]
            if _A():
                P.op("tensor", "matmul", [gps[:]], [diag[:], ones], gps[:], ones, diag[:, hs, :], start=True, stop=True)
            if _A():
                P.op("vector", "tensor_tensor", [E1[:]], [gps[:], gh[:]], out=E1[:, hs, :], in0=gps[:],
                     in1=gh[:, hs].un(2).bc([128, 2, 128]), op=ALU.subtract)
            if _A():
                P.op("scalar", "activation", [egB[:]], [gps[:]], out=egB[:, hs, :], in_=gps[:], func=AF.Exp)
        if _A():
            P.op("gpsimd", "tensor_tensor", [M1[:]], [E1[:], C("bigUs_" + x)], out=M1[:], in0=E1[:],
                 in1=C("bigUs_" + x).un(1).bc([128, 16, 128]), op=ALU.add)
        if _A():
            P.op("scalar", "activation", [Ds[:]], [M1[:]], out=Ds[:], in_=M1[:], func=AF.Exp, scale=-1.0)
        if _A():
            P.op("gpsimd", "tensor_tensor", [M1[:]], [E1[:], C("nbigL_" + x)], out=M1[:], in0=E1[:],
                 in1=C("nbigL_" + x).un(1).bc([128, 16, 128]), op=ALU.add)
        if _A():
            P.op("scalar", "activation", [DT[:]], [M1[:]], out=DT[:], in_=M1[:], func=AF.Exp)
        for h in range(G_H):
            b = it % NB
            it += 1
            if not samp:
                ev = ext[b][:, :, 0:131]
                cur = ev[:, :, 3:131]
                P.op("gpsimd", "tensor_copy", [ext[b][:]], [prev3[:]], out=ev[:, :, 0:3], in_=prev3[:, h, :, :])
                for p3 in range(3):
                    r0 = p3 * 2048 + h * 128
                    P.dma("sync", ev[:, p3, 3:131], s_qkv_fm[r0:r0 + 128, tok])
                P.op("gpsimd", "tensor_copy", [prev3[:]], [ext[b][:]], out=prev3[:, h, :, :], in_=ev[:, :, 128:131])
                taps = [ev[:, :, i:i + 128] for i in range(4)]
                accv = cacc[b][:]
            else:
                ev = ext[b][:].re("p a (s w) -> p a s w", w=11)
                for p3 in range(3):
                    r0 = p3 * 2048 + h * 128
                    P.dma("sync", ev[:, p3, :, 3:11], s_qkv_fm[r0:r0 + 128, tok].re("p (s w) -> p s w", w=8))
                    P.op("tensor", "transpose", [tpc_ps[:]], [sct[:], ident], out=tpc_ps[:], in_=sct[:, r0:r0 + 128],
                         identity=ident[0:48, 0:48])
                    P.op("vector", "tensor_copy", [ext[b][:]], [tpc_ps[:]], out=ev[:, p3, :, 0:3],
                         in_=tpc_ps[:].re("p (s r) -> p s r", r=3))
                taps = None
                accv = cacc[b][:].re("p a (s w) -> p a s w", w=8)
            for p3 in range(3):
                blk = p3 * 16 + h
                if not samp:
                    tp = [ev[:, p3, i:i + 128] for i in range(4)]
                    av = cacc[b][:, p3, :]
                else:
                    tp = [ev[:, p3, :, i:i + 8] for i in range(4)]
                    av = accv[:, p3, :, :]
                P.op("gpsimd", "tensor_scalar", [cacc[b][:]], [ext[b][:], cw[:]], out=av, in0=tp[0], scalar1=cw[:, blk, 0:1],
                     scalar2=None, op0=ALU.mult)
                for i in range(1, 4):
                    P.op("vector", "scalar_tensor_tensor", [cacc[b][:]], [ext[b][:], cw[:], cacc[b][:]], out=av, in0=tp[i],
                         scalar=cw[:, blk, i:i + 1], in1=av, op0=ALU.mult, op1=ALU.add)
            P.op("scalar", "activation", [act[b][:]], [cacc[b][:]], out=act[b][:], in_=cacc[b][:], func=AF.Silu)
            P.op("gpsimd", "tensor_tensor", [sq[b][:]], [act[b][:]], out=sq[b][:], in0=act[b][:, 0:2, :], in1=act[b][:, 0:2, :], op=ALU.mult)
            P.op("tensor", "matmul", [ss_ps[:]], [sq[b][:], ones], ss_ps[:], ones, sq[b][:].re("p a t -> p (a t)"), start=True, stop=True)
            P.op("scalar", "activation", [rn[b][:]], [ss_ps[:], eps_t[:]], out=rn[b][:].re("p a t -> p (a t)"), in_=ss_ps[:], func=AF.Sqrt,
                 bias=eps_t[:], scale=1.0)
            P.op("vector", "reciprocal", [rn[b][:]], [rn[b][:]], out=rn[b][:], in_=rn[b][:])
            P.op("vector", "scalar_tensor_tensor", [qn[b][:]], [act[b][:], rn[b][:]], out=qn[b][:], in0=act[b][:, 0, :], scalar=128.0 ** -0.5,
                 in1=rn[b][:, 0, :], op0=ALU.mult, op1=ALU.mult)
            P.op("vector", "tensor_tensor", [knf[b][:]], [act[b][:], rn[b][:]], out=knf[b][:], in0=act[b][:, 1, :], in1=rn[b][:, 1, :], op=ALU.mult)
            P.op("gpsimd", "tensor_copy", [kn[b][:]], [knf[b][:]], out=kn[b][:], in_=knf[b][:])
            P.op("gpsimd", "tensor_tensor", [qg[b][:]], [qn[b][:], egB[:]], out=qg[b][:], in0=qn[b][:], in1=egB[:, h, :], op=ALU.mult)
            P.op("tensor", "transpose", [tk_ps[:]], [knf[b][:], ident], out=tk_ps[:, 0, :], in_=knf[b][:], identity=ident)
            P.op("tensor", "transpose", [tk_ps[:]], [act[b][:], ident], out=tk_ps[:, 1, :], in_=act[b][:, 2, :], identity=ident, same_ok=True)
            P.op("vector", "tensor_scalar", [rhsw[b][:]], [tk_ps[:], bq[:]], out=rhsw[b][:], in0=tk_ps[:, 0, :], scalar1=bq[:, h:h + 1],
                 scalar2=None, op0=ALU.mult)
            P.op("vector", "tensor_scalar", [kend[b][:]], [tk_ps[:], egsuf[:]], out=kend[b][:], in0=tk_ps[:, 0, :], scalar1=egsuf[:, h:h + 1],
                 scalar2=None, op0=ALU.mult)
            P.op("vector", "tensor_scalar", [rhsv[b][:]], [tk_ps[:], beta[:]], out=rhsv[b][:], in0=tk_ps[:, 1, :], scalar1=beta[:, h:h + 1],
                 scalar2=None, op0=ALU.mult)
            P.op("tensor", "matmul", [kk_ps[:]], [kn[b][:]], kk_ps[:], kn[b][:], kn[b][:], start=True, stop=True)
            P.op("vector", "scalar_tensor_tensor", [mm[b][:]], [kk_ps[:], beta[:], Ds[:]], out=mm[b][:], in0=kk_ps[:], scalar=beta[:, h:h + 1],
                 in1=Ds[:, h, :], op0=ALU.mult, op1=ALU.mult)
            P.op("gpsimd", "tensor_tensor", [Cl[b][:]], [mm[b][:], C("lmask")], out=Cl[b][:, 0:nlev, :],
                 in0=mm[b][:].un(1).bc([128, nlev, 128]), in1=C("lmask").re("p (l f) -> p l f", l=NLEV)[:, 0:nlev, :], op=ALU.mult)
            P.op("vector", "tensor_copy", [U[b][:]], [ident], out=U[b][:], in_=ident)
            P.op("gpsimd", "tensor_copy", [X[b][:]], [ident], out=X[b][:], in_=ident)
            for l in range(nlev):
                P.op("tensor", "matmul", [lp_ps[:]], [Cl[b][:], U[b][:]], lp_ps[:], Cl[b][:, l, :], U[b][:], start=True, stop=True)
                P.op("scalar", "copy", [Psb[b][:]], [lp_ps[:]], out=Psb[b][:], in_=lp_ps[:])
                P.op("tensor", "matmul", [lu_ps[:]], [X[b][:], Psb[b][:]], lu_ps[:], X[b][:], Psb[b][:], start=True, stop=True)
                P.op("vector", "tensor_tensor", [U[b][:]], [U[b][:], lu_ps[:]], out=U[b][:], in0=U[b][:], in1=lu_ps[:], op=ALU.subtract)
                if l < nlev - 1:
                    P.op("tensor", "transpose", [lx_ps[:]], [U[b][:], ident], out=lx_ps[:], in_=U[b][:], identity=ident)
                    P.op("scalar", "copy", [X[b][:]], [lx_ps[:]], out=X[b][:], in_=lx_ps[:])
            P.op("scalar", "copy", [Ubf[b][:]], [U[b][:]], out=Ubf[b][:], in_=U[b][:])
            P.op("tensor", "matmul", [wT_ps[:]], [rhsw[b][:], Ubf[b][:]], wT_ps[:], rhsw[b][:], Ubf[b][:], start=True, stop=True)
            P.op("scalar", "mul", [nwT[b][:]], [wT_ps[:]], out=nwT[b][:], in_=wT_ps[:], mul=-1.0)
            P.op("tensor", "matmul", [vn_ps[:]], [Ubf[b][:], rhsv[b][:]], vn_ps[:], Ubf[b][:], rhsv[b][:], start=True, stop=False)
            if not samp:
                P.op("tensor", "matmul", [vn_ps[:]], [nwT[b][:], Sbf[:]], vn_ps[:], nwT[b][:], Sbf[:, h, :], start=False, stop=True, same_ok=True)
            else:
                P.dma("sync", S0h[b][:], sgdn_d[:, h].re("s k v -> k s v"))
                P.op("gpsimd", "tensor_copy", [S0hb[b][:]], [S0h[b][:]], out=S0hb[b][:], in_=S0h[b][:])
                segF3 = C("segF").re("p (s t) -> p s t", s=NSEQ)
                P.op("vector", "tensor_tensor", [nwTm[:]], [nwT[b][:], C("segF")], out=nwTm[:], in0=nwT[b][:].un(1).bc([128, NSEQ, 128]),
                     in1=segF3, op=ALU.mult)
                P.op("gpsimd", "tensor_tensor", [qgm[:]], [qg[b][:], C("segF")], out=qgm[:], in0=qg[b][:].un(1).bc([128, NSEQ, 128]),
                     in1=segF3, op=ALU.mult)
                P.op("gpsimd", "tensor_tensor", [kendm[:]], [kend[b][:], C("segP")], out=kendm[:], in0=kend[b][:].un(1).bc([128, NSEQ, 128]),
                     in1=C("segP").un(2).bc([128, NSEQ, 128]), op=ALU.mult)
                for s in range(NSEQ):
                    P.op("tensor", "matmul", [vn_ps[:]], [nwTm[:], S0hb[b][:]], vn_ps[:], nwTm[:, s, :], S0hb[b][:, s, :],
                         start=False, stop=(s == NSEQ - 1), same_ok=True)
            P.op("scalar", "copy", [vnew[b][:]], [vn_ps[:]], out=vnew[b][:], in_=vn_ps[:])
            P.op("tensor", "matmul", [qk_ps[:]], [kn[b][:], qn[b][:]], qk_ps[:], kn[b][:], qn[b][:], start=True, stop=True)
            P.op("vector", "tensor_tensor", [qkT[b][:]], [qk_ps[:], DT[:]], out=qkT[b][:], in0=qk_ps[:], in1=DT[:, h, :], op=ALU.mult)
            P.op("tensor", "matmul", [o_ps[:]], [qkT[b][:], vnew[b][:]], o_ps[:], qkT[b][:], vnew[b][:], start=True, stop=False)
            if not samp:
                P.op("tensor", "matmul", [o_ps[:]], [qg[b][:], Sbf[:]], o_ps[:], qg[b][:], Sbf[:, h, :], start=False, stop=True, same_ok=True)
            else:
                for s in range(NSEQ):
                    P.op("tensor", "matmul", [o_ps[:]], [qgm[:], S0hb[b][:]], o_ps[:], qgm[:, s, :], S0hb[b][:, s, :],
                         start=False, stop=(s == NSEQ - 1), same_ok=True)
            if not samp:
                P.op("tensor", "matmul", [sn_ps[:]], [kend[b][:], vnew[b][:]], sn_ps[:, 0, :], kend[b][:], vnew[b][:], start=True, stop=True)
                P.op("vector", "scalar_tensor_tensor", [S[:]], [S[:], egl[:], sn_ps[:]], out=S[:, h, :], in0=S[:, h, :],
                     scalar=egl[:, 0, h:h + 1], in1=sn_ps[:, 0, :], op0=ALU.mult, op1=ALU.add)
                P.op("scalar", "copy", [Sbf[:]], [S[:]], out=Sbf[:, h, :], in_=S[:, h, :])
            else:
                for s4 in range(4):
                    for j in range(4):
                        s = s4 * 4 + j
                        P.op("tensor", "matmul", [sn_ps[:]], [kendm[:], vnew[b][:]], sn_ps[:, j, :], kendm[:, s, :], vnew[b][:],
                             start=True, stop=True, same_ok=(j > 0))
                    for j in range(4):
                        s = s4 * 4 + j
                        P.op("vector", "scalar_tensor_tensor", [S1h[b][:]], [S0h[b][:], egl[:], sn_ps[:]], out=S1h[b][:, s, :],
                             in0=S0h[b][:, s, :], scalar=egl[:, s, h:h + 1], in1=sn_ps[:, j, :], op0=ALU.mult, op1=ALU.add)
                P.dma("sync", gdns_d[:, h].re("s k v -> k s v"), S1h[b][:], semkey=S1h[b])
            P.op("scalar", "activation", [junk[:], ssq[b][:]], [o_ps[:]], out=junk[:], in_=o_ps[:], func=AF.Square, accum_out=ssq[b][:])
            P.op("scalar", "activation", [rstd[b][:]], [ssq[b][:], eps_t[:]], out=rstd[b][:], in_=ssq[b][:], func=AF.Sqrt,
                 bias=eps_t[:], scale=1.0 / 128)
            P.op("vector", "reciprocal", [rstd[b][:]], [rstd[b][:]], out=rstd[b][:], in_=rstd[b][:])
            P.op("vector", "scalar_tensor_tensor", [on[b][:]], [o_ps[:], rstd[b][:], gnB[:]], out=on[b][:], in0=o_ps[:],
                 scalar=rstd[b][:, 0:1], in1=gnB[:], op0=ALU.mult, op1=ALU.mult)
            P.op("gpsimd", "tensor_tensor", [ogall[:]], [on[b][:], gsb[:]], out=ogall[:, h * 128:(h + 1) * 128], in0=on[b][:],
                 in1=gsb[:, h * 128:(h + 1) * 128], op=ALU.mult)
        if t == NTP - 1:
            P.dma("sync", gdnp_d.ap().re("h k v -> k h v"), S[:], semkey=S)
        for g8 in (range(2) if G_H > 0 else []):
            for j in range(8):
                hh = g8 * 8 + j
                P.op("tensor", "transpose", [tp_ps[:]], [ogall[:], ident_bf[:]], out=tp_ps[:, j, :], in_=ogall[:, hh * 128:(hh + 1) * 128],
                     identity=ident_bf[:], same_ok=(j > 0))
            P.op("vector", "tensor_copy", [oTs[g8][:]], [tp_ps[:]], out=oTs[g8][:], in_=tp_ps[:])
            P.dma("sync", s_obT[g8 * 1024:(g8 + 1) * 1024, tok].re("(j p) t -> p j t", p=128), oTs[g8][:], semkey=oTs[g8])
    P.pop()


def _phase_post(P, C, L):
    d = L
    s_oaT, s_obT, s_ma, s_mb, x_d, y_d, wba_d, wbb_d, wo_d, fng_d, ident_bf = (d[k] for k in
        ["s_oaT", "s_obT", "s_ma", "s_mb", "x_d", "y_d", "wba_d", "wbb_d", "wo_d", "fng_d", "ident_bf"])
    P.push()
    gfin = P.sb("gfin", [128, D], F32)
    P.dma("sync", gfin[:], V(fng_d.t.ap()[0:1, :].partition_broadcast(128)[:, 0, :], [fng_d]))
    eps_t = P.sb("eps5", [128, 1], F32)
    P.op("vector", "memset", [eps_t[:]], [], eps_t[:], EPS)
    pA = [P.ps("pp%d" % i, [128, 512], F32) for i in range(4)]
    tp_ps = P.ps("tp5", [128, 4, 128], BF16)
    halves = [list(range(0, 9)), list(range(9, NT))]
    wviews = {id(w): w.t.ap().rearrange("(ko ki) c -> ki ko c", ki=128) for w in (wba_d, wbb_d, wo_d)}

    def wload(dst, w, c0, n):
        for q4 in range(4):
            P.dma("gpsimd", dst[:, q4 * 4:(q4 + 1) * 4, 0:n], V(wviews[id(w)][:, q4 * 4:(q4 + 1) * 4, c0:c0 + n], [w]))
    for hi, tiles in enumerate(halves):
        nt = len(tiles)
        sfx = "_h%d" % hi
        t0 = tiles[0] * 128
        ntok = nt * 128
        P.push()
        mT = P.sb("mT" + sfx, [128, KT, 1152], BF16)
        P.push()
        oaT = P.sb("oaT" + sfx, [128, KT, 1152], BF16)
        obT = P.sb("obT" + sfx, [128, KT, 1152], BF16)
        for q4 in range(4):
            ks = slice(q4 * 4, q4 * 4 + 4)
            P.dma("sync", oaT[:, ks, 0:ntok], s_oaT[q4 * 512:(q4 + 1) * 512, t0:t0 + ntok].re("(k p) t -> p k t", p=128))
            P.dma("sync", obT[:, ks, 0:ntok], s_obT[q4 * 512:(q4 + 1) * 512, t0:t0 + ntok].re("(k p) t -> p k t", p=128))
        wa = P.sb("wa" + sfx, [128, KT, 512], BF16)
        wb = P.sb("wb" + sfx, [128, KT, 512], BF16)
        mat = [P.sb("mat%d" % i + sfx, [128, 512], F32) for i in range(2)]
        mbt = [P.sb("mbt%d" % i + sfx, [128, 512], F32) for i in range(2)]
        t1 = [P.sb("t1_%d" % i + sfx, [128, 512], F32) for i in range(2)]
        t2 = [P.sb("t2_%d" % i + sfx, [128, 512], F32) for i in range(2)]
        mg = [P.sb("mg%d" % i + sfx, [128, 512], BF16) for i in range(2)]
        it = 0
        for j in range(4):
            wload(wa, wba_d, j * 512, 512)
            wload(wb, wbb_d, j * 512, 512)
            for ti, t in enumerate(tiles):
                b = it % 2
                it += 1
                ya, yb = pA[2 * b], pA[2 * b + 1]
                lt = slice(ti * 128, (ti + 1) * 128)
                for k in range(KT):
                    P.op("tensor", "matmul", [ya[:]], [oaT[:], wa[:]], ya[:], oaT[:, k, lt], wa[:, k, :], start=(k == 0), stop=(k == KT - 1), same_ok=True)
                for k in range(KT):
                    P.op("tensor", "matmul", [yb[:]], [obT[:], wb[:]], yb[:], obT[:, k, lt], wb[:, k, :], start=(k == 0), stop=(k == KT - 1), same_ok=True)
                P.dma("sync", mat[b][:], s_ma[t * 128:(t + 1) * 128, j * 512:(j + 1) * 512])
                P.dma("sync", mbt[b][:], s_mb[t * 128:(t + 1) * 128, j * 512:(j + 1) * 512])
                P.op("scalar", "activation", [mat[b][:]], [mat[b][:]], out=mat[b][:], in_=mat[b][:], func=AF.Sigmoid)
                P.op("scalar", "activation", [mbt[b][:]], [mbt[b][:]], out=mbt[b][:], in_=mbt[b][:], func=AF.Sigmoid)
                P.op("vector", "tensor_tensor", [t1[b][:]], [ya[:], mat[b][:]], out=t1[b][:], in0=ya[:], in1=mat[b][:], op=ALU.mult)
                P.op("vector", "tensor_tensor", [t2[b][:]], [yb[:], mbt[b][:]], out=t2[b][:], in0=yb[:], in1=mbt[b][:], op=ALU.mult)
                P.op("gpsimd", "tensor_tensor", [mg[b][:]], [t1[b][:], t2[b][:]], out=mg[b][:], in0=t1[b][:], in1=t2[b][:], op=ALU.add)
                for c in range(4):
                    P.op("tensor", "transpose", [tp_ps[:]], [mg[b][:], ident_bf[:]], out=tp_ps[:, c, :], in_=mg[b][:, c * 128:(c + 1) * 128],
                         identity=ident_bf[:], same_ok=(c > 0))
                P.op("scalar", "copy", [mT[:]], [tp_ps[:]], out=mT[:, 4 * j:4 * j + 4, lt], in_=tp_ps[:])
        P.pop()
        P.push()
        wo = P.sb("wo" + sfx, [128, KT, D], BF16)
        for j in range(4):
            for q4 in range(4):
                P.dma("gpsimd", wo[:, q4 * 4:(q4 + 1) * 4, j * 512:(j + 1) * 512],
                      V(wviews[id(wo_d)][:, q4 * 4:(q4 + 1) * 4, j * 512:(j + 1) * 512], [wo_d]))
        xt = [P.sb("x5_%d" % i + sfx, [128, D], F32) for i in range(2)]
        res = [P.sb("res%d" % i + sfx, [128, D], F32) for i in range(2)]
        yo = [P.sb("yo%d" % i + sfx, [128, D], F32) for i in range(2)]
        junk = P.sb("junk5" + sfx, [128, D], BF16)
        ssq = [P.sb("ssq5_%d" % i + sfx, [128, 1], F32) for i in range(2)]
        for ti, t in enumerate(tiles):
            b = ti % 2
            lt = slice(ti * 128, (ti + 1) * 128)
            P.dma("sync", xt[b][:], x_d[t * 128:(t + 1) * 128, :])
            for j in range(4):
                ps = pA[j]
                for k in range(KT):
                    P.op("tensor", "matmul", [ps[:]], [mT[:], wo[:]], ps[:], mT[:, k, lt], wo[:, k, j * 512:(j + 1) * 512],
                         start=(k == 0), stop=(k == KT - 1), same_ok=True)
                P.op("vector", "tensor_tensor", [res[b][:]], [ps[:], xt[b][:]], out=res[b][:, j * 512:(j + 1) * 512], in0=ps[:],
                     in1=xt[b][:, j * 512:(j + 1) * 512], op=ALU.add)
            P.op("scalar", "activation", [junk[:], ssq[b][:]], [res[b][:]], out=junk[:], in_=res[b][:], func=AF.Square, accum_out=ssq[b][:])
            P.op("scalar", "activation", [ssq[b][:]], [ssq[b][:], eps_t[:]], out=ssq[b][:], in_=ssq[b][:], func=AF.Sqrt, bias=eps_t[:], scale=1.0 / D)
            P.op("vector", "reciprocal", [ssq[b][:]], [ssq[b][:]], out=ssq[b][:], in_=ssq[b][:])
            P.op("vector", "scalar_tensor_tensor", [yo[b][:]], [res[b][:], ssq[b][:], gfin[:]], out=yo[b][:], in0=res[b][:], scalar=ssq[b][:, 0:1],
                 in1=gfin[:], op0=ALU.mult, op1=ALU.mult)
            P.dma("sync", y_d[t * 128:(t + 1) * 128, :], yo[b][:], semkey=yo[b])
        P.pop()
        P.pop()
    P.pop()

def _core_inputs(c, inp):
    b = c // 2
    xs = inp["x_sample"][16 * c:16 * c + 16].reshape(128, D)
    x = np.concatenate([inp["x_prompt"][b], xs], axis=0)
    w2ext = np.concatenate([inp["w_alpha2"][0], inp["b_alpha"][0][None, :]], axis=0)
    cwT = np.ascontiguousarray(inp["conv_w"][0].T.reshape(48, 128, 4).transpose(1, 0, 2))
    return {
        "x": np.ascontiguousarray(x, dtype=np.float32),
        "w_in": np.ascontiguousarray(inp["w_in"][0]),
        "w2ext": np.ascontiguousarray(w2ext),
        "cwT": cwT,
        "a_log": np.ascontiguousarray(inp["a_log"][0][None, :]),
        "dt_bias": np.ascontiguousarray(inp["dt_bias"][0][None, :]),
        "ln_in_g": np.ascontiguousarray(inp["ln_in_g"][0][None, :]),
        "final_norm_g": np.ascontiguousarray(inp["final_norm_g"][None, :]),
        "gla_norm_g": np.ascontiguousarray(inp["gla_norm_g"][0][None, :]),
        "gdn_norm_g": np.ascontiguousarray(inp["gdn_norm_g"][0][None, :]),
        "w_br_a": np.ascontiguousarray(inp["w_br_a"][0]),
        "w_br_b": np.ascontiguousarray(inp["w_br_b"][0]),
        "w_out": np.ascontiguousarray(inp["w_out"][0]),
        "state_gla": np.ascontiguousarray(inp["state_gla"][0, 16 * c:16 * c + 16]),
        "state_gdn": np.ascontiguousarray(inp["state_gdn"][0, 16 * c:16 * c + 16]),
        "state_conv": np.ascontiguousarray(inp["state_conv"][0, 16 * c:16 * c + 16]),
        "consts": _CONST,
    }


def kernel(**inputs):
    inp = {k: np.asarray(v) for k, v in inputs.items()}
    nc = build_nc()
    in_maps = [_core_inputs(c, inp) for c in range(8)]
    res = run_bass_kernel_spmd(nc, in_maps, core_ids=list(range(8)))
    r = res.results
    y_prompt = np.stack([r[2 * b]["y"][:NTP * 128] for b in range(4)], axis=0)
    y_sample = np.concatenate([r[c]["y"][NTP * 128:].reshape(16, 8, D) for c in range(8)], axis=0)
    gla_p = np.stack([r[2 * b]["gla_p"] for b in range(4)], axis=0)[None]
    gdn_p = np.stack([r[2 * b]["gdn_p"] for b in range(4)], axis=0)[None]
    conv_p = np.stack([r[2 * b]["conv_p"] for b in range(4)], axis=0)[None]
    gla_s = np.concatenate([r[c]["gla_s"] for c in range(8)], axis=0)[None]
    gdn_s = np.concatenate([r[c]["gdn_s"] for c in range(8)], axis=0)[None]
    conv_s = np.concatenate([r[c]["conv_s"] for c in range(8)], axis=0)[None]
    return (y_prompt.astype(np.float32), y_sample.astype(np.float32), gla_p.astype(np.float32),
            gdn_p.astype(np.float32), conv_p.astype(np.float32), gla_s.astype(np.float32),
            gdn_s.astype(np.float32), conv_s.astype(np.float32))
```

```python
import contextlib
import os
import numpy as np
import concourse.bass as bass
import concourse.mybir as mybir
from concourse.bass_utils import run_bass_kernel_spmd

F32 = mybir.dt.float32
BF16 = mybir.dt.bfloat16
AF = mybir.ActivationFunctionType
ALU = mybir.AluOpType

D = 2048
KT = 16
NTP = 16
NT = 17
TOK = NT * 128
NSEQ = 16
DIN = 18480
C_QA, C_KA, C_VA, C_LR, C_GA, C_QKV, C_BD, C_GB, C_MA, C_MB = 0, 1024, 2048, 4096, 4112, 6160, 12304, 12336, 14384, 16432
EPS = 1e-6
BIG = 3.0e4
NLEV = 7


class Bank:
    def __init__(self):
        self.reads = {}


class Buf:
    def __init__(self, t, name, bank=None):
        self.t = t
        self.name = name
        self.last_write = None
        self.reads = []
        self.bank = bank

    def __getitem__(self, idx):
        return V(self.t[idx], [self])

    def ap(self):
        return V(self.t.ap() if hasattr(self.t, "ap") and callable(getattr(self.t, "ap")) else self.t, [self])


class SubBuf:
    def __init__(self, parent, ap, name):
        self.parent = parent
        self.t = ap
        self.name = name

    def __getitem__(self, idx):
        return V(self.t[idx], [self.parent])


class V:
    def __init__(self, ap, keys):
        self.ap = ap
        self.keys = keys

    def __getitem__(self, idx):
        return V(self.ap[idx], self.keys)

    def re(self, pattern, **kw):
        return V(self.ap.rearrange(pattern, **kw), self.keys)

    def bc(self, shape):
        return V(self.ap.to_broadcast(list(shape)), self.keys)

    def un(self, axis):
        return V(self.ap.unsqueeze(axis), self.keys)

    def bitcast(self, dt):
        return V(self.ap.bitcast(dt), self.keys)


def _ap(x):
    return x.ap if isinstance(x, V) else x


class Prog:
    def __init__(self, nc):
        self.nc = nc
        self.stack = contextlib.ExitStack()
        self.engs = ["tensor", "vector", "scalar", "gpsimd", "sync"]
        self.ops = {e: [] for e in self.engs}
        self.sem = {}
        self.cnt = {e: 0 for e in self.engs}
        self.seen = {e: {} for e in self.engs}
        for e in self.engs:
            self.sem[e] = self.stack.enter_context(nc.semaphore("s_" + e))
        self.dma_sems = {}
        self.n_wait = 0
        self.n_ops = 0
        self.scopes = [self.stack]
        self.all_ops = {e: [] for e in self.engs}

    def push(self):
        self.scopes.append(contextlib.ExitStack())

    def pop(self):
        self.flush()
        self.scopes.pop().close()

    def sb(self, name, shape, dt):
        t = self.scopes[-1].enter_context(self.nc.sbuf_tensor(name, list(shape), dt))
        return Buf(t, name)

    def ps(self, name, shape, dt=F32):
        t = self.scopes[-1].enter_context(self.nc.psum_tensor(name, list(shape), dt))
        return Buf(t, name, bank=Bank())

    def dram(self, name, shape, dt, kind="Internal"):
        t = self.nc.dram_tensor(name, list(shape), dt, kind=kind)
        return Buf(t, name)

    def _collect(self, eng, ins, outs, same_ok):
        need = {}

        def add(sig):
            if sig is None:
                return
            sem, val, seng = sig
            if seng == eng and same_ok:
                return
            k = id(sem)
            if k not in need or need[k][1] < val:
                need[k] = (sem, val)
        for v in ins:
            if isinstance(v, V):
                for key in v.keys:
                    add(key.last_write)
                    if key.bank is not None:
                        for oe, sig in key.bank.reads.items():
                            if oe != eng:
                                add(sig)
        for v in outs:
            if isinstance(v, V):
                for key in v.keys:
                    add(key.last_write)
                    for r in key.reads:
                        add(r)
        waits = []
        for k, (sem, val) in need.items():
            if self.seen[eng].get(k, 0) >= val:
                continue
            self.seen[eng][k] = val
            waits.append((sem, val))
        return waits

    def _record(self, sig, ins, outs):
        for v in ins:
            if isinstance(v, V):
                for key in v.keys:
                    if key.bank is not None and sig[2] != "tensor":
                        key.bank.reads[sig[2]] = sig
                    key.reads.append(sig)
                    if len(key.reads) > 64:
                        best = {}
                        for s in key.reads:
                            if id(s[0]) not in best or best[id(s[0])][1] < s[1]:
                                best[id(s[0])] = s
                        key.reads = list(best.values())
        for v in outs:
            if isinstance(v, V):
                for key in v.keys:
                    key.last_write = sig
                    key.reads = []

    def op(self, eng, method, outs, ins, *args, same_ok=False, **kw):
        waits = self._collect(eng, ins, outs, same_ok)
        self.cnt[eng] += 1
        sig = (self.sem[eng], self.cnt[eng], eng)
        a2 = [_ap(a) for a in args]
        k2 = {k: _ap(v) for k, v in kw.items()}
        self.ops[eng].append((waits, method, a2, k2, (self.sem[eng], 1)))
        self.n_wait += len(waits)
        self.n_ops += 1
        self._record(sig, ins, outs)

    def dma(self, q, out, in_, semkey=None, **kw):
        waits = self._collect(q, [in_], [out], False)
        if semkey is None:
            semkey = out.keys[0]
        k = id(semkey)
        if k not in self.dma_sems:
            s = self.stack.enter_context(self.nc.semaphore("d_" + semkey.name))
            self.dma_sems[k] = [s, 0]
        ent = self.dma_sems[k]
        ent[1] += 16
        sig = (ent[0], ent[1], "dma")
        self.ops[q].append((waits, "dma_start", [], dict(out=_ap(out), in_=_ap(in_), **kw), (ent[0], 16)))
        self.n_wait += len(waits)
        self.n_ops += 1
        self._record(sig, [in_], [out])

    def final_wait(self, eng, bufs):
        need = {}
        for b in bufs:
            for sig in [b.last_write] + b.reads:
                if sig is None:
                    continue
                sem, val, _ = sig
                k = id(sem)
                if k not in need or need[k][1] < val:
                    need[k] = (sem, val)
        self.ops[eng].append((list(need.values()), None, None, None, None))

    def audit(self):
        final = {id(self.sem[e]): (self.cnt[e], "eng:" + e) for e in self.engs}
        for k, (sm, c) in self.dma_sems.items():
            final[id(sm)] = (c, "dma:" + sm.name)
        bad = 0
        mx = 0
        for e in self.engs:
            for (waits, method, a, k, inc) in self.all_ops[e]:
                for (sm, val) in waits:
                    mx = max(mx, val)
                    if id(sm) not in final or final[id(sm)][0] < val:
                        bad += 1
                        if bad < 10:
                            print("AUDIT: unsatisfiable wait on", e, getattr(sm, "name", sm), val, final.get(id(sm)))
        print("AUDIT: bad waits", bad, "max wait value", mx, "engine counts", dict(self.cnt),
              "max dma sem", max([c for (_, c) in self.dma_sems.values()] + [0]), "n dma sems", len(self.dma_sems))
        return bad

    def flush(self):
        for e in self.engs:
            self.all_ops[e].extend(self.ops[e])
        if any(self.ops[e] for e in self.engs):
            self.emit()
        self.ops = {e: [] for e in self.engs}

    def emit(self):
        nc = self.nc
        self.n_blocks = getattr(self, "n_blocks", 0) + 1
        with nc.named_scope("blk%02d_%s" % (self.n_blocks, getattr(self, "scope_name", "x"))), nc.Block() as block:
            def mk(ename):
                def body(e):
                    for (waits, method, a, k, inc) in self.ops[ename]:
                        for (sem, val) in waits:
                            e.wait_ge(sem, val)
                        if method is None:
                            continue
                        ins = getattr(e, method)(*a, **k)
                        ins.then_inc(inc[0], inc[1])
                return body
            block.tensor(mk("tensor"))
            block.vector(mk("vector"))
            block.scalar(mk("scalar"))
            block.gpsimd(mk("gpsimd"))
            block.sync(mk("sync"))

    def close(self):
        self.stack.close()


def _const_table():
    idx = np.arange(128)
    P_, F_ = np.meshgrid(idx, idx, indexing="ij")
    segp = np.zeros_like(P_)
    segs_p, segs_f = P_ // 8, F_ // 8
    c = {}
    c["ident"] = (P_ == F_).astype(np.float32)
    c["ones"] = np.ones((128, 128), np.float32)
    c["notI"] = (P_ != F_).astype(np.float32)
    for x, same in (("p", np.ones_like(P_, bool)), ("s", segs_p == segs_f)):
        c["triT_" + x] = ((P_ <= F_) & same).astype(np.float32)
        c["sufT_" + x] = ((P_ > F_) & same).astype(np.float32)
        c["bigU_" + x] = np.where((F_ <= P_) & same, 0.0, BIG).astype(np.float32)
        c["nbigL_" + x] = np.where((F_ >= P_) & same, 0.0, -BIG).astype(np.float32)
        c["bigUs_" + x] = np.where((F_ < P_) & same, 0.0, BIG).astype(np.float32)
    lm = np.zeros((128, NLEV, 128), np.float32)
    for l in range(NLEV):
        s = 1 << l
        lm[:, l, :] = ((P_ // (2 * s) == F_ // (2 * s)) & (P_ % (2 * s) >= s) & (F_ % (2 * s) < s))
    c["lmask"] = lm.reshape(128, NLEV * 128)
    segF = np.zeros((128, NSEQ, 128), np.float32)
    for s in range(NSEQ):
        segF[:, s, 8 * s:8 * s + 8] = 1.0
    c["segF"] = segF.reshape(128, NSEQ * 128)
    c["segP"] = (idx[:, None] // 8 == np.arange(NSEQ)[None, :]).astype(np.float32)
    c["selL_s"] = (idx[:, None] == (8 * np.arange(NSEQ)[None, :] + 7)).astype(np.float32)
    c["selL_p"] = (idx[:, None] == 127).astype(np.float32)
    offs = {}
    cols = []
    o = 0
    for k, v in c.items():
        offs[k] = (o, v.shape[1])
        cols.append(v)
        o += v.shape[1]
    return np.ascontiguousarray(np.concatenate(cols, axis=1)), offs


_CONST, _COFF = _const_table()
NCONST = _CONST.shape[1]


def build_nc(upto=99, debug=False):
    nc = bass.Bass("TRN2", target_bir_lowering=False)
    P = Prog(nc)

    def din(name, shape, dt=F32):
        return Buf(nc.dram_tensor(name, list(shape), dt, kind="ExternalInput"), name)

    def dout(name, shape, dt=F32):
        return Buf(nc.dram_tensor(name, list(shape), dt, kind="ExternalOutput"), name)

    def dscr(name, shape, dt=F32):
        if debug:
            return dout(name, shape, dt)
        return P.dram(name, shape, dt)

    x_d = din("x", [TOK, D])
    w_in_d = din("w_in", [D, DIN])
    w2e_d = din("w2ext", [17, 1024])
    cwT_d = din("cwT", [128, 48, 4])
    alog_d = din("a_log", [1, 16])
    dtb_d = din("dt_bias", [1, 16])
    lng_d = din("ln_in_g", [1, D])
    fng_d = din("final_norm_g", [1, D])
    gna_d = din("gla_norm_g", [1, 512])
    gnb_d = din("gdn_norm_g", [1, 128])
    wba_d = din("w_br_a", [D, D])
    wbb_d = din("w_br_b", [D, D])
    wo_d = din("w_out", [D, D])
    sgla_d = din("state_gla", [NSEQ, 4, 256, 512])
    sgdn_d = din("state_gdn", [NSEQ, 16, 128, 128])
    sconv_d = din("state_conv", [NSEQ, 3, 6144])
    const_d = din("consts", [128, NCONST])

    y_d = dout("y", [TOK, D])
    glap_d = dout("gla_p", [4, 256, 512])
    gdnp_d = dout("gdn_p", [16, 128, 128])
    convp_d = dout("conv_p", [3, 6144])
    glas_d = dout("gla_s", [NSEQ, 4, 256, 512])
    gdns_d = dout("gdn_s", [NSEQ, 16, 128, 128])
    convs_d = dout("conv_s", [NSEQ, 3, 6144])
    outs_all = [y_d, glap_d, gdnp_d, convp_d, glas_d, gdns_d, convs_d]

    s_ka_tm = dscr("s_ka_tm", [TOK, 1024])
    s_va = dscr("s_va", [TOK, 2048], BF16)
    s_ga = dscr("s_ga", [TOK, 2048])
    s_bd = dscr("s_bd", [TOK, 32])
    s_gb = dscr("s_gb", [TOK, 2048])
    s_ma = dscr("s_ma", [TOK, 2048])
    s_mb = dscr("s_mb", [TOK, 2048])
    s_qkv_tm = dscr("s_qkv_tm", [256, 6144])
    s_qa_fm = dscr("s_qa_fm", [1024, TOK])
    s_ka_fm = dscr("s_ka_fm", [1024, TOK])
    s_lr_fm = dscr("s_lr_fm", [16, TOK])
    s_qkv_fm = dscr("s_qkv_fm", [6144, TOK])
    s_oaT = dscr("s_oaT", [2048, TOK], BF16)
    s_obT = dscr("s_obT", [2048, TOK], BF16)
    dbg = [s_ka_tm, s_va, s_ga, s_bd, s_gb, s_ma, s_mb, s_qkv_tm, s_qa_fm, s_ka_fm, s_lr_fm, s_qkv_fm, s_oaT, s_obT]

    cst = P.sb("cst", [128, NCONST], F32)
    P.dma("sync", cst[:], const_d[:])

    def C(name):
        o, n = _COFF[name]
        return cst[:, o:o + n]

    ident_bf = P.sb("ident_bf", [128, 128], BF16)
    P.op("vector", "tensor_copy", [ident_bf[:]], [cst[:]], out=ident_bf[:], in_=C("ident"))

    P.scope_name = "p12_gemm"
    P.push()
    gB = P.sb("gB", [128, D], F32)
    def bload(dst, src, n):
        P.dma("sync", dst, V(src.t.ap()[0:1, 0:n].partition_broadcast(128)[:, 0, :], [src]))

    bload(gB[:], lng_d, D)

    psA = [P.ps("psA%d" % i, [128, 512], F32) for i in range(4)]
    psT = [P.ps("psT%d" % i, [128, 1024], BF16) for i in range(2)]
    psM = [P.ps("psM%d" % i, [128, 512], F32) for i in range(2)]

    hT = P.sb("hT", [128, KT, TOK], BF16)
    xt = [P.sb("xt%d" % i, [128, D], F32) for i in range(2)]
    hb = [P.sb("hb%d" % i, [128, D], BF16) for i in range(2)]
    junk = P.sb("junk", [128, D], BF16)
    ss = [P.sb("ss%d" % i, [128, 1], F32) for i in range(2)]
    rs = [P.sb("rs%d" % i, [128, 1], F32) for i in range(2)]
    eps_t = P.sb("eps_t", [128, 1], F32)
    P.op("vector", "memset", [eps_t[:]], [], eps_t[:], EPS)
    for t in range(NT):
        b = t % 2
        P.dma("sync", xt[b][:], x_d[t * 128:(t + 1) * 128, :])
        P.op("scalar", "activation", [junk[:], ss[b][:]], [xt[b][:]], out=junk[:], in_=xt[b][:], func=AF.Square,
             accum_out=ss[b][:])
        P.op("scalar", "activation", [rs[b][:]], [ss[b][:], eps_t[:]], out=rs[b][:], in_=ss[b][:], func=AF.Sqrt,
             bias=eps_t[:], scale=1.0 / D)
        P.op("vector", "reciprocal", [rs[b][:]], [rs[b][:]], out=rs[b][:], in_=rs[b][:])
        P.op("vector", "scalar_tensor_tensor", [hb[b][:]], [xt[b][:], rs[b][:], gB[:]], out=hb[b][:], in0=xt[b][:],
             scalar=rs[b][:, 0:1], in1=gB[:], op0=ALU.mult, op1=ALU.mult)
        for g in range(2):
            pt = psT[g]
            for j in range(8):
                k = g * 8 + j
                P.op("tensor", "transpose", [pt[:]], [hb[b][:], ident_bf[:]], out=pt[:, j * 128:(j + 1) * 128],
                     in_=hb[b][:, k * 128:(k + 1) * 128], identity=ident_bf[:], same_ok=True)
            dst = hT[:, g * 8:(g + 1) * 8, t * 128:(t + 1) * 128]
            src = pt[:].re("p (a b) -> p a b", a=8)
            if g == 0:
                P.op("vector", "tensor_copy", [dst], [pt[:]], out=dst, in_=src)
            else:
                P.op("scalar", "copy", [dst], [pt[:]], out=dst, in_=src)

    wblk = [P.sb("wblk%d" % i, [128, KT, 512], BF16) for i in range(2)]
    stg = [P.sb("stg%d" % i, [128, 512], F32) for i in range(4)]
    stgb = [P.sb("stgb%d" % i, [128, 512], BF16) for i in range(2)]
    all_tiles = list(range(NT))
    groups = [
        (C_QA, 1024, s_qa_fm, None, None, False),
        (C_KA, 1024, s_ka_fm, s_ka_tm, all_tiles, False),
        (C_VA, 2048, None, s_va, all_tiles, True),
        (C_LR, 16, s_lr_fm, None, None, False),
        (C_GA, 2048, None, s_ga, all_tiles, False),
        (C_QKV, 6144, s_qkv_fm, s_qkv_tm, [NTP - 1, NT - 1], False),
        (C_BD, 32, None, s_bd, all_tiles, False),
        (C_GB, 2048, None, s_gb, all_tiles, False),
        (C_MA, 2048, None, s_ma, all_tiles, False),
        (C_MB, 2048, None, s_mb, all_tiles, False),
    ]
    blocks = []
    for (c0, n, fm, tm, tiles, tbf) in groups:
        for o in range(0, n, 512):
            blocks.append((c0 + o, min(512, n - o), o, fm, tm, tiles, tbf))
    w_view = w_in_d.t.ap().rearrange("(ko ki) c -> ki ko c", ki=128)
    cnt = {"ps": 0, "stg": 0, "stgb": 0, "ev": 0}

    def evac(dst, src):
        cnt["ev"] += 1
        if cnt["ev"] % 2:
            P.op("vector", "tensor_copy", [dst], [src], out=dst, in_=src)
        else:
            P.op("scalar", "copy", [dst], [src], out=dst, in_=src)

    tok_groups = [(0, 512), (512, 512), (1024, 512), (1536, 512), (2048, 128)]
    if upto >= 2:
        for bi, (cg, n, off, fm, tm, tiles, tbf) in enumerate(blocks):
            wb = wblk[bi % 2]
            for q4 in range(4):
                P.dma("gpsimd", wb[:, q4 * 4:(q4 + 1) * 4, 0:n], V(w_view[:, q4 * 4:(q4 + 1) * 4, cg:cg + n], [w_in_d]))
            if fm is not None:
                for sbo in range(0, n, 128):
                    m = min(128, n - sbo)
                    for (t0, tn) in tok_groups:
                        ps = psA[cnt["ps"] % 4]
                        cnt["ps"] += 1
                        for k in range(KT):
                            P.op("tensor", "matmul", [ps[:]], [wb[:], hT[:]], ps[0:m, 0:tn], wb[:, k, sbo:sbo + m],
                                 hT[:, k, t0:t0 + tn], start=(k == 0), stop=(k == KT - 1), same_ok=True)
                        st = stg[cnt["stg"] % 4]
                        cnt["stg"] += 1
                        evac(st[0:m, 0:tn], ps[0:m, 0:tn])
                        P.dma("sync", fm[off + sbo:off + sbo + m, t0:t0 + tn], st[0:m, 0:tn], semkey=st)
            if tm is not None:
                for t in tiles:
                    ps = psA[cnt["ps"] % 4]
                    cnt["ps"] += 1
                    for k in range(KT):
                        P.op("tensor", "matmul", [ps[:]], [wb[:], hT[:]], ps[:, 0:n], hT[:, k, t * 128:(t + 1) * 128],
                             wb[:, k, 0:n], start=(k == 0), stop=(k == KT - 1), same_ok=True)
                    if tbf:
                        st = stgb[cnt["stgb"] % 2]
                        cnt["stgb"] += 1
                    else:
                        st = stg[cnt["stg"] % 4]
                        cnt["stg"] += 1
                    evac(st[:, 0:n], ps[:, 0:n])
                    if tm is s_qkv_tm:
                        r0 = 0 if t == NTP - 1 else 128
                    else:
                        r0 = t * 128
                    P.dma("sync", tm[r0:r0 + 128, off:off + n], st[:, 0:n], semkey=st)

    P.pop()

    if upto >= 3:
        _phase_gla(P, C, locals())
    if upto >= 4:
        _phase_gdn(P, C, locals())
    if upto >= 5:
        _phase_post(P, C, locals())
    final = list(outs_all) + (dbg if debug else [])
    P.final_wait("sync", final + dbg)
    P.flush()
    P.close()
    print("ops", P.n_ops, "waits", P.n_wait)
    P.audit()
    return nc


def _phase_gla(P, C, L):
    d = L
    P.scope_name = "p3_gla"
    s_qa_fm, s_ka_fm, s_ka_tm, s_va, s_ga, s_lr_fm, s_oaT = (d[k] for k in
        ["s_qa_fm", "s_ka_fm", "s_ka_tm", "s_va", "s_ga", "s_lr_fm", "s_oaT"])
    w2e_d, gna_d, sgla_d, glap_d, glas_d, ident_bf = (d[k] for k in
        ["w2e_d", "gna_d", "sgla_d", "glap_d", "glas_d", "ident_bf"])
    P.push()
    w2e = P.sb("w2e", [17, 1024], F32)
    P.dma("sync", w2e[:], w2e_d[:])
    gnA = P.sb("gnA", [128, 512], F32)
    P.dma("sync", gnA[:], V(gna_d.t.ap()[0:1, :].partition_broadcast(128)[:, 0, :], [gna_d]))
    eps_t = P.sb("eps3", [128, 1], F32)
    P.op("vector", "memset", [eps_t[:]], [], eps_t[:], EPS)
    S = [P.sb("S%d" % h, [128, 2, 512], F32) for h in range(4)]
    Sbf = [P.sb("Sbf%d" % h, [128, 2, 512], BF16) for h in range(4)]
    for h in range(4):
        P.op("gpsimd", "memset", [S[h][:]], [], S[h][:], 0.0)
        P.op("gpsimd", "memset", [Sbf[h][:]], [], Sbf[h][:], 0.0)
    NB = 2
    lrT = [P.sb("lrT%d" % i, [17, 128], F32) for i in range(NB)]
    for i in range(NB):
        P.op("vector", "memset", [lrT[i][:]], [], lrT[i][:], 1.0)

    def mk(name, shape, dt):
        return [P.sb("%s%d" % (name, i), shape, dt) for i in range(NB)]
    qT = mk("qT", [128, 2, 128], F32); kT = mk("kT", [128, 2, 128], F32)
    ktok = mk("ktok", [128, 256], F32); vv = mk("vv", [128, 512], BF16); gate = mk("gate", [128, 512], F32)
    ee = mk("ee", [128, 256], F32); sp = mk("sp", [128, 256], F32)
    ebT = mk("ebT", [128, 2, 128], F32); enbT = mk("enbT", [128, 2, 128], F32); esuf = mk("esuf", [128, 256], F32)
    qd = mk("qd", [128, 2, 128], BF16); kd = mk("kd", [128, 2, 128], BF16); kend = mk("kend", [128, 256], BF16)
    attT = mk("attT", [128, 128], BF16)
    on = mk("on", [128, 512], F32); gs = mk("gs", [128, 512], F32); og = mk("og", [128, 512], BF16)
    oT = mk("oT", [128, 4, 128], BF16)
    junk = P.sb("junk3", [128, 512], BF16)
    ssq = mk("ssq", [128, 1], F32); rstd = mk("rstd", [128, 1], F32)
    qm = P.sb("qm", [128, 2, NSEQ, 128], BF16)
    kendm = P.sb("kendm", [128, NSEQ, 256], BF16)
    S0 = [P.sb("S0_%d" % i, [128, 2, 512], F32) for i in range(3)]
    S0b = [P.sb("S0b_%d" % i, [128, 2, 512], BF16) for i in range(3)]
    S1 = [P.sb("S1_%d" % i, [128, 2, 512], F32) for i in range(3)]
    z_ps = P.ps("z_ps", [128, 256], F32)
    bT_ps = P.ps("bT_ps", [128, 2, 128], F32)
    bsuf_ps = P.ps("bsuf_ps", [128, 256], F32)
    att_ps = P.ps("att_ps", [128, 128], F32)
    o_ps = P.ps("o_ps", [128, 512], F32)
    sn_ps = [P.ps("sn_ps%d" % i, [128, 512], F32) for i in range(2)]
    tp_ps = P.ps("tp_ps", [128, 4, 128], BF16)
    it = 0
    nsn = 0
    for t in range(NT):
        samp = (t == NT - 1)
        x = "s" if samp else "p"
        triT, sufT = C("triT_" + x), C("sufT_" + x)
        tok = slice(t * 128, (t + 1) * 128)
        for h in range(4):
            b = it % NB
            it += 1
            P.dma("sync", lrT[b][0:16, :], s_lr_fm[:, tok])
            P.dma("sync", qT[b][:], s_qa_fm[h * 256:(h + 1) * 256, tok].re("(c p) t -> p c t", p=128))
            P.dma("sync", kT[b][:], s_ka_fm[h * 256:(h + 1) * 256, tok].re("(c p) t -> p c t", p=128))
            P.dma("sync", ktok[b][:], s_ka_tm[tok, h * 256:(h + 1) * 256])
            P.dma("sync", vv[b][:], s_va[tok, h * 512:(h + 1) * 512])
            P.dma("sync", gate[b][:], s_ga[tok, h * 512:(h + 1) * 512])
            P.op("tensor", "matmul", [z_ps[:]], [lrT[b][:], w2e[:]], z_ps[:], lrT[b][:], w2e[:, h * 256:(h + 1) * 256],
                 start=True, stop=True)
            P.op("scalar", "activation", [ee[b][:]], [z_ps[:]], out=ee[b][:], in_=z_ps[:], func=AF.Exp, scale=-1.0)
            P.op("scalar", "activation", [sp[b][:]], [ee[b][:]], out=sp[b][:], in_=ee[b][:], func=AF.Ln, bias=1.0, scale=1.0)
            for c in range(2):
                P.op("tensor", "matmul", [bT_ps[:]], [sp[b][:], triT], bT_ps[:, c, :], sp[b][:, c * 128:(c + 1) * 128], triT,
                     start=True, stop=True, same_ok=True)
            P.op("tensor", "matmul", [bsuf_ps[:]], [sp[b][:], sufT], bsuf_ps[:], sufT, sp[b][:], start=True, stop=True)
            P.op("scalar", "activation", [ebT[b][:]], [bT_ps[:]], out=ebT[b][:], in_=bT_ps[:], func=AF.Exp, scale=-1.0 / 16)
            P.op("scalar", "activation", [enbT[b][:]], [bT_ps[:]], out=enbT[b][:], in_=bT_ps[:], func=AF.Exp, scale=1.0 / 16)
            P.op("scalar", "activation", [esuf[b][:]], [bsuf_ps[:]], out=esuf[b][:], in_=bsuf_ps[:], func=AF.Exp, scale=-1.0 / 16)
            P.op("vector", "scalar_tensor_tensor", [qd[b][:]], [qT[b][:], ebT[b][:]], out=qd[b][:], in0=qT[b][:], scalar=0.0625,
                 in1=ebT[b][:], op0=ALU.mult, op1=ALU.mult)
            P.op("vector", "tensor_tensor", [kd[b][:]], [kT[b][:], enbT[b][:]], out=kd[b][:], in0=kT[b][:], in1=enbT[b][:], op=ALU.mult)
            P.op("gpsimd", "tensor_tensor", [kend[b][:]], [ktok[b][:], esuf[b][:]], out=kend[b][:], in0=ktok[b][:], in1=esuf[b][:],
                 op=ALU.mult)
            for c in range(2):
                P.op("tensor", "matmul", [att_ps[:]], [kd[b][:], qd[b][:]], att_ps[:], kd[b][:, c, :], qd[b][:, c, :],
                     start=(c == 0), stop=(c == 1), same_ok=True)
            P.op("vector", "tensor_tensor", [attT[b][:]], [att_ps[:], triT], out=attT[b][:], in0=att_ps[:], in1=triT, op=ALU.mult)
            P.op("tensor", "matmul", [o_ps[:]], [attT[b][:], vv[b][:]], o_ps[:], attT[b][:], vv[b][:], start=True, stop=False)
            if not samp:
                for c in range(2):
                    P.op("tensor", "matmul", [o_ps[:]], [qd[b][:], Sbf[h][:]], o_ps[:], qd[b][:, c, :], Sbf[h][:, c, :],
                         start=False, stop=(c == 1), same_ok=True)
                for c in range(2):
                    sn = sn_ps[nsn % 2]; nsn += 1
                    P.op("tensor", "matmul", [sn[:]], [kend[b][:], vv[b][:]], sn[:], kend[b][:, c * 128:(c + 1) * 128], vv[b][:],
                         start=True, stop=True)
                    P.op("vector", "scalar_tensor_tensor", [S[h][:]], [S[h][:], ebT[b][:], sn[:]], out=S[h][:, c, :],
                         in0=S[h][:, c, :], scalar=ebT[b][:, c, 127:128], in1=sn[:], op0=ALU.mult, op1=ALU.add)
                P.op("scalar", "copy", [Sbf[h][:]], [S[h][:]], out=Sbf[h][:], in_=S[h][:])
                if t == NTP - 1:
                    P.dma("sync", glap_d[h].re("(c p) v -> p c v", p=128), S[h][:], semkey=S[h])
            else:
                for c in range(2):
                    P.op("vector", "tensor_tensor", [qm[:]], [qd[b][:], C("segF")], out=qm[:, c, :, :],
                         in0=qd[b][:, c, :].un(1).bc([128, NSEQ, 128]), in1=C("segF").re("p (s t) -> p s t", s=NSEQ), op=ALU.mult)
                P.op("gpsimd", "tensor_tensor", [kendm[:]], [kend[b][:], C("segP")], out=kendm[:],
                     in0=kend[b][:].un(1).bc([128, NSEQ, 256]), in1=C("segP").un(2).bc([128, NSEQ, 256]), op=ALU.mult)
                for s in range(NSEQ):
                    r = s % 3
                    P.dma("sync", S0[r][:], sgla_d[s, h].re("(c p) v -> p c v", p=128))
                    P.op("scalar", "copy", [S0b[r][:]], [S0[r][:]], out=S0b[r][:], in_=S0[r][:])
                    for c in range(2):
                        P.op("tensor", "matmul", [o_ps[:]], [qm[:], S0b[r][:]], o_ps[:], qm[:, c, s, :], S0b[r][:, c, :],
                             start=False, stop=(s == NSEQ - 1 and c == 1), same_ok=True)
                    for c in range(2):
                        sn = sn_ps[nsn % 2]; nsn += 1
                        P.op("tensor", "matmul", [sn[:]], [kendm[:], vv[b][:]], sn[:], kendm[:, s, c * 128:(c + 1) * 128], vv[b][:],
                             start=True, stop=True)
                        P.op("vector", "scalar_tensor_tensor", [S1[r][:]], [S0[r][:], ebT[b][:], sn[:]], out=S1[r][:, c, :],
                             in0=S0[r][:, c, :], scalar=ebT[b][:, c, 8 * s + 7:8 * s + 8], in1=sn[:], op0=ALU.mult, op1=ALU.add)
                    P.dma("sync", glas_d[s, h].re("(c p) v -> p c v", p=128), S1[r][:], semkey=S1[r])
            P.op("scalar", "activation", [junk[:], ssq[b][:]], [o_ps[:]], out=junk[:], in_=o_ps[:], func=AF.Square, accum_out=ssq[b][:])
            P.op("scalar", "activation", [rstd[b][:]], [ssq[b][:], eps_t[:]], out=rstd[b][:], in_=ssq[b][:], func=AF.Sqrt,
                 bias=eps_t[:], scale=1.0 / 512)
            P.op("vector", "reciprocal", [rstd[b][:]], [rstd[b][:]], out=rstd[b][:], in_=rstd[b][:])
            P.op("vector", "scalar_tensor_tensor", [on[b][:]], [o_ps[:], rstd[b][:], gnA[:]], out=on[b][:], in0=o_ps[:],
                 scalar=rstd[b][:, 0:1], in1=gnA[:], op0=ALU.mult, op1=ALU.mult)
            P.op("scalar", "activation", [gs[b][:]], [gate[b][:]], out=gs[b][:], in_=gate[b][:], func=AF.Silu)
            P.op("gpsimd", "tensor_tensor", [og[b][:]], [on[b][:], gs[b][:]], out=og[b][:], in0=on[b][:], in1=gs[b][:], op=ALU.mult)
            for j in range(4):
                P.op("tensor", "transpose", [tp_ps[:]], [og[b][:], ident_bf[:]], out=tp_ps[:, j, :], in_=og[b][:, j * 128:(j + 1) * 128],
                     identity=ident_bf[:], same_ok=True)
            P.op("vector", "tensor_copy", [oT[b][:]], [tp_ps[:]], out=oT[b][:], in_=tp_ps[:])
            P.dma("sync", s_oaT[h * 512:(h + 1) * 512, tok].re("(j p) t -> p j t", p=128), oT[b][:], semkey=oT[b])
    P.pop()


def _phase_gdn(P, C, L):
    d = L
    P.scope_name = "p4_gdn"
    s_qkv_fm, s_qkv_tm, s_bd, s_gb, s_obT = (d[k] for k in ["s_qkv_fm", "s_qkv_tm", "s_bd", "s_gb", "s_obT"])
    cwT_d, alog_d, dtb_d, gnb_d, sgdn_d, sconv_d, gdnp_d, gdns_d, convp_d, convs_d, ident_bf = (d[k] for k in
        ["cwT_d", "alog_d", "dtb_d", "gnb_d", "sgdn_d", "sconv_d", "gdnp_d", "gdns_d", "convp_d", "convs_d", "ident_bf"])
    P.push()

    def bl(name, src, n):
        t = P.sb(name, [128, n], F32)
        P.dma("sync", t[:], V(src.t.ap()[0:1, 0:n].partition_broadcast(128)[:, 0, :], [src]))
        return t
    gnB = bl("gnB", gnb_d, 128)
    alog_b = bl("alog_b", alog_d, 16)
    dtb_b = bl("dtb_b", dtb_d, 16)
    negA = P.sb("negA", [128, 16], F32)
    P.op("scalar", "activation", [negA[:]], [alog_b[:]], out=negA[:], in_=alog_b[:], func=AF.Exp)
    P.op("vector", "tensor_scalar", [negA[:]], [negA[:]], out=negA[:], in0=negA[:], scalar1=-1.0, scalar2=None, op0=ALU.mult)
    cw = P.sb("cw", [128, 48, 4], F32)
    P.dma("sync", cw[:], cwT_d[:])
    eps_t = P.sb("eps4", [128, 1], F32)
    P.op("vector", "memset", [eps_t[:]], [], eps_t[:], EPS)
    P.push()
    cvs = P.sb("cvs", [51, 6144], F32)
    P.dma("sync", cvs[48:51, :], s_qkv_tm[125:128, :])
    P.dma("sync", convp_d[:], cvs[48:51, :], semkey=cvs)
    for s in range(NSEQ):
        P.dma("sync", cvs[3 * s:3 * s + 3, :], s_qkv_tm[128 + 8 * s + 5:128 + 8 * s + 8, :])
    P.dma("sync", convs_d.ap().re("s r c -> (s r) c"), cvs[0:48, :], semkey=cvs)
    P.pop()
    sct = P.sb("sct", [48, 6144], F32)
    P.dma("sync", sct[:], sconv_d.ap().re("s r c -> (s r) c"))
    S = [P.sb("Sg%d" % i, [128, 128], F32) for i in range(16)]
    Sbf = [P.sb("Sgbf%d" % i, [128, 128], BF16) for i in range(16)]
    for i in range(16):
        P.op("gpsimd", "memset", [S[i][:]], [], S[i][:], 0.0)
        P.op("gpsimd", "memset", [Sbf[i][:]], [], Sbf[i][:], 0.0)
    prev3 = P.sb("prev3", [128, 16, 3, 3], F32)
    P.op("gpsimd", "memset", [prev3[:]], [], prev3[:], 0.0)
    bd = P.sb("bd", [128, 32], F32)
    beta = P.sb("beta", [128, 16], F32); tmp16 = P.sb("tmp16", [128, 16], F32); g = P.sb("g", [128, 16], F32)
    gh = P.sb("gh", [128, 16], F32); eg = P.sb("eg", [128, 16], F32); egsuf = P.sb("egsuf", [128, 16], F32)
    bq = P.sb("bq", [128, 16], F32)
    Rl = P.sb("Rl", [128, NSEQ, 16], F32); egl = P.sb("egl", [128, NSEQ, 16], F32)
    diag = P.sb("diag", [128, 16, 128], F32)
    E1 = P.sb("E1", [128, 16, 128], F32); M1 = diag
    Ds = P.sb("Ds", [128, 16, 128], F32); DT = P.sb("DT", [128, 16, 128], F32); egB = P.sb("egB", [128, 16, 128], F32)
    gbt = P.sb("gbt", [128, 2048], F32); gsb = gbt
    ogall = P.sb("ogall", [128, 2048], BF16)
    oTs = [P.sb("oTs%d" % i, [128, 8, 128], BF16) for i in range(2)]
    NB = int(os.environ.get("GDN_G", "3"))

    def mk(name, shape, dt):
        return [P.sb("%s%d" % (name, i), shape, dt) for i in range(NB)]
    ext = mk("ext", [128, 3, 176], F32)
    cacc = mk("cacc", [128, 3, 128], F32); act = mk("act", [128, 3, 128], F32)
    sq = mk("sq", [128, 2, 128], F32); rn = mk("rn", [128, 2, 128], F32)
    qn = mk("qn", [128, 128], BF16); qg = mk("qg", [128, 128], BF16); knf = mk("knf", [128, 128], F32); kn = mk("kn", [128, 128], BF16)
    rhsv = mk("rhsv", [128, 128], BF16); rhsw = mk("rhsw", [128, 128], BF16); kend = mk("kendg", [128, 128], BF16)
    mm = mk("mm", [128, 128], F32); Cl = mk("Cl", [128, NLEV, 128], F32)
    U = mk("U", [128, 128], F32); X = mk("X", [128, 128], F32); Psb = mk("Psb", [128, 128], F32); Ubf = mk("Ubf", [128, 128], BF16)
    nwT = mk("nwT", [128, 128], BF16); vnew = mk("vnew", [128, 128], BF16); qkT = mk("qkT", [128, 128], BF16)
    on = mk("ong", [128, 128], F32); ssq = mk("ssqg", [128, 1], F32); rstd = mk("rstdg", [128, 1], F32)
    junk = P.sb("junk4", [128, 128], BF16)
    nwTm = P.sb("nwTm", [128, NSEQ, 128], BF16); qgm = P.sb("qgm", [128, NSEQ, 128], BF16); kendm = P.sb("kendmg", [128, NSEQ, 128], BF16)
    S0h = [P.sb("S0h", [128, NSEQ, 128], F32)] * NB; S0hb = [P.sb("S0hb", [128, NSEQ, 128], BF16)] * NB
    S1h = [P.sb("S1h", [128, NSEQ, 128], F32)] * NB
    banks = [P.ps("gbank%d" % i, [128, 512], F32) for i in range(7)]

    def sub(bk, c0, n, name, pat=None, **kw):
        ap = banks[bk].t[:, c0:c0 + n]
        if pat:
            ap = ap.rearrange(pat, **kw)
        return SubBuf(banks[bk], ap, name)
    CB = NB
    ghB_half = [sub(CB, 0, 256, "ghB_ps0", "p (a b) -> p a b", a=2)] * 2
    sm_ps = sub(CB, 256, 32, "sm_ps")
    glB_ps = sub(CB, 288, 128, "glB_ps", "p (a b) -> p a b", b=16)
    SN = [NB + 1, NB + 2]
    sn_bank = [sub(SN[i], 0, 512, "sng_ps%d" % i, "p (a b) -> p a b", a=4) for i in range(2)]
    tp_ps = P.ps("tpg_ps", [128, 8, 128], BF16)
    slot_ps = []
    for i in range(NB):
        slot_ps.append(dict(
            ss=sub(i, 0, 256, "ss%d" % i), tk=sub(i, 256, 256, "tk%d" % i, "p (a b) -> p a b", a=2), tpc=sub(i, 0, 48, "tpc%d" % i),
            kk=sub(i, 0, 128, "kk%d" % i), lp=sub(i, 128, 128, "lp%d" % i), lu=sub(i, 256, 128, "lu%d" % i), lx=sub(i, 384, 128, "lx%d" % i),
            wT=sub(i, 0, 128, "wT%d" % i), vn=sub(i, 128, 128, "vn%d" % i), qk=sub(i, 256, 128, "qk%d" % i), o=sub(i, 384, 128, "o%d" % i)))
    ident, ones = C("ident"), C("ones")
    it = 0
    G_T = [int(v) for v in os.environ.get("GDN_T", ",".join(str(i) for i in range(NT))).split(",") if v != ""]
    G_H = int(os.environ.get("GDN_H", "16"))
    G_A = int(os.environ.get("GDN_A", "1000"))
    for t in G_T:
        samp = (t == NT - 1)
        x = "s" if samp else "p"
        nseg = NSEQ if samp else 1
        nlev = 3 if samp else NLEV
        triT, sufT = C("triT_" + x), C("sufT_" + x)
        tok = slice(t * 128, (t + 1) * 128)
        _ac = [0]

        def _A():
            _ac[0] += 1
            return _ac[0] <= G_A
        if _A():
            P.dma("sync", bd[:], s_bd[tok, :])
        if _A():
            P.dma("sync", gbt[:], s_gb[tok, :])
        if _A():
            P.op("scalar", "activation", [gsb[:]], [gbt[:]], out=gsb[:], in_=gbt[:], func=AF.Silu)
        if _A():
            P.op("scalar", "activation", [beta[:]], [bd[:]], out=beta[:], in_=bd[:, 0:16], func=AF.Sigmoid)
        if _A():
            P.op("vector", "tensor_tensor", [tmp16[:]], [bd[:], dtb_b[:]], out=tmp16[:], in0=bd[:, 16:32], in1=dtb_b[:], op=ALU.add)
        if _A():
            P.op("scalar", "activation", [tmp16[:]], [tmp16[:]], out=tmp16[:], in_=tmp16[:], func=AF.Exp)
        if _A():
            P.op("scalar", "activation", [tmp16[:]], [tmp16[:]], out=tmp16[:], in_=tmp16[:], func=AF.Ln, bias=1.0, scale=1.0)
        if _A():
            P.op("vector", "tensor_tensor", [g[:]], [tmp16[:], negA[:]], out=g[:], in0=tmp16[:], in1=negA[:], op=ALU.mult)
        if _A():
            P.op("tensor", "matmul", [sm_ps[:]], [g[:], triT], sm_ps[:, 0:16], triT, g[:], start=True, stop=True)
        if _A():
            P.op("tensor", "matmul", [sm_ps[:]], [g[:], sufT], sm_ps[:, 16:32], sufT, g[:], start=True, stop=True, same_ok=True)
        if _A():
            P.op("vector", "tensor_copy", [gh[:]], [sm_ps[:]], out=gh[:], in_=sm_ps[:, 0:16])
        if _A():
            P.op("scalar", "activation", [eg[:]], [sm_ps[:]], out=eg[:], in_=sm_ps[:, 0:16], func=AF.Exp)
        if _A():
            P.op("scalar", "activation", [egsuf[:]], [sm_ps[:]], out=egsuf[:], in_=sm_ps[:, 16:32], func=AF.Exp)
        if _A():
            P.op("vector", "tensor_tensor", [bq[:]], [beta[:], eg[:]], out=bq[:], in0=beta[:], in1=eg[:], op=ALU.mult)
        selL = C("selL_" + x)
        if _A():
            P.op("vector", "tensor_tensor", [Rl[:]], [gh[:], selL], out=Rl[:, 0:nseg, :], in0=gh[:].un(1).bc([128, nseg, 16]),
                 in1=selL.un(2).bc([128, nseg, 16]), op=ALU.mult)
        for s0 in range(0, nseg, 8):
            sn_ = min(8, nseg - s0)
            P.op("tensor", "matmul", [glB_ps[:]], [Rl[:], ones], glB_ps[:, 0:sn_, :], ones, Rl[:, s0:s0 + sn_, :], start=True, stop=True)
            P.op("scalar", "activation", [egl[:]], [glB_ps[:]], out=egl[:, s0:s0 + sn_, :], in_=glB_ps[:, 0:sn_, :], func=AF.Exp)
        if _A():
            P.op("gpsimd", "tensor_tensor", [diag[:]], [gh[:], ident], out=diag[:], in0=ident.un(1).bc([128, 16, 128]),
                 in1=gh[:].un(2).bc([128, 16, 128]), op=ALU.mult)
        for q8 in range(8):
            hs = slice(q8 * 2, q8 * 2 + 2)
            gps = ghB_half[q8 % 2]
            if _A():
                P.op("tensor", "matmul", [gps[:]], [diag[:], ones], gps[:], ones, diag[:, hs, :], start=True, stop=True)
            if _A():
                P.op("vector", "tensor_tensor", [E1[:]], [gps[:], gh[:]], out=E1[:, hs, :], in0=gps[:],
                     in1=gh[:, hs].un(2).bc([128, 2, 128]), op=ALU.subtract)
            if _A():
                P.op("scalar", "activation", [egB[:]], [gps[:]], out=egB[:, hs, :], in_=gps[:], func=AF.Exp)
        if _A():
            P.op("gpsimd", "tensor_tensor", [M1[:]], [E1[:], C("bigUs_" + x)], out=M1[:], in0=E1[:],
                 in1=C("bigUs_" + x).un(1).bc([128, 16, 128]), op=ALU.add)
        if _A():
            P.op("scalar", "activation", [Ds[:]], [M1[:]], out=Ds[:], in_=M1[:], func=AF.Exp, scale=-1.0)
        if _A():
            P.op("gpsimd", "tensor_tensor", [M1[:]], [E1[:], C("nbigL_" + x)], out=M1[:], in0=E1[:],
                 in1=C("nbigL_" + x).un(1).bc([128, 16, 128]), op=ALU.add)
        if _A():
            P.op("scalar", "activation", [DT[:]], [M1[:]], out=DT[:], in_=M1[:], func=AF.Exp)
        def head_gen(h, b):
            pz = slot_ps[b]
            tpc_ps, ss_ps, tk_ps, kk_ps, lp_ps, lu_ps, lx_ps = pz['tpc'], pz['ss'], pz['tk'], pz['kk'], pz['lp'], pz['lu'], pz['lx']
            wT_ps, vn_ps, qk_ps, o_ps = pz['wT'], pz['vn'], pz['qk'], pz['o']
            sn_ps = sn_bank[h % 2]
            if not samp:
                ev = ext[b][:, :, 0:131]
                cur = ev[:, :, 3:131]
                P.op("gpsimd", "tensor_copy", [ext[b][:]], [prev3[:]], out=ev[:, :, 0:3], in_=prev3[:, h, :, :])
                for p3 in range(3):
                    r0 = p3 * 2048 + h * 128
                    P.dma("sync", ev[:, p3, 3:131], s_qkv_fm[r0:r0 + 128, tok])
                P.op("gpsimd", "tensor_copy", [prev3[:]], [ext[b][:]], out=prev3[:, h, :, :], in_=ev[:, :, 128:131])
                taps = [ev[:, :, i:i + 128] for i in range(4)]
                accv = cacc[b][:]
            else:
                ev = ext[b][:].re("p a (s w) -> p a s w", w=11)
                for p3 in range(3):
                    r0 = p3 * 2048 + h * 128
                    P.dma("sync", ev[:, p3, :, 3:11], s_qkv_fm[r0:r0 + 128, tok].re("p (s w) -> p s w", w=8))
                    P.op("tensor", "transpose", [tpc_ps[:]], [sct[:], ident], out=tpc_ps[:], in_=sct[:, r0:r0 + 128],
                         identity=ident[0:48, 0:48])
                    P.op("vector", "tensor_copy", [ext[b][:]], [tpc_ps[:]], out=ev[:, p3, :, 0:3],
                         in_=tpc_ps[:].re("p (s r) -> p s r", r=3))
                taps = None
                accv = cacc[b][:].re("p a (s w) -> p a s w", w=8)
            for p3 in range(3):
                blk = p3 * 16 + h
                if not samp:
                    tp = [ev[:, p3, i:i + 128] for i in range(4)]
                    av = cacc[b][:, p3, :]
                else:
                    tp = [ev[:, p3, :, i:i + 8] for i in range(4)]
                    av = accv[:, p3, :, :]
                P.op("gpsimd", "tensor_scalar", [cacc[b][:]], [ext[b][:], cw[:]], out=av, in0=tp[0], scalar1=cw[:, blk, 0:1],
                     scalar2=None, op0=ALU.mult)
                for i in range(1, 4):
                    P.op("vector", "scalar_tensor_tensor", [cacc[b][:]], [ext[b][:], cw[:], cacc[b][:]], out=av, in0=tp[i],
                         scalar=cw[:, blk, i:i + 1], in1=av, op0=ALU.mult, op1=ALU.add)
            P.op("scalar", "activation", [act[b][:]], [cacc[b][:]], out=act[b][:], in_=cacc[b][:], func=AF.Silu)
            yield
            P.op("gpsimd", "tensor_tensor", [sq[b][:]], [act[b][:]], out=sq[b][:], in0=act[b][:, 0:2, :], in1=act[b][:, 0:2, :], op=ALU.mult)
            yield
            P.op("tensor", "matmul", [ss_ps[:]], [sq[b][:], ones], ss_ps[:], ones, sq[b][:].re("p a t -> p (a t)"), start=True, stop=True)
            yield
            P.op("scalar", "activation", [rn[b][:]], [ss_ps[:], eps_t[:]], out=rn[b][:].re("p a t -> p (a t)"), in_=ss_ps[:], func=AF.Sqrt,
                 bias=eps_t[:], scale=1.0)
            P.op("vector", "reciprocal", [rn[b][:]], [rn[b][:]], out=rn[b][:], in_=rn[b][:])
            yield
            P.op("vector", "scalar_tensor_tensor", [qn[b][:]], [act[b][:], rn[b][:]], out=qn[b][:], in0=act[b][:, 0, :], scalar=128.0 ** -0.5,
                 in1=rn[b][:, 0, :], op0=ALU.mult, op1=ALU.mult)
            P.op("vector", "tensor_tensor", [knf[b][:]], [act[b][:], rn[b][:]], out=knf[b][:], in0=act[b][:, 1, :], in1=rn[b][:, 1, :], op=ALU.mult)
            yield
            P.op("gpsimd", "tensor_copy", [kn[b][:]], [knf[b][:]], out=kn[b][:], in_=knf[b][:])
            yield
            P.op("gpsimd", "tensor_tensor", [qg[b][:]], [qn[b][:], egB[:]], out=qg[b][:], in0=qn[b][:], in1=egB[:, h, :], op=ALU.mult)
            yield
            P.op("tensor", "transpose", [tk_ps[:]], [knf[b][:], ident], out=tk_ps[:, 0, :], in_=knf[b][:], identity=ident)
            yield
            P.op("tensor", "transpose", [tk_ps[:]], [act[b][:], ident], out=tk_ps[:, 1, :], in_=act[b][:, 2, :], identity=ident, same_ok=True)
            yield
            P.op("vector", "tensor_scalar", [rhsw[b][:]], [tk_ps[:], bq[:]], out=rhsw[b][:], in0=tk_ps[:, 0, :], scalar1=bq[:, h:h + 1],
                 scalar2=None, op0=ALU.mult)
            P.op("vector", "tensor_scalar", [kend[b][:]], [tk_ps[:], egsuf[:]], out=kend[b][:], in0=tk_ps[:, 0, :], scalar1=egsuf[:, h:h + 1],
                 scalar2=None, op0=ALU.mult)
            P.op("vector", "tensor_scalar", [rhsv[b][:]], [tk_ps[:], beta[:]], out=rhsv[b][:], in0=tk_ps[:, 1, :], scalar1=beta[:, h:h + 1],
                 scalar2=None, op0=ALU.mult)
            P.op("tensor", "matmul", [kk_ps[:]], [kn[b][:]], kk_ps[:], kn[b][:], kn[b][:], start=True, stop=True)
            yield
            P.op("vector", "scalar_tensor_tensor", [mm[b][:]], [kk_ps[:], beta[:], Ds[:]], out=mm[b][:], in0=kk_ps[:], scalar=beta[:, h:h + 1],
                 in1=Ds[:, h, :], op0=ALU.mult, op1=ALU.mult)
            P.op("gpsimd", "tensor_tensor", [Cl[b][:]], [mm[b][:], C("lmask")], out=Cl[b][:, 0:nlev, :],
                 in0=mm[b][:].un(1).bc([128, nlev, 128]), in1=C("lmask").re("p (l f) -> p l f", l=NLEV)[:, 0:nlev, :], op=ALU.mult)
            P.op("vector", "tensor_copy", [U[b][:]], [ident], out=U[b][:], in_=ident)
            yield
            P.op("gpsimd", "tensor_copy", [X[b][:]], [ident], out=X[b][:], in_=ident)
            yield
            for l in range(nlev):
                P.op("tensor", "matmul", [lp_ps[:]], [Cl[b][:], U[b][:]], lp_ps[:], Cl[b][:, l, :], U[b][:], start=True, stop=True)
                P.op("scalar", "copy", [Psb[b][:]], [lp_ps[:]], out=Psb[b][:], in_=lp_ps[:])
                P.op("tensor", "matmul", [lu_ps[:]], [X[b][:], Psb[b][:]], lu_ps[:], X[b][:], Psb[b][:], start=True, stop=True)
                P.op("vector", "tensor_tensor", [U[b][:]], [U[b][:], lu_ps[:]], out=U[b][:], in0=U[b][:], in1=lu_ps[:], op=ALU.subtract)
                if l < nlev - 1:
                    P.op("tensor", "transpose", [lx_ps[:]], [U[b][:], ident], out=lx_ps[:], in_=U[b][:], identity=ident)
                    P.op("scalar", "copy", [X[b][:]], [lx_ps[:]], out=X[b][:], in_=lx_ps[:])
            P.op("scalar", "copy", [Ubf[b][:]], [U[b][:]], out=Ubf[b][:], in_=U[b][:])
            yield
            P.op("tensor", "matmul", [wT_ps[:]], [rhsw[b][:], Ubf[b][:]], wT_ps[:], rhsw[b][:], Ubf[b][:], start=True, stop=True)
            yield
            P.op("scalar", "mul", [nwT[b][:]], [wT_ps[:]], out=nwT[b][:], in_=wT_ps[:], mul=-1.0)
            yield
            P.op("tensor", "matmul", [vn_ps[:]], [Ubf[b][:], rhsv[b][:]], vn_ps[:], Ubf[b][:], rhsv[b][:], start=True, stop=False)
            yield
            if not samp:
                P.op("tensor", "matmul", [vn_ps[:]], [nwT[b][:], Sbf[h][:]], vn_ps[:], nwT[b][:], Sbf[h][:], start=False, stop=True, same_ok=True)
            else:
                P.dma("sync", S0h[b][:], sgdn_d[:, h].re("s k v -> k s v"))
                P.op("gpsimd", "tensor_copy", [S0hb[b][:]], [S0h[b][:]], out=S0hb[b][:], in_=S0h[b][:])
                segF3 = C("segF").re("p (s t) -> p s t", s=NSEQ)
                P.op("vector", "tensor_tensor", [nwTm[:]], [nwT[b][:], C("segF")], out=nwTm[:], in0=nwT[b][:].un(1).bc([128, NSEQ, 128]),
                     in1=segF3, op=ALU.mult)
                P.op("gpsimd", "tensor_tensor", [qgm[:]], [qg[b][:], C("segF")], out=qgm[:], in0=qg[b][:].un(1).bc([128, NSEQ, 128]),
                     in1=segF3, op=ALU.mult)
                P.op("gpsimd", "tensor_tensor", [kendm[:]], [kend[b][:], C("segP")], out=kendm[:], in0=kend[b][:].un(1).bc([128, NSEQ, 128]),
                     in1=C("segP").un(2).bc([128, NSEQ, 128]), op=ALU.mult)
                for s in range(NSEQ):
                    P.op("tensor", "matmul", [vn_ps[:]], [nwTm[:], S0hb[b][:]], vn_ps[:], nwTm[:, s, :], S0hb[b][:, s, :],
                         start=False, stop=(s == NSEQ - 1), same_ok=True)
            P.op("scalar", "copy", [vnew[b][:]], [vn_ps[:]], out=vnew[b][:], in_=vn_ps[:])
            yield
            P.op("tensor", "matmul", [qk_ps[:]], [kn[b][:], qn[b][:]], qk_ps[:], kn[b][:], qn[b][:], start=True, stop=True)
            yield
            P.op("vector", "tensor_tensor", [qkT[b][:]], [qk_ps[:], DT[:]], out=qkT[b][:], in0=qk_ps[:], in1=DT[:, h, :], op=ALU.mult)
            yield
            P.op("tensor", "matmul", [o_ps[:]], [qkT[b][:], vnew[b][:]], o_ps[:], qkT[b][:], vnew[b][:], start=True, stop=False)
            yield
            if not samp:
                P.op("tensor", "matmul", [o_ps[:]], [qg[b][:], Sbf[h][:]], o_ps[:], qg[b][:], Sbf[h][:], start=False, stop=True, same_ok=True)
            else:
                for s in range(NSEQ):
                    P.op("tensor", "matmul", [o_ps[:]], [qgm[:], S0hb[b][:]], o_ps[:], qgm[:, s, :], S0hb[b][:, s, :],
                         start=False, stop=(s == NSEQ - 1), same_ok=True)
            if not samp:
                P.op("tensor", "matmul", [sn_ps[:]], [kend[b][:], vnew[b][:]], sn_ps[:, 0, :], kend[b][:], vnew[b][:], start=True, stop=True)
                P.op("vector", "scalar_tensor_tensor", [S[h][:]], [S[h][:], egl[:], sn_ps[:]], out=S[h][:], in0=S[h][:],
                     scalar=egl[:, 0, h:h + 1], in1=sn_ps[:, 0, :], op0=ALU.mult, op1=ALU.add)
                P.op("scalar", "copy", [Sbf[h][:]], [S[h][:]], out=Sbf[h][:], in_=S[h][:])
            else:
                for s4 in range(4):
                    for j in range(4):
                        s = s4 * 4 + j
                        P.op("tensor", "matmul", [sn_ps[:]], [kendm[:], vnew[b][:]], sn_ps[:, j, :], kendm[:, s, :], vnew[b][:],
                             start=True, stop=True, same_ok=(j > 0))
                    for j in range(4):
                        s = s4 * 4 + j
                        P.op("vector", "scalar_tensor_tensor", [S1h[b][:]], [S0h[b][:], egl[:], sn_ps[:]], out=S1h[b][:, s, :],
                             in0=S0h[b][:, s, :], scalar=egl[:, s, h:h + 1], in1=sn_ps[:, j, :], op0=ALU.mult, op1=ALU.add)
                P.dma("sync", gdns_d[:, h].re("s k v -> k s v"), S1h[b][:], semkey=S1h[b])
            P.op("scalar", "activation", [junk[:], ssq[b][:]], [o_ps[:]], out=junk[:], in_=o_ps[:], func=AF.Square, accum_out=ssq[b][:])
            yield
            P.op("scalar", "activation", [rstd[b][:]], [ssq[b][:], eps_t[:]], out=rstd[b][:], in_=ssq[b][:], func=AF.Sqrt,
                 bias=eps_t[:], scale=1.0 / 128)
            P.op("vector", "reciprocal", [rstd[b][:]], [rstd[b][:]], out=rstd[b][:], in_=rstd[b][:])
            yield
            P.op("vector", "scalar_tensor_tensor", [on[b][:]], [o_ps[:], rstd[b][:], gnB[:]], out=on[b][:], in0=o_ps[:],
                 scalar=rstd[b][:, 0:1], in1=gnB[:], op0=ALU.mult, op1=ALU.mult)
            P.op("gpsimd", "tensor_tensor", [ogall[:]], [on[b][:], gsb[:]], out=ogall[:, h * 128:(h + 1) * 128], in0=on[b][:],
                 in1=gsb[:, h * 128:(h + 1) * 128], op=ALU.mult)

        GI = 1 if samp else NB
        for h0_ in range(0, G_H, GI):
            gens = [head_gen(h, h - h0_) for h in range(h0_, min(G_H, h0_ + GI))]
            while gens:
                alive = []
                for g_ in gens:
                    try:
                        next(g_)
                        alive.append(g_)
                    except StopIteration:
                        pass
                gens = alive
        if t == NTP - 1:
            for hh in range(G_H):
                P.dma("sync", gdnp_d[hh], S[hh][:], semkey=S[hh])
        for g8 in (range(2) if G_H > 0 else []):
            for j in range(8):
                hh = g8 * 8 + j
                P.op("tensor", "transpose", [tp_ps[:]], [ogall[:], ident_bf[:]], out=tp_ps[:, j, :], in_=ogall[:, hh * 128:(hh + 1) * 128],
                     identity=ident_bf[:], same_ok=(j > 0))
            P.op("vector", "tensor_copy", [oTs[g8][:]], [tp_ps[:]], out=oTs[g8][:], in_=tp_ps[:])
            P.dma("sync", s_obT[g8 * 1024:(g8 + 1) * 1024, tok].re("(j p) t -> p j t", p=128), oTs[g8][:], semkey=oTs[g8])
    P.pop()


def _phase_post(P, C, L):
    d = L
    P.scope_name = "p5_post"
    s_oaT, s_obT, s_ma, s_mb, x_d, y_d, wba_d, wbb_d, wo_d, fng_d, ident_bf = (d[k] for k in
        ["s_oaT", "s_obT", "s_ma", "s_mb", "x_d", "y_d", "wba_d", "wbb_d", "wo_d", "fng_d", "ident_bf"])
    P.push()
    gfin = P.sb("gfin", [128, D], F32)
    P.dma("sync", gfin[:], V(fng_d.t.ap()[0:1, :].partition_broadcast(128)[:, 0, :], [fng_d]))
    eps_t = P.sb("eps5", [128, 1], F32)
    P.op("vector", "memset", [eps_t[:]], [], eps_t[:], EPS)
    pA = [P.ps("pp%d" % i, [128, 512], F32) for i in range(4)]
    tp_ps = P.ps("tp5", [128, 4, 128], BF16)
    halves = [list(range(0, 9)), list(range(9, NT))]
    wviews = {id(w): w.t.ap().rearrange("(ko ki) c -> ki ko c", ki=128) for w in (wba_d, wbb_d, wo_d)}

    def wload(dst, w, c0, n):
        for q4 in range(4):
            P.dma("gpsimd", dst[:, q4 * 4:(q4 + 1) * 4, 0:n], V(wviews[id(w)][:, q4 * 4:(q4 + 1) * 4, c0:c0 + n], [w]))
    for hi, tiles in enumerate(halves):
        nt = len(tiles)
        sfx = "_h%d" % hi
        t0 = tiles[0] * 128
        ntok = nt * 128
        P.push()
        mT = P.sb("mT" + sfx, [128, KT, 1152], BF16)
        P.push()
        oaT = P.sb("oaT" + sfx, [128, KT, 1152], BF16)
        obT = P.sb("obT" + sfx, [128, KT, 1152], BF16)
        for q4 in range(4):
            ks = slice(q4 * 4, q4 * 4 + 4)
            P.dma("sync", oaT[:, ks, 0:ntok], s_oaT[q4 * 512:(q4 + 1) * 512, t0:t0 + ntok].re("(k p) t -> p k t", p=128))
            P.dma("sync", obT[:, ks, 0:ntok], s_obT[q4 * 512:(q4 + 1) * 512, t0:t0 + ntok].re("(k p) t -> p k t", p=128))
        wa = P.sb("wa" + sfx, [128, KT, 512], BF16)
        wb = P.sb("wb" + sfx, [128, KT, 512], BF16)
        mat = [P.sb("mat%d" % i + sfx, [128, 512], F32) for i in range(2)]
        mbt = [P.sb("mbt%d" % i + sfx, [128, 512], F32) for i in range(2)]
        t1 = [P.sb("t1_%d" % i + sfx, [128, 512], F32) for i in range(2)]
        t2 = [P.sb("t2_%d" % i + sfx, [128, 512], F32) for i in range(2)]
        mg = [P.sb("mg%d" % i + sfx, [128, 512], BF16) for i in range(2)]
        it = 0
        for j in range(4):
            wload(wa, wba_d, j * 512, 512)
            wload(wb, wbb_d, j * 512, 512)
            for ti, t in enumerate(tiles):
                b = it % 2
                it += 1
                ya, yb = pA[2 * b], pA[2 * b + 1]
                lt = slice(ti * 128, (ti + 1) * 128)
                for k in range(KT):
                    P.op("tensor", "matmul", [ya[:]], [oaT[:], wa[:]], ya[:], oaT[:, k, lt], wa[:, k, :], start=(k == 0), stop=(k == KT - 1), same_ok=True)
                for k in range(KT):
                    P.op("tensor", "matmul", [yb[:]], [obT[:], wb[:]], yb[:], obT[:, k, lt], wb[:, k, :], start=(k == 0), stop=(k == KT - 1), same_ok=True)
                P.dma("sync", mat[b][:], s_ma[t * 128:(t + 1) * 128, j * 512:(j + 1) * 512])
                P.dma("sync", mbt[b][:], s_mb[t * 128:(t + 1) * 128, j * 512:(j + 1) * 512])
                P.op("scalar", "activation", [mat[b][:]], [mat[b][:]], out=mat[b][:], in_=mat[b][:], func=AF.Sigmoid)
                P.op("scalar", "activation", [mbt[b][:]], [mbt[b][:]], out=mbt[b][:], in_=mbt[b][:], func=AF.Sigmoid)
                P.op("vector", "tensor_tensor", [t1[b][:]], [ya[:], mat[b][:]], out=t1[b][:], in0=ya[:], in1=mat[b][:], op=ALU.mult)
                P.op("vector", "tensor_tensor", [t2[b][:]], [yb[:], mbt[b][:]], out=t2[b][:], in0=yb[:], in1=mbt[b][:], op=ALU.mult)
                P.op("gpsimd", "tensor_tensor", [mg[b][:]], [t1[b][:], t2[b][:]], out=mg[b][:], in0=t1[b][:], in1=t2[b][:], op=ALU.add)
                for c in range(4):
                    P.op("tensor", "transpose", [tp_ps[:]], [mg[b][:], ident_bf[:]], out=tp_ps[:, c, :], in_=mg[b][:, c * 128:(c + 1) * 128],
                         identity=ident_bf[:], same_ok=(c > 0))
                P.op("scalar", "copy", [mT[:]], [tp_ps[:]], out=mT[:, 4 * j:4 * j + 4, lt], in_=tp_ps[:])
        P.pop()
        P.push()
        wo = P.sb("wo" + sfx, [128, KT, D], BF16)
        for j in range(4):
            for q4 in range(4):
                P.dma("gpsimd", wo[:, q4 * 4:(q4 + 1) * 4, j * 512:(j + 1) * 512],
                      V(wviews[id(wo_d)][:, q4 * 4:(q4 + 1) * 4, j * 512:(j + 1) * 512], [wo_d]))
        xt = [P.sb("x5_%d" % i + sfx, [128, D], F32) for i in range(2)]
        res = [P.sb("res%d" % i + sfx, [128, D], F32) for i in range(2)]
        yo = [P.sb("yo%d" % i + sfx, [128, D], F32) for i in range(2)]
        junk = P.sb("junk5" + sfx, [128, D], BF16)
        ssq = [P.sb("ssq5_%d" % i + sfx, [128, 1], F32) for i in range(2)]
        for ti, t in enumerate(tiles):
            b = ti % 2
            lt = slice(ti * 128, (ti + 1) * 128)
            P.dma("sync", xt[b][:], x_d[t * 128:(t + 1) * 128, :])
            for j in range(4):
                ps = pA[j]
                for k in range(KT):
                    P.op("tensor", "matmul", [ps[:]], [mT[:], wo[:]], ps[:], mT[:, k, lt], wo[:, k, j * 512:(j + 1) * 512],
                         start=(k == 0), stop=(k == KT - 1), same_ok=True)
                P.op("vector", "tensor_tensor", [res[b][:]], [ps[:], xt[b][:]], out=res[b][:, j * 512:(j + 1) * 512], in0=ps[:],
                     in1=xt[b][:, j * 512:(j + 1) * 512], op=ALU.add)
            P.op("scalar", "activation", [junk[:], ssq[b][:]], [res[b][:]], out=junk[:], in_=res[b][:], func=AF.Square, accum_out=ssq[b][:])
            P.op("scalar", "activation", [ssq[b][:]], [ssq[b][:], eps_t[:]], out=ssq[b][:], in_=ssq[b][:], func=AF.Sqrt, bias=eps_t[:], scale=1.0 / D)
            P.op("vector", "reciprocal", [ssq[b][:]], [ssq[b][:]], out=ssq[b][:], in_=ssq[b][:])
            P.op("vector", "scalar_tensor_tensor", [yo[b][:]], [res[b][:], ssq[b][:], gfin[:]], out=yo[b][:], in0=res[b][:], scalar=ssq[b][:, 0:1],
                 in1=gfin[:], op0=ALU.mult, op1=ALU.mult)
            P.dma("sync", y_d[t * 128:(t + 1) * 128, :], yo[b][:], semkey=yo[b])
        P.pop()
        P.pop()
    P.pop()

def _core_inputs(c, inp):
    b = c // 2
    xs = inp["x_sample"][16 * c:16 * c + 16].reshape(128, D)
    x = np.concatenate([inp["x_prompt"][b], xs], axis=0)
    w2ext = np.concatenate([inp["w_alpha2"][0], inp["b_alpha"][0][None, :]], axis=0)
    cwT = np.ascontiguousarray(inp["conv_w"][0].T.reshape(48, 128, 4).transpose(1, 0, 2))
    return {
        "x": np.ascontiguousarray(x, dtype=np.float32),
        "w_in": np.ascontiguousarray(inp["w_in"][0]),
        "w2ext": np.ascontiguousarray(w2ext),
        "cwT": cwT,
        "a_log": np.ascontiguousarray(inp["a_log"][0][None, :]),
        "dt_bias": np.ascontiguousarray(inp["dt_bias"][0][None, :]),
        "ln_in_g": np.ascontiguousarray(inp["ln_in_g"][0][None, :]),
        "final_norm_g": np.ascontiguousarray(inp["final_norm_g"][None, :]),
        "gla_norm_g": np.ascontiguousarray(inp["gla_norm_g"][0][None, :]),
        "gdn_norm_g": np.ascontiguousarray(inp["gdn_norm_g"][0][None, :]),
        "w_br_a": np.ascontiguousarray(inp["w_br_a"][0]),
        "w_br_b": np.ascontiguousarray(inp["w_br_b"][0]),
        "w_out": np.ascontiguousarray(inp["w_out"][0]),
        "state_gla": np.ascontiguousarray(inp["state_gla"][0, 16 * c:16 * c + 16]),
        "state_gdn": np.ascontiguousarray(inp["state_gdn"][0, 16 * c:16 * c + 16]),
        "state_conv": np.ascontiguousarray(inp["state_conv"][0, 16 * c:16 * c + 16]),
        "consts": _CONST,
    }


def kernel(**inputs):
    inp = {k: np.asarray(v) for k, v in inputs.items()}
    nc = build_nc()
    in_maps = [_core_inputs(c, inp) for c in range(8)]
    res = run_bass_kernel_spmd(nc, in_maps, core_ids=list(range(8)))
    r = res.results
    y_prompt = np.stack([r[2 * b]["y"][:NTP * 128] for b in range(4)], axis=0)
    y_sample = np.concatenate([r[c]["y"][NTP * 128:].reshape(16, 8, D) for c in range(8)], axis=0)
    gla_p = np.stack([r[2 * b]["gla_p"] for b in range(4)], axis=0)[None]
    gdn_p = np.stack([r[2 * b]["gdn_p"] for b in range(4)], axis=0)[None]
    conv_p = np.stack([r[2 * b]["conv_p"] for b in range(4)], axis=0)[None]
    gla_s = np.concatenate([r[c]["gla_s"] for c in range(8)], axis=0)[None]
    gdn_s = np.concatenate([r[c]["gdn_s"] for c in range(8)], axis=0)[None]
    conv_s = np.concatenate([r[c]["conv_s"] for c in range(8)], axis=0)[None]
    return (y_prompt.astype(np.float32), y_sample.astype(np.float32), gla_p.astype(np.float32),
            gdn_p.astype(np.float32), conv_p.astype(np.float32), gla_s.astype(np.float32),
            gdn_s.astype(np.float32), conv_s.astype(np.float32))
```

```python
import contextlib
import os
import numpy as np
import concourse.bass as bass
import concourse.mybir as mybir
from concourse.bass_utils import run_bass_kernel_spmd

F32 = mybir.dt.float32
BF16 = mybir.dt.bfloat16
AF = mybir.ActivationFunctionType
ALU = mybir.AluOpType

D = 2048
KT = 16
NTP = 16
NT = 17
TOK = NT * 128
NSEQ = 16
DIN = 18480
C_QA, C_KA, C_VA, C_LR, C_GA, C_QKV, C_BD, C_GB, C_MA, C_MB = 0, 1024, 2048, 4096, 4112, 6160, 12304, 12336, 14384, 16432
EPS = 1e-6
BIG = 3.0e4
NLEV = 7
ATTACH_WAIT = os.environ.get("ATTACH_WAIT", "1") == "1"


class Bank:
    def __init__(self):
        self.reads = {}


class Buf:
    def __init__(self, t, name, bank=None):
        self.t = t
        self.name = name
        self.last_write = None
        self.reads = []
        self.bank = bank

    def __getitem__(self, idx):
        return V(self.t[idx], [self])

    def ap(self):
        return V(self.t.ap() if hasattr(self.t, "ap") and callable(getattr(self.t, "ap")) else self.t, [self])


class SubBuf:
    def __init__(self, parent, ap, name):
        self.parent = parent
        self.t = ap
        self.name = name

    def __getitem__(self, idx):
        return V(self.t[idx], [self.parent])


class V:
    def __init__(self, ap, keys):
        self.ap = ap
        self.keys = keys

    def __getitem__(self, idx):
        return V(self.ap[idx], self.keys)

    def re(self, pattern, **kw):
        return V(self.ap.rearrange(pattern, **kw), self.keys)

    def bc(self, shape):
        return V(self.ap.to_broadcast(list(shape)), self.keys)

    def un(self, axis):
        return V(self.ap.unsqueeze(axis), self.keys)

    def bitcast(self, dt):
        return V(self.ap.bitcast(dt), self.keys)


def _ap(x):
    return x.ap if isinstance(x, V) else x


class Prog:
    def __init__(self, nc):
        self.nc = nc
        self.stack = contextlib.ExitStack()
        self.engs = ["tensor", "vector", "scalar", "gpsimd", "sync"]
        self.ops = {e: [] for e in self.engs}
        self.sem = {}
        self.cnt = {e: 0 for e in self.engs}
        self.seen = {e: {} for e in self.engs}
        for e in self.engs:
            self.sem[e] = self.stack.enter_context(nc.semaphore("s_" + e))
        self.dma_sems = {}
        self.n_wait = 0
        self.n_ops = 0
        self.scopes = [self.stack]
        self.all_ops = {e: [] for e in self.engs}

    def push(self):
        self.scopes.append(contextlib.ExitStack())

    def pop(self):
        self.flush()
        self.scopes.pop().close()

    def sb(self, name, shape, dt):
        t = self.scopes[-1].enter_context(self.nc.sbuf_tensor(name, list(shape), dt))
        return Buf(t, name)

    def ps(self, name, shape, dt=F32):
        t = self.scopes[-1].enter_context(self.nc.psum_tensor(name, list(shape), dt))
        return Buf(t, name, bank=Bank())

    def dram(self, name, shape, dt, kind="Internal"):
        t = self.nc.dram_tensor(name, list(shape), dt, kind=kind)
        return Buf(t, name)

    def _collect(self, eng, ins, outs, same_ok):
        need = {}

        def add(sig):
            if sig is None:
                return
            sem, val, seng = sig
            if seng == eng and same_ok:
                return
            k = id(sem)
            if k not in need or need[k][1] < val:
                need[k] = (sem, val)
        for v in ins:
            if isinstance(v, V):
                for key in v.keys:
                    add(key.last_write)
                    if key.bank is not None:
                        for oe, sig in key.bank.reads.items():
                            if oe != eng:
                                add(sig)
        for v in outs:
            if isinstance(v, V):
                for key in v.keys:
                    add(key.last_write)
                    for r in key.reads:
                        add(r)
        waits = []
        for k, (sem, val) in need.items():
            if self.seen[eng].get(k, 0) >= val:
                continue
            self.seen[eng][k] = val
            waits.append((sem, val))
        return waits

    def _record(self, sig, ins, outs):
        for v in ins:
            if isinstance(v, V):
                for key in v.keys:
                    if key.bank is not None and sig[2] != "tensor":
                        key.bank.reads[sig[2]] = sig
                    key.reads.append(sig)
                    if len(key.reads) > 64:
                        best = {}
                        for s in key.reads:
                            if id(s[0]) not in best or best[id(s[0])][1] < s[1]:
                                best[id(s[0])] = s
                        key.reads = list(best.values())
        for v in outs:
            if isinstance(v, V):
                for key in v.keys:
                    key.last_write = sig
                    key.reads = []

    def op(self, eng, method, outs, ins, *args, same_ok=False, **kw):
        waits = self._collect(eng, ins, outs, same_ok)
        self.cnt[eng] += 1
        sig = (self.sem[eng], self.cnt[eng], eng)
        a2 = [_ap(a) for a in args]
        k2 = {k: _ap(v) for k, v in kw.items()}
        self.ops[eng].append((waits, method, a2, k2, (self.sem[eng], 1)))
        self.n_wait += len(waits)
        self.n_ops += 1
        self._record(sig, ins, outs)

    def dma(self, q, out, in_, semkey=None, **kw):
        waits = self._collect(q, [in_], [out], False)
        if semkey is None:
            semkey = out.keys[0]
        k = id(semkey)
        if k not in self.dma_sems:
            s = self.stack.enter_context(self.nc.semaphore("d_" + semkey.name))
            self.dma_sems[k] = [s, 0]
        ent = self.dma_sems[k]
        ent[1] += 16
        sig = (ent[0], ent[1], "dma")
        self.ops[q].append((waits, "dma_start", [], dict(out=_ap(out), in_=_ap(in_), **kw), (ent[0], 16)))
        self.n_wait += len(waits)
        self.n_ops += 1
        self._record(sig, [in_], [out])

    def final_wait(self, eng, bufs):
        need = {}
        for b in bufs:
            for sig in [b.last_write] + b.reads:
                if sig is None:
                    continue
                sem, val, _ = sig
                k = id(sem)
                if k not in need or need[k][1] < val:
                    need[k] = (sem, val)
        self.ops[eng].append((list(need.values()), None, None, None, None))

    def audit(self):
        final = {id(self.sem[e]): (self.cnt[e], "eng:" + e) for e in self.engs}
        for k, (sm, c) in self.dma_sems.items():
            final[id(sm)] = (c, "dma:" + sm.name)
        bad = 0
        mx = 0
        for e in self.engs:
            for (waits, method, a, k, inc) in self.all_ops[e]:
                for (sm, val) in waits:
                    mx = max(mx, val)
                    if id(sm) not in final or final[id(sm)][0] < val:
                        bad += 1
                        if bad < 10:
                            print("AUDIT: unsatisfiable wait on", e, getattr(sm, "name", sm), val, final.get(id(sm)))
        print("AUDIT: bad waits", bad, "max wait value", mx, "engine counts", dict(self.cnt),
              "max dma sem", max([c for (_, c) in self.dma_sems.values()] + [0]), "n dma sems", len(self.dma_sems))
        return bad

    def flush(self):
        for e in self.engs:
            self.all_ops[e].extend(self.ops[e])
        if any(self.ops[e] for e in self.engs):
            self.emit()
        self.ops = {e: [] for e in self.engs}

    def emit(self):
        nc = self.nc
        self.n_blocks = getattr(self, "n_blocks", 0) + 1
        with nc.named_scope("blk%02d_%s" % (self.n_blocks, getattr(self, "scope_name", "x"))), nc.Block() as block:
            def mk(ename):
                def body(e):
                    for (waits, method, a, k, inc) in self.ops[ename]:
                        attach = None
                        if ATTACH_WAIT and method is not None and method != "dma_start" and waits:
                            attach = waits[-1]
                            waits = waits[:-1]
                        for (sem, val) in waits:
                            e.wait_ge(sem, val)
                        if method is None:
                            continue
                        ins = getattr(e, method)(*a, **k)
                        if attach is not None:
                            ins._wait_ge(attach[0], attach[1])
                        ins.then_inc(inc[0], inc[1])
                return body
            block.tensor(mk("tensor"))
            block.vector(mk("vector"))
            block.scalar(mk("scalar"))
            block.gpsimd(mk("gpsimd"))
            block.sync(mk("sync"))

    def close(self):
        self.stack.close()


def _const_table():
    idx = np.arange(128)
    P_, F_ = np.meshgrid(idx, idx, indexing="ij")
    segp = np.zeros_like(P_)
    segs_p, segs_f = P_ // 8, F_ // 8
    c = {}
    c["ident"] = (P_ == F_).astype(np.float32)
    c["ones"] = np.ones((128, 128), np.float32)
    c["notI"] = (P_ != F_).astype(np.float32)
    for x, same in (("p", np.ones_like(P_, bool)), ("s", segs_p == segs_f)):
        c["triT_" + x] = ((P_ <= F_) & same).astype(np.float32)
        c["sufT_" + x] = ((P_ > F_) & same).astype(np.float32)
        c["bigU_" + x] = np.where((F_ <= P_) & same, 0.0, BIG).astype(np.float32)
        c["nbigL_" + x] = np.where((F_ >= P_) & same, 0.0, -BIG).astype(np.float32)
        c["bigUs_" + x] = np.where((F_ < P_) & same, 0.0, BIG).astype(np.float32)
    lm = np.zeros((128, NLEV, 128), np.float32)
    for l in range(NLEV):
        s = 1 << l
        lm[:, l, :] = ((P_ // (2 * s) == F_ // (2 * s)) & (P_ % (2 * s) >= s) & (F_ % (2 * s) < s))
    c["lmask"] = lm.reshape(128, NLEV * 128)
    segF = np.zeros((128, NSEQ, 128), np.float32)
    for s in range(NSEQ):
        segF[:, s, 8 * s:8 * s + 8] = 1.0
    c["segF"] = segF.reshape(128, NSEQ * 128)
    c["segP"] = (idx[:, None] // 8 == np.arange(NSEQ)[None, :]).astype(np.float32)
    c["selL_s"] = (idx[:, None] == (8 * np.arange(NSEQ)[None, :] + 7)).astype(np.float32)
    c["selL_p"] = (idx[:, None] == 127).astype(np.float32)
    offs = {}
    cols = []
    o = 0
    for k, v in c.items():
        offs[k] = (o, v.shape[1])
        cols.append(v)
        o += v.shape[1]
    return np.ascontiguousarray(np.concatenate(cols, axis=1)), offs


_CONST, _COFF = _const_table()
NCONST = _CONST.shape[1]


def build_nc(upto=99, debug=False):
    nc = bass.Bass("TRN2", target_bir_lowering=False)
    P = Prog(nc)

    def din(name, shape, dt=F32):
        return Buf(nc.dram_tensor(name, list(shape), dt, kind="ExternalInput"), name)

    def dout(name, shape, dt=F32):
        return Buf(nc.dram_tensor(name, list(shape), dt, kind="ExternalOutput"), name)

    def dscr(name, shape, dt=F32):
        if debug:
            return dout(name, shape, dt)
        return P.dram(name, shape, dt)

    x_d = din("x", [TOK, D])
    w_in_d = din("w_in", [D, DIN])
    w2e_d = din("w2ext", [17, 1024])
    cwT_d = din("cwT", [128, 48, 4])
    alog_d = din("a_log", [1, 16])
    dtb_d = din("dt_bias", [1, 16])
    lng_d = din("ln_in_g", [1, D])
    fng_d = din("final_norm_g", [1, D])
    gna_d = din("gla_norm_g", [1, 512])
    gnb_d = din("gdn_norm_g", [1, 128])
    wba_d = din("w_br_a", [D, D])
    wbb_d = din("w_br_b", [D, D])
    wo_d = din("w_out", [D, D])
    sgla_d = din("state_gla", [NSEQ, 4, 256, 512])
    sgdn_d = din("state_gdn", [NSEQ, 16, 128, 128])
    sconv_d = din("state_conv", [NSEQ, 3, 6144])
    const_d = din("consts", [128, NCONST])

    y_d = dout("y", [TOK, D])
    glap_d = dout("gla_p", [4, 256, 512])
    gdnp_d = dout("gdn_p", [16, 128, 128])
    convp_d = dout("conv_p", [3, 6144])
    glas_d = dout("gla_s", [NSEQ, 4, 256, 512])
    gdns_d = dout("gdn_s", [NSEQ, 16, 128, 128])
    convs_d = dout("conv_s", [NSEQ, 3, 6144])
    outs_all = [y_d, glap_d, gdnp_d, convp_d, glas_d, gdns_d, convs_d]

    s_ka_tm = dscr("s_ka_tm", [TOK, 1024])
    s_va = dscr("s_va", [TOK, 2048], BF16)
    s_ga = dscr("s_ga", [TOK, 2048])
    s_bd = dscr("s_bd", [TOK, 32])
    s_gb = dscr("s_gb", [TOK, 2048])
    s_ma = dscr("s_ma", [TOK, 2048])
    s_mb = dscr("s_mb", [TOK, 2048])
    s_qkv_tm = dscr("s_qkv_tm", [256, 6144])
    s_qa_fm = dscr("s_qa_fm", [1024, TOK])
    s_ka_fm = dscr("s_ka_fm", [1024, TOK])
    s_lr_fm = dscr("s_lr_fm", [16, TOK])
    s_qkv_fm = dscr("s_qkv_fm", [6144, TOK])
    s_oaT = dscr("s_oaT", [2048, TOK], BF16)
    s_obT = dscr("s_obT", [2048, TOK], BF16)
    dbg = [s_ka_tm, s_va, s_ga, s_bd, s_gb, s_ma, s_mb, s_qkv_tm, s_qa_fm, s_ka_fm, s_lr_fm, s_qkv_fm, s_oaT, s_obT]

    cst = P.sb("cst", [128, NCONST], F32)
    P.dma("sync", cst[:], const_d[:])

    def C(name):
        o, n = _COFF[name]
        return cst[:, o:o + n]

    ident_bf = P.sb("ident_bf", [128, 128], BF16)
    P.op("vector", "tensor_copy", [ident_bf[:]], [cst[:]], out=ident_bf[:], in_=C("ident"))

    P.scope_name = "p12_gemm"
    P.push()
    gB = P.sb("gB", [128, D], F32)
    def bload(dst, src, n):
        P.dma("sync", dst, V(src.t.ap()[0:1, 0:n].partition_broadcast(128)[:, 0, :], [src]))

    bload(gB[:], lng_d, D)

    psA = [P.ps("psA%d" % i, [128, 512], F32) for i in range(4)]
    psT = [P.ps("psT%d" % i, [128, 1024], BF16) for i in range(2)]
    psM = [P.ps("psM%d" % i, [128, 512], F32) for i in range(2)]

    hT = P.sb("hT", [128, KT, TOK], BF16)
    xt = [P.sb("xt%d" % i, [128, D], F32) for i in range(2)]
    hb = [P.sb("hb%d" % i, [128, D], BF16) for i in range(2)]
    junk = P.sb("junk", [128, D], BF16)
    ss = [P.sb("ss%d" % i, [128, 1], F32) for i in range(2)]
    rs = [P.sb("rs%d" % i, [128, 1], F32) for i in range(2)]
    eps_t = P.sb("eps_t", [128, 1], F32)
    P.op("vector", "memset", [eps_t[:]], [], eps_t[:], EPS)
    for t in range(NT):
        b = t % 2
        P.dma("sync", xt[b][:], x_d[t * 128:(t + 1) * 128, :])
        P.op("scalar", "activation", [junk[:], ss[b][:]], [xt[b][:]], out=junk[:], in_=xt[b][:], func=AF.Square,
             accum_out=ss[b][:])
        P.op("scalar", "activation", [rs[b][:]], [ss[b][:], eps_t[:]], out=rs[b][:], in_=ss[b][:], func=AF.Sqrt,
             bias=eps_t[:], scale=1.0 / D)
        P.op("vector", "reciprocal", [rs[b][:]], [rs[b][:]], out=rs[b][:], in_=rs[b][:])
        P.op("vector", "scalar_tensor_tensor", [hb[b][:]], [xt[b][:], rs[b][:], gB[:]], out=hb[b][:], in0=xt[b][:],
             scalar=rs[b][:, 0:1], in1=gB[:], op0=ALU.mult, op1=ALU.mult)
        for g in range(2):
            pt = psT[g]
            for j in range(8):
                k = g * 8 + j
                P.op("tensor", "transpose", [pt[:]], [hb[b][:], ident_bf[:]], out=pt[:, j * 128:(j + 1) * 128],
                     in_=hb[b][:, k * 128:(k + 1) * 128], identity=ident_bf[:], same_ok=True)
            dst = hT[:, g * 8:(g + 1) * 8, t * 128:(t + 1) * 128]
            src = pt[:].re("p (a b) -> p a b", a=8)
            if g == 0:
                P.op("vector", "tensor_copy", [dst], [pt[:]], out=dst, in_=src)
            else:
                P.op("scalar", "copy", [dst], [pt[:]], out=dst, in_=src)

    wblk = [P.sb("wblk%d" % i, [128, KT, 512], BF16) for i in range(2)]
    stg = [P.sb("stg%d" % i, [128, 512], F32) for i in range(4)]
    stgb = [P.sb("stgb%d" % i, [128, 512], BF16) for i in range(2)]
    all_tiles = list(range(NT))
    groups = [
        (C_QA, 1024, s_qa_fm, None, None, False),
        (C_KA, 1024, s_ka_fm, s_ka_tm, all_tiles, False),
        (C_VA, 2048, None, s_va, all_tiles, True),
        (C_LR, 16, s_lr_fm, None, None, False),
        (C_GA, 2048, None, s_ga, all_tiles, False),
        (C_QKV, 6144, s_qkv_fm, s_qkv_tm, [NTP - 1, NT - 1], False),
        (C_BD, 32, None, s_bd, all_tiles, False),
        (C_GB, 2048, None, s_gb, all_tiles, False),
        (C_MA, 2048, None, s_ma, all_tiles, False),
        (C_MB, 2048, None, s_mb, all_tiles, False),
    ]
    blocks = []
    for (c0, n, fm, tm, tiles, tbf) in groups:
        for o in range(0, n, 512):
            blocks.append((c0 + o, min(512, n - o), o, fm, tm, tiles, tbf))
    w_view = w_in_d.t.ap().rearrange("(ko ki) c -> ki ko c", ki=128)
    cnt = {"ps": 0, "stg": 0, "stgb": 0, "ev": 0}

    def evac(dst, src):
        cnt["ev"] += 1
        if cnt["ev"] % 2:
            P.op("vector", "tensor_copy", [dst], [src], out=dst, in_=src)
        else:
            P.op("scalar", "copy", [dst], [src], out=dst, in_=src)

    tok_groups = [(0, 512), (512, 512), (1024, 512), (1536, 512), (2048, 128)]
    if upto >= 2:
        for bi, (cg, n, off, fm, tm, tiles, tbf) in enumerate(blocks):
            wb = wblk[bi % 2]
            for q4 in range(4):
                P.dma("gpsimd", wb[:, q4 * 4:(q4 + 1) * 4, 0:n], V(w_view[:, q4 * 4:(q4 + 1) * 4, cg:cg + n], [w_in_d]))
            if fm is not None:
                for sbo in range(0, n, 128):
                    m = min(128, n - sbo)
                    for (t0, tn) in tok_groups:
                        ps = psA[cnt["ps"] % 4]
                        cnt["ps"] += 1
                        for k in range(KT):
                            P.op("tensor", "matmul", [ps[:]], [wb[:], hT[:]], ps[0:m, 0:tn], wb[:, k, sbo:sbo + m],
                                 hT[:, k, t0:t0 + tn], start=(k == 0), stop=(k == KT - 1), same_ok=True)
                        st = stg[cnt["stg"] % 4]
                        cnt["stg"] += 1
                        evac(st[0:m, 0:tn], ps[0:m, 0:tn])
                        P.dma("sync", fm[off + sbo:off + sbo + m, t0:t0 + tn], st[0:m, 0:tn], semkey=st)
            if tm is not None:
                for t in tiles:
                    ps = psA[cnt["ps"] % 4]
                    cnt["ps"] += 1
                    for k in range(KT):
                        P.op("tensor", "matmul", [ps[:]], [wb[:], hT[:]], ps[:, 0:n], hT[:, k, t * 128:(t + 1) * 128],
                             wb[:, k, 0:n], start=(k == 0), stop=(k == KT - 1), same_ok=True)
                    if tbf:
                        st = stgb[cnt["stgb"] % 2]
                        cnt["stgb"] += 1
                    else:
                        st = stg[cnt["stg"] % 4]
                        cnt["stg"] += 1
                    evac(st[:, 0:n], ps[:, 0:n])
                    if tm is s_qkv_tm:
                        r0 = 0 if t == NTP - 1 else 128
                    else:
                        r0 = t * 128
                    P.dma("sync", tm[r0:r0 + 128, off:off + n], st[:, 0:n], semkey=st)

    P.pop()

    if upto >= 3:
        _phase_gla(P, C, locals())
    if upto >= 4:
        _phase_gdn(P, C, locals())
    if upto >= 5:
        _phase_post(P, C, locals())
    final = list(outs_all) + (dbg if debug else [])
    P.final_wait("sync", final + dbg)
    P.flush()
    P.close()
    print("ops", P.n_ops, "waits", P.n_wait)
    P.audit()
    return nc


def _phase_gla(P, C, L):
    d = L
    P.scope_name = "p3_gla"
    s_qa_fm, s_ka_fm, s_ka_tm, s_va, s_ga, s_lr_fm, s_oaT = (d[k] for k in
        ["s_qa_fm", "s_ka_fm", "s_ka_tm", "s_va", "s_ga", "s_lr_fm", "s_oaT"])
    w2e_d, gna_d, sgla_d, glap_d, glas_d, ident_bf = (d[k] for k in
        ["w2e_d", "gna_d", "sgla_d", "glap_d", "glas_d", "ident_bf"])
    P.push()
    w2e = P.sb("w2e", [17, 1024], F32)
    P.dma("sync", w2e[:], w2e_d[:])
    gnA = P.sb("gnA", [128, 512], F32)
    P.dma("sync", gnA[:], V(gna_d.t.ap()[0:1, :].partition_broadcast(128)[:, 0, :], [gna_d]))
    eps_t = P.sb("eps3", [128, 1], F32)
    P.op("vector", "memset", [eps_t[:]], [], eps_t[:], EPS)
    S = [P.sb("S%d" % h, [128, 2, 512], F32) for h in range(4)]
    Sbf = [P.sb("Sbf%d" % h, [128, 2, 512], BF16) for h in range(4)]
    for h in range(4):
        P.op("gpsimd", "memset", [S[h][:]], [], S[h][:], 0.0)
        P.op("gpsimd", "memset", [Sbf[h][:]], [], Sbf[h][:], 0.0)
    NB = 2
    lrT = [P.sb("lrT%d" % i, [17, 128], F32) for i in range(NB)]
    for i in range(NB):
        P.op("vector", "memset", [lrT[i][:]], [], lrT[i][:], 1.0)

    def mk(name, shape, dt):
        return [P.sb("%s%d" % (name, i), shape, dt) for i in range(NB)]
    qT = mk("qT", [128, 2, 128], F32); kT = mk("kT", [128, 2, 128], F32)
    ktok = mk("ktok", [128, 256], F32); vv = mk("vv", [128, 512], BF16); gate = mk("gate", [128, 512], F32)
    ee = mk("ee", [128, 256], F32); sp = mk("sp", [128, 256], F32)
    ebT = mk("ebT", [128, 2, 128], F32); enbT = mk("enbT", [128, 2, 128], F32); esuf = mk("esuf", [128, 256], F32)
    qd = mk("qd", [128, 2, 128], BF16); kd = mk("kd", [128, 2, 128], BF16); kend = mk("kend", [128, 256], BF16)
    attT = mk("attT", [128, 128], BF16)
    on = mk("on", [128, 512], F32); gs = mk("gs", [128, 512], F32); og = mk("og", [128, 512], BF16)
    oT = mk("oT", [128, 4, 128], BF16)
    junk = P.sb("junk3", [128, 512], BF16)
    ssq = mk("ssq", [128, 1], F32); rstd = mk("rstd", [128, 1], F32)
    qm = P.sb("qm", [128, 2, NSEQ, 128], BF16)
    kendm = P.sb("kendm", [128, NSEQ, 256], BF16)
    S0 = [P.sb("S0_%d" % i, [128, 2, 512], F32) for i in range(3)]
    S0b = [P.sb("S0b_%d" % i, [128, 2, 512], BF16) for i in range(3)]
    S1 = [P.sb("S1_%d" % i, [128, 2, 512], F32) for i in range(3)]
    z_ps = P.ps("z_ps", [128, 256], F32)
    bT_ps = P.ps("bT_ps", [128, 2, 128], F32)
    bsuf_ps = P.ps("bsuf_ps", [128, 256], F32)
    att_ps = P.ps("att_ps", [128, 128], F32)
    o_ps = P.ps("o_ps", [128, 512], F32)
    sn_ps = [P.ps("sn_ps%d" % i, [128, 512], F32) for i in range(2)]
    tp_ps = P.ps("tp_ps", [128, 4, 128], BF16)
    it = 0
    nsn = 0
    for t in range(NT):
        samp = (t == NT - 1)
        x = "s" if samp else "p"
        triT, sufT = C("triT_" + x), C("sufT_" + x)
        tok = slice(t * 128, (t + 1) * 128)
        for h in range(4):
            b = it % NB
            it += 1
            P.dma("sync", lrT[b][0:16, :], s_lr_fm[:, tok])
            P.dma("sync", qT[b][:], s_qa_fm[h * 256:(h + 1) * 256, tok].re("(c p) t -> p c t", p=128))
            P.dma("sync", kT[b][:], s_ka_fm[h * 256:(h + 1) * 256, tok].re("(c p) t -> p c t", p=128))
            P.dma("sync", ktok[b][:], s_ka_tm[tok, h * 256:(h + 1) * 256])
            P.dma("sync", vv[b][:], s_va[tok, h * 512:(h + 1) * 512])
            P.dma("sync", gate[b][:], s_ga[tok, h * 512:(h + 1) * 512])
            P.op("tensor", "matmul", [z_ps[:]], [lrT[b][:], w2e[:]], z_ps[:], lrT[b][:], w2e[:, h * 256:(h + 1) * 256],
                 start=True, stop=True)
            P.op("scalar", "activation", [ee[b][:]], [z_ps[:]], out=ee[b][:], in_=z_ps[:], func=AF.Exp, scale=-1.0)
            P.op("scalar", "activation", [sp[b][:]], [ee[b][:]], out=sp[b][:], in_=ee[b][:], func=AF.Ln, bias=1.0, scale=1.0)
            for c in range(2):
                P.op("tensor", "matmul", [bT_ps[:]], [sp[b][:], triT], bT_ps[:, c, :], sp[b][:, c * 128:(c + 1) * 128], triT,
                     start=True, stop=True, same_ok=True)
            P.op("tensor", "matmul", [bsuf_ps[:]], [sp[b][:], sufT], bsuf_ps[:], sufT, sp[b][:], start=True, stop=True)
            P.op("scalar", "activation", [ebT[b][:]], [bT_ps[:]], out=ebT[b][:], in_=bT_ps[:], func=AF.Exp, scale=-1.0 / 16)
            P.op("scalar", "activation", [enbT[b][:]], [bT_ps[:]], out=enbT[b][:], in_=bT_ps[:], func=AF.Exp, scale=1.0 / 16)
            P.op("scalar", "activation", [esuf[b][:]], [bsuf_ps[:]], out=esuf[b][:], in_=bsuf_ps[:], func=AF.Exp, scale=-1.0 / 16)
            P.op("vector", "scalar_tensor_tensor", [qd[b][:]], [qT[b][:], ebT[b][:]], out=qd[b][:], in0=qT[b][:], scalar=0.0625,
                 in1=ebT[b][:], op0=ALU.mult, op1=ALU.mult)
            P.op("vector", "tensor_tensor", [kd[b][:]], [kT[b][:], enbT[b][:]], out=kd[b][:], in0=kT[b][:], in1=enbT[b][:], op=ALU.mult)
            P.op("gpsimd", "tensor_tensor", [kend[b][:]], [ktok[b][:], esuf[b][:]], out=kend[b][:], in0=ktok[b][:], in1=esuf[b][:],
                 op=ALU.mult)
            for c in range(2):
                P.op("tensor", "matmul", [att_ps[:]], [kd[b][:], qd[b][:]], att_ps[:], kd[b][:, c, :], qd[b][:, c, :],
                     start=(c == 0), stop=(c == 1), same_ok=True)
            P.op("vector", "tensor_tensor", [attT[b][:]], [att_ps[:], triT], out=attT[b][:], in0=att_ps[:], in1=triT, op=ALU.mult)
            P.op("tensor", "matmul", [o_ps[:]], [attT[b][:], vv[b][:]], o_ps[:], attT[b][:], vv[b][:], start=True, stop=False)
            if not samp:
                for c in range(2):
                    P.op("tensor", "matmul", [o_ps[:]], [qd[b][:], Sbf[h][:]], o_ps[:], qd[b][:, c, :], Sbf[h][:, c, :],
                         start=False, stop=(c == 1), same_ok=True)
                for c in range(2):
                    sn = sn_ps[nsn % 2]; nsn += 1
                    P.op("tensor", "matmul", [sn[:]], [kend[b][:], vv[b][:]], sn[:], kend[b][:, c * 128:(c + 1) * 128], vv[b][:],
                         start=True, stop=True)
                    P.op("vector", "scalar_tensor_tensor", [S[h][:]], [S[h][:], ebT[b][:], sn[:]], out=S[h][:, c, :],
                         in0=S[h][:, c, :], scalar=ebT[b][:, c, 127:128], in1=sn[:], op0=ALU.mult, op1=ALU.add)
                P.op("scalar", "copy", [Sbf[h][:]], [S[h][:]], out=Sbf[h][:], in_=S[h][:])
                if t == NTP - 1:
                    P.dma("sync", glap_d[h].re("(c p) v -> p c v", p=128), S[h][:], semkey=S[h])
            else:
                for c in range(2):
                    P.op("vector", "tensor_tensor", [qm[:]], [qd[b][:], C("segF")], out=qm[:, c, :, :],
                         in0=qd[b][:, c, :].un(1).bc([128, NSEQ, 128]), in1=C("segF").re("p (s t) -> p s t", s=NSEQ), op=ALU.mult)
                P.op("gpsimd", "tensor_tensor", [kendm[:]], [kend[b][:], C("segP")], out=kendm[:],
                     in0=kend[b][:].un(1).bc([128, NSEQ, 256]), in1=C("segP").un(2).bc([128, NSEQ, 256]), op=ALU.mult)
                for s in range(NSEQ):
                    r = s % 3
                    P.dma("sync", S0[r][:], sgla_d[s, h].re("(c p) v -> p c v", p=128))
                    P.op("scalar", "copy", [S0b[r][:]], [S0[r][:]], out=S0b[r][:], in_=S0[r][:])
                    for c in range(2):
                        P.op("tensor", "matmul", [o_ps[:]], [qm[:], S0b[r][:]], o_ps[:], qm[:, c, s, :], S0b[r][:, c, :],
                             start=False, stop=(s == NSEQ - 1 and c == 1), same_ok=True)
                    for c in range(2):
                        sn = sn_ps[nsn % 2]; nsn += 1
                        P.op("tensor", "matmul", [sn[:]], [kendm[:], vv[b][:]], sn[:], kendm[:, s, c * 128:(c + 1) * 128], vv[b][:],
                             start=True, stop=True)
                        P.op("vector", "scalar_tensor_tensor", [S1[r][:]], [S0[r][:], ebT[b][:], sn[:]], out=S1[r][:, c, :],
                             in0=S0[r][:, c, :], scalar=ebT[b][:, c, 8 * s + 7:8 * s + 8], in1=sn[:], op0=ALU.mult, op1=ALU.add)
                    P.dma("sync", glas_d[s, h].re("(c p) v -> p c v", p=128), S1[r][:], semkey=S1[r])
            P.op("scalar", "activation", [junk[:], ssq[b][:]], [o_ps[:]], out=junk[:], in_=o_ps[:], func=AF.Square, accum_out=ssq[b][:])
            P.op("scalar", "activation", [rstd[b][:]], [ssq[b][:], eps_t[:]], out=rstd[b][:], in_=ssq[b][:], func=AF.Sqrt,
                 bias=eps_t[:], scale=1.0 / 512)
            P.op("vector", "reciprocal", [rstd[b][:]], [rstd[b][:]], out=rstd[b][:], in_=rstd[b][:])
            P.op("vector", "scalar_tensor_tensor", [on[b][:]], [o_ps[:], rstd[b][:], gnA[:]], out=on[b][:], in0=o_ps[:],
                 scalar=rstd[b][:, 0:1], in1=gnA[:], op0=ALU.mult, op1=ALU.mult)
            P.op("scalar", "activation", [gs[b][:]], [gate[b][:]], out=gs[b][:], in_=gate[b][:], func=AF.Silu)
            P.op("gpsimd", "tensor_tensor", [og[b][:]], [on[b][:], gs[b][:]], out=og[b][:], in0=on[b][:], in1=gs[b][:], op=ALU.mult)
            for j in range(4):
                P.op("tensor", "transpose", [tp_ps[:]], [og[b][:], ident_bf[:]], out=tp_ps[:, j, :], in_=og[b][:, j * 128:(j + 1) * 128],
                     identity=ident_bf[:], same_ok=True)
            P.op("vector", "tensor_copy", [oT[b][:]], [tp_ps[:]], out=oT[b][:], in_=tp_ps[:])
            P.dma("sync", s_oaT[h * 512:(h + 1) * 512, tok].re("(j p) t -> p j t", p=128), oT[b][:], semkey=oT[b])
    P.pop()


def _phase_gdn(P, C, L):
    d = L
    P.scope_name = "p4_gdn"
    s_qkv_fm, s_qkv_tm, s_bd, s_gb, s_obT = (d[k] for k in ["s_qkv_fm", "s_qkv_tm", "s_bd", "s_gb", "s_obT"])
    cwT_d, alog_d, dtb_d, gnb_d, sgdn_d, sconv_d, gdnp_d, gdns_d, convp_d, convs_d, ident_bf = (d[k] for k in
        ["cwT_d", "alog_d", "dtb_d", "gnb_d", "sgdn_d", "sconv_d", "gdnp_d", "gdns_d", "convp_d", "convs_d", "ident_bf"])
    P.push()

    def bl(name, src, n):
        t = P.sb(name, [128, n], F32)
        P.dma("sync", t[:], V(src.t.ap()[0:1, 0:n].partition_broadcast(128)[:, 0, :], [src]))
        return t
    gnB = bl("gnB", gnb_d, 128)
    alog_b = bl("alog_b", alog_d, 16)
    dtb_b = bl("dtb_b", dtb_d, 16)
    negA = P.sb("negA", [128, 16], F32)
    P.op("scalar", "activation", [negA[:]], [alog_b[:]], out=negA[:], in_=alog_b[:], func=AF.Exp)
    P.op("vector", "tensor_scalar", [negA[:]], [negA[:]], out=negA[:], in0=negA[:], scalar1=-1.0, scalar2=None, op0=ALU.mult)
    cw = P.sb("cw", [128, 48, 4], F32)
    P.dma("sync", cw[:], cwT_d[:])
    eps_t = P.sb("eps4", [128, 1], F32)
    P.op("vector", "memset", [eps_t[:]], [], eps_t[:], EPS)
    P.push()
    cvs = P.sb("cvs", [51, 6144], F32)
    P.dma("sync", cvs[48:51, :], s_qkv_tm[125:128, :])
    P.dma("sync", convp_d[:], cvs[48:51, :], semkey=cvs)
    for s in range(NSEQ):
        P.dma("sync", cvs[3 * s:3 * s + 3, :], s_qkv_tm[128 + 8 * s + 5:128 + 8 * s + 8, :])
    P.dma("sync", convs_d.ap().re("s r c -> (s r) c"), cvs[0:48, :], semkey=cvs)
    P.pop()
    sct = P.sb("sct", [48, 6144], F32)
    P.dma("sync", sct[:], sconv_d.ap().re("s r c -> (s r) c"))
    S = [P.sb("Sg%d" % i, [128, 128], F32) for i in range(16)]
    Sbf = [P.sb("Sgbf%d" % i, [128, 128], BF16) for i in range(16)]
    for i in range(16):
        P.op("gpsimd", "memset", [S[i][:]], [], S[i][:], 0.0)
        P.op("gpsimd", "memset", [Sbf[i][:]], [], Sbf[i][:], 0.0)
    prev3 = P.sb("prev3", [128, 16, 3, 3], F32)
    P.op("gpsimd", "memset", [prev3[:]], [], prev3[:], 0.0)
    bd = P.sb("bd", [128, 32], F32)
    beta = P.sb("beta", [128, 16], F32); tmp16 = P.sb("tmp16", [128, 16], F32); g = P.sb("g", [128, 16], F32)
    gh = P.sb("gh", [128, 16], F32); eg = P.sb("eg", [128, 16], F32); egsuf = P.sb("egsuf", [128, 16], F32)
    bq = P.sb("bq", [128, 16], F32)
    Rl = P.sb("Rl", [128, NSEQ, 16], F32); egl = P.sb("egl", [128, NSEQ, 16], F32)
    diag = P.sb("diag", [128, 16, 128], F32)
    M1 = diag
    Ds = P.sb("Ds", [128, 16, 128], F32); DT = P.sb("DT", [128, 16, 128], F32); egB = P.sb("egB", [128, 16, 128], F32)
    gbt = P.sb("gbt", [128, 2048], F32); gsb = gbt
    ogall = P.sb("ogall", [128, 2048], BF16)
    oTs = [P.sb("oTs%d" % i, [128, 8, 128], BF16) for i in range(2)]
    NB = int(os.environ.get("GDN_G", "4"))

    def mk(name, shape, dt):
        return [P.sb("%s%d" % (name, i), shape, dt) for i in range(NB)]
    ext = mk("ext", [128, 3, 176], F32)
    cacc = mk("cacc", [128, 3, 128], F32); act = mk("act", [128, 3, 128], F32)
    sq = mk("sq", [128, 2, 128], F32); rn = mk("rn", [128, 2, 128], F32)
    qn = mk("qn", [128, 128], BF16); qg = mk("qg", [128, 128], BF16); knf = mk("knf", [128, 128], F32); kn = mk("kn", [128, 128], BF16)
    rhsv = mk("rhsv", [128, 128], BF16); rhsw = mk("rhsw", [128, 128], BF16); kend = mk("kendg", [128, 128], BF16)
    mm = mk("mm", [128, 128], F32); Cl = mk("Cl", [128, NLEV, 128], F32)
    U = mk("U", [128, 128], F32); X = mk("X", [128, 128], F32); Psb = mk("Psb", [128, 128], F32); Ubf = mk("Ubf", [128, 128], BF16)
    nwT = mk("nwT", [128, 128], BF16); vnew = mk("vnew", [128, 128], BF16); qkT = mk("qkT", [128, 128], BF16)
    on = mk("ong", [128, 128], F32); ssq = mk("ssqg", [128, 1], F32); rstd = mk("rstdg", [128, 1], F32)
    junk = P.sb("junk4", [128, 128], BF16)
    nwTm = P.sb("nwTm", [128, NSEQ, 128], BF16); qgm = P.sb("qgm", [128, NSEQ, 128], BF16); kendm = P.sb("kendmg", [128, NSEQ, 128], BF16)
    S0h = [P.sb("S0h", [128, NSEQ, 128], F32)] * NB; S0hb = [P.sb("S0hb", [128, NSEQ, 128], BF16)] * NB
    S1h = [P.sb("S1h", [128, NSEQ, 128], F32)] * NB
    E1 = S1h[0]
    banks = [P.ps("gbank%d" % i, [128, 512], F32) for i in range(7)]

    def sub(bk, c0, n, name, pat=None, **kw):
        ap = banks[bk].t[:, c0:c0 + n]
        if pat:
            ap = ap.rearrange(pat, **kw)
        return SubBuf(banks[bk], ap, name)
    CB = NB
    ghB_half = [sub(CB, 0, 256, "ghB_ps0", "p (a b) -> p a b", a=2)] * 2
    sm_ps = sub(CB, 256, 32, "sm_ps")
    glB_ps = sub(CB, 288, 128, "glB_ps", "p (a b) -> p a b", b=16)
    SN = [NB + 1, NB + 2]
    sn_bank = [sub(SN[i], 0, 512, "sng_ps%d" % i, "p (a b) -> p a b", a=4) for i in range(2)]
    tp_ps = P.ps("tpg_ps", [128, 8, 128], BF16)
    slot_ps = []
    for i in range(NB):
        slot_ps.append(dict(
            ss=sub(i, 0, 256, "ss%d" % i), tk=sub(i, 256, 256, "tk%d" % i, "p (a b) -> p a b", a=2), tpc=sub(i, 0, 48, "tpc%d" % i),
            kk=sub(i, 0, 128, "kk%d" % i), lp=sub(i, 128, 128, "lp%d" % i), lu=sub(i, 256, 128, "lu%d" % i), lx=sub(i, 384, 128, "lx%d" % i),
            sn=sub(i, 0, 512, "sn%d" % i, "p (a b) -> p a b", a=4), wT=sub(i, 0, 128, "wT%d" % i), vn=sub(i, 128, 128, "vn%d" % i), qk=sub(i, 256, 128, "qk%d" % i), o=sub(i, 384, 128, "o%d" % i)))
    ident, ones = C("ident"), C("ones")
    it = 0
    G_T = [int(v) for v in os.environ.get("GDN_T", ",".join(str(i) for i in range(NT))).split(",") if v != ""]
    G_H = int(os.environ.get("GDN_H", "16"))
    G_A = int(os.environ.get("GDN_A", "1000"))
    for t in G_T:
        samp = (t == NT - 1)
        x = "s" if samp else "p"
        nseg = NSEQ if samp else 1
        nlev = 3 if samp else NLEV
        triT, sufT = C("triT_" + x), C("sufT_" + x)
        tok = slice(t * 128, (t + 1) * 128)
        _ac = [0]

        def _A():
            _ac[0] += 1
            return _ac[0] <= G_A
        if _A():
            P.dma("sync", bd[:], s_bd[tok, :])
        if _A():
            P.dma("sync", gbt[:], s_gb[tok, :])
        if _A():
            P.op("scalar", "activation", [gsb[:]], [gbt[:]], out=gsb[:], in_=gbt[:], func=AF.Silu)
        if _A():
            P.op("scalar", "activation", [beta[:]], [bd[:]], out=beta[:], in_=bd[:, 0:16], func=AF.Sigmoid)
        if _A():
            P.op("vector", "tensor_tensor", [tmp16[:]], [bd[:], dtb_b[:]], out=tmp16[:], in0=bd[:, 16:32], in1=dtb_b[:], op=ALU.add)
        if _A():
            P.op("scalar", "activation", [tmp16[:]], [tmp16[:]], out=tmp16[:], in_=tmp16[:], func=AF.Exp)
        if _A():
            P.op("scalar", "activation", [tmp16[:]], [tmp16[:]], out=tmp16[:], in_=tmp16[:], func=AF.Ln, bias=1.0, scale=1.0)
        if _A():
            P.op("vector", "tensor_tensor", [g[:]], [tmp16[:], negA[:]], out=g[:], in0=tmp16[:], in1=negA[:], op=ALU.mult)
        if _A():
            P.op("tensor", "matmul", [sm_ps[:]], [g[:], triT], sm_ps[:, 0:16], triT, g[:], start=True, stop=True)
        if _A():
            P.op("tensor", "matmul", [sm_ps[:]], [g[:], sufT], sm_ps[:, 16:32], sufT, g[:], start=True, stop=True, same_ok=True)
        if _A():
            P.op("vector", "tensor_copy", [gh[:]], [sm_ps[:]], out=gh[:], in_=sm_ps[:, 0:16])
        if _A():
            P.op("scalar", "activation", [eg[:]], [sm_ps[:]], out=eg[:], in_=sm_ps[:, 0:16], func=AF.Exp)
        if _A():
            P.op("scalar", "activation", [egsuf[:]], [sm_ps[:]], out=egsuf[:], in_=sm_ps[:, 16:32], func=AF.Exp)
        if _A():
            P.op("vector", "tensor_tensor", [bq[:]], [beta[:], eg[:]], out=bq[:], in0=beta[:], in1=eg[:], op=ALU.mult)
        selL = C("selL_" + x)
        if _A():
            P.op("vector", "tensor_tensor", [Rl[:]], [gh[:], selL], out=Rl[:, 0:nseg, :], in0=gh[:].un(1).bc([128, nseg, 16]),
                 in1=selL.un(2).bc([128, nseg, 16]), op=ALU.mult)
        for s0 in range(0, nseg, 8):
            sn_ = min(8, nseg - s0)
            P.op("tensor", "matmul", [glB_ps[:]], [Rl[:], ones], glB_ps[:, 0:sn_, :], ones, Rl[:, s0:s0 + sn_, :], start=True, stop=True)
            P.op("scalar", "activation", [egl[:]], [glB_ps[:]], out=egl[:, s0:s0 + sn_, :], in_=glB_ps[:, 0:sn_, :], func=AF.Exp)
        if _A():
            P.op("gpsimd", "tensor_tensor", [diag[:]], [gh[:], ident], out=diag[:], in0=ident.un(1).bc([128, 16, 128]),
                 in1=gh[:].un(2).bc([128, 16, 128]), op=ALU.mult)
        for q8 in range(8):
            hs = slice(q8 * 2, q8 * 2 + 2)
            gps = ghB_half[q8 % 2]
            if _A():
                P.op("tensor", "matmul", [gps[:]], [diag[:], ones], gps[:], ones, diag[:, hs, :], start=True, stop=True)
            if _A():
                P.op("vector", "tensor_tensor", [E1[:]], [gps[:], gh[:]], out=E1[:, hs, :], in0=gps[:],
                     in1=gh[:, hs].un(2).bc([128, 2, 128]), op=ALU.subtract)
            if _A():
                P.op("scalar", "activation", [egB[:]], [gps[:]], out=egB[:, hs, :], in_=gps[:], func=AF.Exp)
        if _A():
            P.op("gpsimd", "tensor_tensor", [M1[:]], [E1[:], C("bigUs_" + x)], out=M1[:], in0=E1[:],
                 in1=C("bigUs_" + x).un(1).bc([128, 16, 128]), op=ALU.add)
        if _A():
            P.op("scalar", "activation", [Ds[:]], [M1[:]], out=Ds[:], in_=M1[:], func=AF.Exp, scale=-1.0)
        if _A():
            P.op("gpsimd", "tensor_tensor", [M1[:]], [E1[:], C("nbigL_" + x)], out=M1[:], in0=E1[:],
                 in1=C("nbigL_" + x).un(1).bc([128, 16, 128]), op=ALU.add)
        if _A():
            P.op("scalar", "activation", [DT[:]], [M1[:]], out=DT[:], in_=M1[:], func=AF.Exp)
        def head_gen(h, b):
            pz = slot_ps[b]
            tpc_ps, ss_ps, tk_ps, kk_ps, lp_ps, lu_ps, lx_ps = pz['tpc'], pz['ss'], pz['tk'], pz['kk'], pz['lp'], pz['lu'], pz['lx']
            wT_ps, vn_ps, qk_ps, o_ps = pz['wT'], pz['vn'], pz['qk'], pz['o']
            sn_ps = sn_bank[h % 2] if samp else pz['sn']
            if not samp:
                ev = ext[b][:, :, 0:131]
                cur = ev[:, :, 3:131]
                P.op("gpsimd", "tensor_copy", [ext[b][:]], [prev3[:]], out=ev[:, :, 0:3], in_=prev3[:, h, :, :])
                yield
                for p3 in range(3):
                    r0 = p3 * 2048 + h * 128
                    P.dma("sync", ev[:, p3, 3:131], s_qkv_fm[r0:r0 + 128, tok])
                    yield
                P.op("gpsimd", "tensor_copy", [prev3[:]], [ext[b][:]], out=prev3[:, h, :, :], in_=ev[:, :, 128:131])
                yield
                taps = [ev[:, :, i:i + 128] for i in range(4)]
                accv = cacc[b][:]
            else:
                ev = ext[b][:].re("p a (s w) -> p a s w", w=11)
                for p3 in range(3):
                    r0 = p3 * 2048 + h * 128
                    P.dma("sync", ev[:, p3, :, 3:11], s_qkv_fm[r0:r0 + 128, tok].re("p (s w) -> p s w", w=8))
                    yield
                    P.op("tensor", "transpose", [tpc_ps[:]], [sct[:], ident], out=tpc_ps[:], in_=sct[:, r0:r0 + 128],
                         identity=ident[0:48, 0:48])
                    yield
                    P.op("vector", "tensor_copy", [ext[b][:]], [tpc_ps[:]], out=ev[:, p3, :, 0:3],
                         in_=tpc_ps[:].re("p (s r) -> p s r", r=3))
                    yield
                taps = None
                accv = cacc[b][:].re("p a (s w) -> p a s w", w=8)
            for p3 in range(3):
                blk = p3 * 16 + h
                if not samp:
                    tp = [ev[:, p3, i:i + 128] for i in range(4)]
                    av = cacc[b][:, p3, :]
                else:
                    tp = [ev[:, p3, :, i:i + 8] for i in range(4)]
                    av = accv[:, p3, :, :]
                P.op("gpsimd", "tensor_scalar", [cacc[b][:]], [ext[b][:], cw[:]], out=av, in0=tp[0], scalar1=cw[:, blk, 0:1],
                     scalar2=None, op0=ALU.mult)
                yield
                for i in range(1, 4):
                    P.op("vector", "scalar_tensor_tensor", [cacc[b][:]], [ext[b][:], cw[:], cacc[b][:]], out=av, in0=tp[i],
                         scalar=cw[:, blk, i:i + 1], in1=av, op0=ALU.mult, op1=ALU.add)
                    yield
            P.op("scalar", "activation", [act[b][:]], [cacc[b][:]], out=act[b][:], in_=cacc[b][:], func=AF.Silu)
            yield
            P.op("gpsimd", "tensor_tensor", [sq[b][:]], [act[b][:]], out=sq[b][:], in0=act[b][:, 0:2, :], in1=act[b][:, 0:2, :], op=ALU.mult)
            yield
            P.op("tensor", "matmul", [ss_ps[:]], [sq[b][:], ones], ss_ps[:], ones, sq[b][:].re("p a t -> p (a t)"), start=True, stop=True)
            yield
            P.op("scalar", "activation", [rn[b][:]], [ss_ps[:], eps_t[:]], out=rn[b][:].re("p a t -> p (a t)"), in_=ss_ps[:], func=AF.Sqrt,
                 bias=eps_t[:], scale=1.0)
            yield
            P.op("vector", "reciprocal", [rn[b][:]], [rn[b][:]], out=rn[b][:], in_=rn[b][:])
            yield
            P.op("vector", "scalar_tensor_tensor", [qn[b][:]], [act[b][:], rn[b][:]], out=qn[b][:], in0=act[b][:, 0, :], scalar=128.0 ** -0.5,
                 in1=rn[b][:, 0, :], op0=ALU.mult, op1=ALU.mult)
            yield
            P.op("vector", "tensor_tensor", [knf[b][:]], [act[b][:], rn[b][:]], out=knf[b][:], in0=act[b][:, 1, :], in1=rn[b][:, 1, :], op=ALU.mult)
            yield
            P.op("gpsimd", "tensor_copy", [kn[b][:]], [knf[b][:]], out=kn[b][:], in_=knf[b][:])
            yield
            P.op("gpsimd", "tensor_tensor", [qg[b][:]], [qn[b][:], egB[:]], out=qg[b][:], in0=qn[b][:], in1=egB[:, h, :], op=ALU.mult)
            yield
            P.op("tensor", "transpose", [tk_ps[:]], [knf[b][:], ident], out=tk_ps[:, 0, :], in_=knf[b][:], identity=ident)
            yield
            P.op("tensor", "transpose", [tk_ps[:]], [act[b][:], ident], out=tk_ps[:, 1, :], in_=act[b][:, 2, :], identity=ident, same_ok=True)
            yield
            P.op("vector", "tensor_scalar", [rhsw[b][:]], [tk_ps[:], bq[:]], out=rhsw[b][:], in0=tk_ps[:, 0, :], scalar1=bq[:, h:h + 1],
                 scalar2=None, op0=ALU.mult)
            yield
            P.op("vector", "tensor_scalar", [kend[b][:]], [tk_ps[:], egsuf[:]], out=kend[b][:], in0=tk_ps[:, 0, :], scalar1=egsuf[:, h:h + 1],
                 scalar2=None, op0=ALU.mult)
            yield
            P.op("vector", "tensor_scalar", [rhsv[b][:]], [tk_ps[:], beta[:]], out=rhsv[b][:], in0=tk_ps[:, 1, :], scalar1=beta[:, h:h + 1],
                 scalar2=None, op0=ALU.mult)
            yield
            P.op("tensor", "matmul", [kk_ps[:]], [kn[b][:]], kk_ps[:], kn[b][:], kn[b][:], start=True, stop=True)
            yield
            P.op("vector", "scalar_tensor_tensor", [mm[b][:]], [kk_ps[:], beta[:], Ds[:]], out=mm[b][:], in0=kk_ps[:], scalar=beta[:, h:h + 1],
                 in1=Ds[:, h, :], op0=ALU.mult, op1=ALU.mult)
            yield
            P.op("gpsimd", "tensor_tensor", [Cl[b][:]], [mm[b][:], C("lmask")], out=Cl[b][:, 0:nlev, :],
                 in0=mm[b][:].un(1).bc([128, nlev, 128]), in1=C("lmask").re("p (l f) -> p l f", l=NLEV)[:, 0:nlev, :], op=ALU.mult)
            yield
            P.op("vector", "tensor_copy", [U[b][:]], [ident], out=U[b][:], in_=ident)
            yield
            P.op("gpsimd", "tensor_copy", [X[b][:]], [ident], out=X[b][:], in_=ident)
            yield
            for l in range(nlev):
                P.op("tensor", "matmul", [lp_ps[:]], [Cl[b][:], U[b][:]], lp_ps[:], Cl[b][:, l, :], U[b][:], start=True, stop=True)
                yield
                P.op("scalar", "copy", [Psb[b][:]], [lp_ps[:]], out=Psb[b][:], in_=lp_ps[:])
                yield
                P.op("tensor", "matmul", [lu_ps[:]], [X[b][:], Psb[b][:]], lu_ps[:], X[b][:], Psb[b][:], start=True, stop=True)
                yield
                P.op("vector", "tensor_tensor", [U[b][:]], [U[b][:], lu_ps[:]], out=U[b][:], in0=U[b][:], in1=lu_ps[:], op=ALU.subtract)
                yield
                if l < nlev - 1:
                    P.op("tensor", "transpose", [lx_ps[:]], [U[b][:], ident], out=lx_ps[:], in_=U[b][:], identity=ident)
                    yield
                    P.op("scalar", "copy", [X[b][:]], [lx_ps[:]], out=X[b][:], in_=lx_ps[:])
                    yield
            P.op("scalar", "copy", [Ubf[b][:]], [U[b][:]], out=Ubf[b][:], in_=U[b][:])
            yield
            P.op("tensor", "matmul", [wT_ps[:]], [rhsw[b][:], Ubf[b][:]], wT_ps[:], rhsw[b][:], Ubf[b][:], start=True, stop=True)
            yield
            P.op("scalar", "mul", [nwT[b][:]], [wT_ps[:]], out=nwT[b][:], in_=wT_ps[:], mul=-1.0)
            yield
            P.op("tensor", "matmul", [vn_ps[:]], [Ubf[b][:], rhsv[b][:]], vn_ps[:], Ubf[b][:], rhsv[b][:], start=True, stop=False)
            yield
            if not samp:
                P.op("tensor", "matmul", [vn_ps[:]], [nwT[b][:], Sbf[h][:]], vn_ps[:], nwT[b][:], Sbf[h][:], start=False, stop=True, same_ok=True)
                yield
            else:
                P.dma("sync", S0h[b][:], sgdn_d[:, h].re("s k v -> k s v"))
                yield
                P.op("gpsimd", "tensor_copy", [S0hb[b][:]], [S0h[b][:]], out=S0hb[b][:], in_=S0h[b][:])
                yield
                segF3 = C("segF").re("p (s t) -> p s t", s=NSEQ)
                P.op("vector", "tensor_tensor", [nwTm[:]], [nwT[b][:], C("segF")], out=nwTm[:], in0=nwT[b][:].un(1).bc([128, NSEQ, 128]),
                     in1=segF3, op=ALU.mult)
                yield
                P.op("gpsimd", "tensor_tensor", [qgm[:]], [qg[b][:], C("segF")], out=qgm[:], in0=qg[b][:].un(1).bc([128, NSEQ, 128]),
                     in1=segF3, op=ALU.mult)
                yield
                P.op("gpsimd", "tensor_tensor", [kendm[:]], [kend[b][:], C("segP")], out=kendm[:], in0=kend[b][:].un(1).bc([128, NSEQ, 128]),
                     in1=C("segP").un(2).bc([128, NSEQ, 128]), op=ALU.mult)
                yield
                for s in range(NSEQ):
                    P.op("tensor", "matmul", [vn_ps[:]], [nwTm[:], S0hb[b][:]], vn_ps[:], nwTm[:, s, :], S0hb[b][:, s, :],
                         start=False, stop=(s == NSEQ - 1), same_ok=True)
                    yield
            P.op("scalar", "copy", [vnew[b][:]], [vn_ps[:]], out=vnew[b][:], in_=vn_ps[:])
            yield
            P.op("tensor", "matmul", [qk_ps[:]], [kn[b][:], qn[b][:]], qk_ps[:], kn[b][:], qn[b][:], start=True, stop=True)
            yield
            P.op("vector", "tensor_tensor", [qkT[b][:]], [qk_ps[:], DT[:]], out=qkT[b][:], in0=qk_ps[:], in1=DT[:, h, :], op=ALU.mult)
            yield
            P.op("tensor", "matmul", [o_ps[:]], [qkT[b][:], vnew[b][:]], o_ps[:], qkT[b][:], vnew[b][:], start=True, stop=False)
            yield
            if not samp:
                P.op("tensor", "matmul", [o_ps[:]], [qg[b][:], Sbf[h][:]], o_ps[:], qg[b][:], Sbf[h][:], start=False, stop=True, same_ok=True)
                yield
            else:
                for s in range(NSEQ):
                    P.op("tensor", "matmul", [o_ps[:]], [qgm[:], S0hb[b][:]], o_ps[:], qgm[:, s, :], S0hb[b][:, s, :],
                         start=False, stop=(s == NSEQ - 1), same_ok=True)
                    yield
            if not samp:
                P.op("tensor", "matmul", [sn_ps[:]], [kend[b][:], vnew[b][:]], sn_ps[:, 0, :], kend[b][:], vnew[b][:], start=True, stop=True)
                yield
                P.op("vector", "scalar_tensor_tensor", [S[h][:]], [S[h][:], egl[:], sn_ps[:]], out=S[h][:], in0=S[h][:],
                     scalar=egl[:, 0, h:h + 1], in1=sn_ps[:, 0, :], op0=ALU.mult, op1=ALU.add)
                yield
                P.op("scalar", "copy", [Sbf[h][:]], [S[h][:]], out=Sbf[h][:], in_=S[h][:])
                yield
            else:
                for s4 in range(4):
                    for j in range(4):
                        s = s4 * 4 + j
                        P.op("tensor", "matmul", [sn_ps[:]], [kendm[:], vnew[b][:]], sn_ps[:, j, :], kendm[:, s, :], vnew[b][:],
                             start=True, stop=True, same_ok=(j > 0))
                        yield
                    for j in range(4):
                        s = s4 * 4 + j
                        P.op("vector", "scalar_tensor_tensor", [S1h[b][:]], [S0h[b][:], egl[:], sn_ps[:]], out=S1h[b][:, s, :],
                             in0=S0h[b][:, s, :], scalar=egl[:, s, h:h + 1], in1=sn_ps[:, j, :], op0=ALU.mult, op1=ALU.add)
                        yield
                P.dma("sync", gdns_d[:, h].re("s k v -> k s v"), S1h[b][:], semkey=S1h[b])
                yield
            P.op("scalar", "activation", [junk[:], ssq[b][:]], [o_ps[:]], out=junk[:], in_=o_ps[:], func=AF.Square, accum_out=ssq[b][:])
            yield
            P.op("scalar", "activation", [rstd[b][:]], [ssq[b][:], eps_t[:]], out=rstd[b][:], in_=ssq[b][:], func=AF.Sqrt,
                 bias=eps_t[:], scale=1.0 / 128)
            yield
            P.op("vector", "reciprocal", [rstd[b][:]], [rstd[b][:]], out=rstd[b][:], in_=rstd[b][:])
            yield
            P.op("vector", "scalar_tensor_tensor", [on[b][:]], [o_ps[:], rstd[b][:], gnB[:]], out=on[b][:], in0=o_ps[:],
                 scalar=rstd[b][:, 0:1], in1=gnB[:], op0=ALU.mult, op1=ALU.mult)
            yield
            P.op("gpsimd", "tensor_tensor", [ogall[:]], [on[b][:], gsb[:]], out=ogall[:, h * 128:(h + 1) * 128], in0=on[b][:],
                 in1=gsb[:, h * 128:(h + 1) * 128], op=ALU.mult)
            yield

        GI = 1 if samp else NB
        for h0_ in range(0, G_H, GI):
            gens = [head_gen(h, h - h0_) for h in range(h0_, min(G_H, h0_ + GI))]
            while gens:
                alive = []
                for g_ in gens:
                    try:
                        next(g_)
                        alive.append(g_)
                    except StopIteration:
                        pass
                gens = alive
        if t == NTP - 1:
            for hh in range(G_H):
                P.dma("sync", gdnp_d[hh], S[hh][:], semkey=S[hh])
        for g8 in (range(2) if G_H > 0 else []):
            for j in range(8):
                hh = g8 * 8 + j
                P.op("tensor", "transpose", [tp_ps[:]], [ogall[:], ident_bf[:]], out=tp_ps[:, j, :], in_=ogall[:, hh * 128:(hh + 1) * 128],
                     identity=ident_bf[:], same_ok=(j > 0))
            P.op("vector", "tensor_copy", [oTs[g8][:]], [tp_ps[:]], out=oTs[g8][:], in_=tp_ps[:])
            P.dma("sync", s_obT[g8 * 1024:(g8 + 1) * 1024, tok].re("(j p) t -> p j t", p=128), oTs[g8][:], semkey=oTs[g8])
    P.pop()


def _phase_post(P, C, L):
    d = L
    P.scope_name = "p5_post"
    s_oaT, s_obT, s_ma, s_mb, x_d, y_d, wba_d, wbb_d, wo_d, fng_d, ident_bf = (d[k] for k in
        ["s_oaT", "s_obT", "s_ma", "s_mb", "x_d", "y_d", "wba_d", "wbb_d", "wo_d", "fng_d", "ident_bf"])
    P.push()
    gfin = P.sb("gfin", [128, D], F32)
    P.dma("sync", gfin[:], V(fng_d.t.ap()[0:1, :].partition_broadcast(128)[:, 0, :], [fng_d]))
    eps_t = P.sb("eps5", [128, 1], F32)
    P.op("vector", "memset", [eps_t[:]], [], eps_t[:], EPS)
    pA = [P.ps("pp%d" % i, [128, 512], F32) for i in range(4)]
    tp_ps = P.ps("tp5", [128, 4, 128], BF16)
    halves = [list(range(0, 9)), list(range(9, NT))]
    wviews = {id(w): w.t.ap().rearrange("(ko ki) c -> ki ko c", ki=128) for w in (wba_d, wbb_d, wo_d)}

    def wload(dst, w, c0, n):
        for q4 in range(4):
            P.dma("gpsimd", dst[:, q4 * 4:(q4 + 1) * 4, 0:n], V(wviews[id(w)][:, q4 * 4:(q4 + 1) * 4, c0:c0 + n], [w]))
    for hi, tiles in enumerate(halves):
        nt = len(tiles)
        sfx = "_h%d" % hi
        t0 = tiles[0] * 128
        ntok = nt * 128
        P.push()
        mT = P.sb("mT" + sfx, [128, KT, 1152], BF16)
        P.push()
        oaT = P.sb("oaT" + sfx, [128, KT, 1152], BF16)
        obT = P.sb("obT" + sfx, [128, KT, 1152], BF16)
        for q4 in range(4):
            ks = slice(q4 * 4, q4 * 4 + 4)
            P.dma("sync", oaT[:, ks, 0:ntok], s_oaT[q4 * 512:(q4 + 1) * 512, t0:t0 + ntok].re("(k p) t -> p k t", p=128))
            P.dma("sync", obT[:, ks, 0:ntok], s_obT[q4 * 512:(q4 + 1) * 512, t0:t0 + ntok].re("(k p) t -> p k t", p=128))
        wa = P.sb("wa" + sfx, [128, KT, 512], BF16)
        wb = P.sb("wb" + sfx, [128, KT, 512], BF16)
        mat = [P.sb("mat%d" % i + sfx, [128, 512], F32) for i in range(2)]
        mbt = [P.sb("mbt%d" % i + sfx, [128, 512], F32) for i in range(2)]
        t1 = [P.sb("t1_%d" % i + sfx, [128, 512], F32) for i in range(2)]
        t2 = [P.sb("t2_%d" % i + sfx, [128, 512], F32) for i in range(2)]
        mg = [P.sb("mg%d" % i + sfx, [128, 512], BF16) for i in range(2)]
        it = 0
        for j in range(4):
            wload(wa, wba_d, j * 512, 512)
            wload(wb, wbb_d, j * 512, 512)
            for ti, t in enumerate(tiles):
                b = it % 2
                it += 1
                ya, yb = pA[2 * b], pA[2 * b + 1]
                lt = slice(ti * 128, (ti + 1) * 128)
                for k in range(KT):
                    P.op("tensor", "matmul", [ya[:]], [oaT[:], wa[:]], ya[:], oaT[:, k, lt], wa[:, k, :], start=(k == 0), stop=(k == KT - 1), same_ok=True)
                for k in range(KT):
                    P.op("tensor", "matmul", [yb[:]], [obT[:], wb[:]], yb[:], obT[:, k, lt], wb[:, k, :], start=(k == 0), stop=(k == KT - 1), same_ok=True)
                P.dma("sync", mat[b][:], s_ma[t * 128:(t + 1) * 128, j * 512:(j + 1) * 512])
                P.dma("sync", mbt[b][:], s_mb[t * 128:(t + 1) * 128, j * 512:(j + 1) * 512])
                P.op("scalar", "activation", [mat[b][:]], [mat[b][:]], out=mat[b][:], in_=mat[b][:], func=AF.Sigmoid)
                P.op("scalar", "activation", [mbt[b][:]], [mbt[b][:]], out=mbt[b][:], in_=mbt[b][:], func=AF.Sigmoid)
                P.op("vector", "tensor_tensor", [t1[b][:]], [ya[:], mat[b][:]], out=t1[b][:], in0=ya[:], in1=mat[b][:], op=ALU.mult)
                P.op("vector", "tensor_tensor", [t2[b][:]], [yb[:], mbt[b][:]], out=t2[b][:], in0=yb[:], in1=mbt[b][:], op=ALU.mult)
                P.op("gpsimd", "tensor_tensor", [mg[b][:]], [t1[b][:], t2[b][:]], out=mg[b][:], in0=t1[b][:], in1=t2[b][:], op=ALU.add)
                for c in range(4):
                    P.op("tensor", "transpose", [tp_ps[:]], [mg[b][:], ident_bf[:]], out=tp_ps[:, c, :], in_=mg[b][:, c * 128:(c + 1) * 128],
                         identity=ident_bf[:], same_ok=(c > 0))
                P.op("scalar", "copy", [mT[:]], [tp_ps[:]], out=mT[:, 4 * j:4 * j + 4, lt], in_=tp_ps[:])
        P.pop()
        P.push()
        wo = P.sb("wo" + sfx, [128, KT, D], BF16)
        for j in range(4):
            for q4 in range(4):
                P.dma("gpsimd", wo[:, q4 * 4:(q4 + 1) * 4, j * 512:(j + 1) * 512],
                      V(wviews[id(wo_d)][:, q4 * 4:(q4 + 1) * 4, j * 512:(j + 1) * 512], [wo_d]))
        xt = [P.sb("x5_%d" % i + sfx, [128, D], F32) for i in range(2)]
        res = [P.sb("res%d" % i + sfx, [128, D], F32) for i in range(2)]
        yo = [P.sb("yo%d" % i + sfx, [128, D], F32) for i in range(2)]
        junk = P.sb("junk5" + sfx, [128, D], BF16)
        ssq = [P.sb("ssq5_%d" % i + sfx, [128, 1], F32) for i in range(2)]
        for ti, t in enumerate(tiles):
            b = ti % 2
            lt = slice(ti * 128, (ti + 1) * 128)
            P.dma("sync", xt[b][:], x_d[t * 128:(t + 1) * 128, :])
            for j in range(4):
                ps = pA[j]
                for k in range(KT):
                    P.op("tensor", "matmul", [ps[:]], [mT[:], wo[:]], ps[:], mT[:, k, lt], wo[:, k, j * 512:(j + 1) * 512],
                         start=(k == 0), stop=(k == KT - 1), same_ok=True)
                P.op("vector", "tensor_tensor", [res[b][:]], [ps[:], xt[b][:]], out=res[b][:, j * 512:(j + 1) * 512], in0=ps[:],
                     in1=xt[b][:, j * 512:(j + 1) * 512], op=ALU.add)
            P.op("scalar", "activation", [junk[:], ssq[b][:]], [res[b][:]], out=junk[:], in_=res[b][:], func=AF.Square, accum_out=ssq[b][:])
            P.op("scalar", "activation", [ssq[b][:]], [ssq[b][:], eps_t[:]], out=ssq[b][:], in_=ssq[b][:], func=AF.Sqrt, bias=eps_t[:], scale=1.0 / D)
            P.op("vector", "reciprocal", [ssq[b][:]], [ssq[b][:]], out=ssq[b][:], in_=ssq[b][:])
            P.op("vector", "scalar_tensor_tensor", [yo[b][:]], [res[b][:], ssq[b][:], gfin[:]], out=yo[b][:], in0=res[b][:], scalar=ssq[b][:, 0:1],
                 in1=gfin[:], op0=ALU.mult, op1=ALU.mult)
            P.dma("sync", y_d[t * 128:(t + 1) * 128, :], yo[b][:], semkey=yo[b])
        P.pop()
        P.pop()
    P.pop()

def _core_inputs(c, inp):
    b = c // 2
    xs = inp["x_sample"][16 * c:16 * c + 16].reshape(128, D)
    x = np.concatenate([inp["x_prompt"][b], xs], axis=0)
    w2ext = np.concatenate([inp["w_alpha2"][0], inp["b_alpha"][0][None, :]], axis=0)
    cwT = np.ascontiguousarray(inp["conv_w"][0].T.reshape(48, 128, 4).transpose(1, 0, 2))
    return {
        "x": np.ascontiguousarray(x, dtype=np.float32),
        "w_in": np.ascontiguousarray(inp["w_in"][0]),
        "w2ext": np.ascontiguousarray(w2ext),
        "cwT": cwT,
        "a_log": np.ascontiguousarray(inp["a_log"][0][None, :]),
        "dt_bias": np.ascontiguousarray(inp["dt_bias"][0][None, :]),
        "ln_in_g": np.ascontiguousarray(inp["ln_in_g"][0][None, :]),
        "final_norm_g": np.ascontiguousarray(inp["final_norm_g"][None, :]),
        "gla_norm_g": np.ascontiguousarray(inp["gla_norm_g"][0][None, :]),
        "gdn_norm_g": np.ascontiguousarray(inp["gdn_norm_g"][0][None, :]),
        "w_br_a": np.ascontiguousarray(inp["w_br_a"][0]),
        "w_br_b": np.ascontiguousarray(inp["w_br_b"][0]),
        "w_out": np.ascontiguousarray(inp["w_out"][0]),
        "state_gla": np.ascontiguousarray(inp["state_gla"][0, 16 * c:16 * c + 16]),
        "state_gdn": np.ascontiguousarray(inp["state_gdn"][0, 16 * c:16 * c + 16]),
        "state_conv": np.ascontiguousarray(inp["state_conv"][0, 16 * c:16 * c + 16]),
        "consts": _CONST,
    }


def kernel(**inputs):
    inp = {k: np.asarray(v) for k, v in inputs.items()}
    nc = build_nc()
    in_maps = [_core_inputs(c, inp) for c in range(8)]
    res = run_bass_kernel_spmd(nc, in_maps, core_ids=list(range(8)))
    r = res.results
    y_prompt = np.stack([r[2 * b]["y"][:NTP * 128] for b in range(4)], axis=0)
    y_sample = np.concatenate([r[c]["y"][NTP * 128:].reshape(16, 8, D) for c in range(8)], axis=0)
    gla_p = np.stack([r[2 * b]["gla_p"] for b in range(4)], axis=0)[None]
    gdn_p = np.stack([r[2 * b]["gdn_p"] for b in range(4)], axis=0)[None]
    conv_p = np.stack([r[2 * b]["conv_p"] for b in range(4)], axis=0)[None]
    gla_s = np.concatenate([r[c]["gla_s"] for c in range(8)], axis=0)[None]
    gdn_s = np.concatenate([r[c]["gdn_s"] for c in range(8)], axis=0)[None]
    conv_s = np.concatenate([r[c]["conv_s"] for c in range(8)], axis=0)[None]
    return (y_prompt.astype(np.float32), y_sample.astype(np.float32), gla_p.astype(np.float32),
            gdn_p.astype(np.float32), conv_p.astype(np.float32), gla_s.astype(np.float32),
            gdn_s.astype(np.float32), conv_s.astype(np.float32))
```

```python
import contextlib
import os
import numpy as np
import concourse.bass as bass
import concourse.mybir as mybir
from concourse.bass_utils import run_bass_kernel_spmd

F32 = mybir.dt.float32
BF16 = mybir.dt.bfloat16
AF = mybir.ActivationFunctionType
ALU = mybir.AluOpType

D = 2048
KT = 16
NTP = 16
NT = 17
TOK = NT * 128
NSEQ = 16
DIN = 18480
C_QA, C_KA, C_VA, C_LR, C_GA, C_QKV, C_BD, C_GB, C_MA, C_MB = 0, 1024, 2048, 4096, 4112, 6160, 12304, 12336, 14384, 16432
EPS = 1e-6
BIG = 3.0e4
NLEV = 7
ATTACH_WAIT = os.environ.get("ATTACH_WAIT", "1") == "1"


class Bank:
    def __init__(self):
        self.reads = {}


class Buf:
    def __init__(self, t, name, bank=None):
        self.t = t
        self.name = name
        self.last_write = None
        self.reads = []
        self.bank = bank

    def __getitem__(self, idx):
        return V(self.t[idx], [self])

    def ap(self):
        return V(self.t.ap() if hasattr(self.t, "ap") and callable(getattr(self.t, "ap")) else self.t, [self])


class SubBuf:
    def __init__(self, parent, ap, name):
        self.parent = parent
        self.t = ap
        self.name = name

    def __getitem__(self, idx):
        return V(self.t[idx], [self.parent])


class V:
    def __init__(self, ap, keys):
        self.ap = ap
        self.keys = keys

    def __getitem__(self, idx):
        return V(self.ap[idx], self.keys)

    def re(self, pattern, **kw):
        return V(self.ap.rearrange(pattern, **kw), self.keys)

    def bc(self, shape):
        return V(self.ap.to_broadcast(list(shape)), self.keys)

    def un(self, axis):
        return V(self.ap.unsqueeze(axis), self.keys)

    def bitcast(self, dt):
        return V(self.ap.bitcast(dt), self.keys)


def _ap(x):
    return x.ap if isinstance(x, V) else x


class Prog:
    def __init__(self, nc):
        self.nc = nc
        self.stack = contextlib.ExitStack()
        self.engs = ["tensor", "vector", "scalar", "gpsimd", "sync"]
        self.ops = {e: [] for e in self.engs}
        self.sem = {}
        self.cnt = {e: 0 for e in self.engs}
        self.seen = {e: {} for e in self.engs}
        for e in self.engs:
            self.sem[e] = self.stack.enter_context(nc.semaphore("s_" + e))
        self.dma_sems = {}
        self.n_wait = 0
        self.n_ops = 0
        self.scopes = [self.stack]
        self.all_ops = {e: [] for e in self.engs}

    def push(self):
        self.scopes.append(contextlib.ExitStack())

    def pop(self):
        self.flush()
        self.scopes.pop().close()

    def sb(self, name, shape, dt):
        t = self.scopes[-1].enter_context(self.nc.sbuf_tensor(name, list(shape), dt))
        return Buf(t, name)

    def ps(self, name, shape, dt=F32):
        t = self.scopes[-1].enter_context(self.nc.psum_tensor(name, list(shape), dt))
        return Buf(t, name, bank=Bank())

    def dram(self, name, shape, dt, kind="Internal"):
        t = self.nc.dram_tensor(name, list(shape), dt, kind=kind)
        return Buf(t, name)

    def _collect(self, eng, ins, outs, same_ok):
        need = {}

        def add(sig):
            if sig is None:
                return
            sem, val, seng = sig
            if seng == eng and same_ok:
                return
            k = id(sem)
            if k not in need or need[k][1] < val:
                need[k] = (sem, val)
        for v in ins:
            if isinstance(v, V):
                for key in v.keys:
                    add(key.last_write)
                    if key.bank is not None:
                        for oe, sig in key.bank.reads.items():
                            if oe != eng:
                                add(sig)
        for v in outs:
            if isinstance(v, V):
                for key in v.keys:
                    add(key.last_write)
                    for r in key.reads:
                        add(r)
        waits = []
        for k, (sem, val) in need.items():
            if self.seen[eng].get(k, 0) >= val:
                continue
            self.seen[eng][k] = val
            waits.append((sem, val))
        return waits

    def _record(self, sig, ins, outs):
        for v in ins:
            if isinstance(v, V):
                for key in v.keys:
                    if key.bank is not None and sig[2] != "tensor":
                        key.bank.reads[sig[2]] = sig
                    key.reads.append(sig)
                    if len(key.reads) > 64:
                        best = {}
                        for s in key.reads:
                            if id(s[0]) not in best or best[id(s[0])][1] < s[1]:
                                best[id(s[0])] = s
                        key.reads = list(best.values())
        for v in outs:
            if isinstance(v, V):
                for key in v.keys:
                    key.last_write = sig
                    key.reads = []

    def op(self, eng, method, outs, ins, *args, same_ok=False, **kw):
        waits = self._collect(eng, ins, outs, same_ok)
        self.cnt[eng] += 1
        sig = (self.sem[eng], self.cnt[eng], eng)
        a2 = [_ap(a) for a in args]
        k2 = {k: _ap(v) for k, v in kw.items()}
        self.ops[eng].append((waits, method, a2, k2, (self.sem[eng], 1)))
        self.n_wait += len(waits)
        self.n_ops += 1
        self._record(sig, ins, outs)

    def dma(self, q, out, in_, semkey=None, **kw):
        waits = self._collect(q, [in_], [out], False)
        if semkey is None:
            semkey = out.keys[0]
        k = id(semkey)
        if k not in self.dma_sems:
            s = self.stack.enter_context(self.nc.semaphore("d_" + semkey.name))
            self.dma_sems[k] = [s, 0]
        ent = self.dma_sems[k]
        ent[1] += 16
        sig = (ent[0], ent[1], "dma")
        self.ops[q].append((waits, "dma_start", [], dict(out=_ap(out), in_=_ap(in_), **kw), (ent[0], 16)))
        self.n_wait += len(waits)
        self.n_ops += 1
        self._record(sig, [in_], [out])

    def final_wait(self, eng, bufs):
        need = {}
        for b in bufs:
            for sig in [b.last_write] + b.reads:
                if sig is None:
                    continue
                sem, val, _ = sig
                k = id(sem)
                if k not in need or need[k][1] < val:
                    need[k] = (sem, val)
        self.ops[eng].append((list(need.values()), None, None, None, None))

    def audit(self):
        final = {id(self.sem[e]): (self.cnt[e], "eng:" + e) for e in self.engs}
        for k, (sm, c) in self.dma_sems.items():
            final[id(sm)] = (c, "dma:" + sm.name)
        bad = 0
        mx = 0
        for e in self.engs:
            for (waits, method, a, k, inc) in self.all_ops[e]:
                for (sm, val) in waits:
                    mx = max(mx, val)
                    if id(sm) not in final or final[id(sm)][0] < val:
                        bad += 1
                        if bad < 10:
                            print("AUDIT: unsatisfiable wait on", e, getattr(sm, "name", sm), val, final.get(id(sm)))
        print("AUDIT: bad waits", bad, "max wait value", mx, "engine counts", dict(self.cnt),
              "max dma sem", max([c for (_, c) in self.dma_sems.values()] + [0]), "n dma sems", len(self.dma_sems))
        return bad

    def flush(self):
        for e in self.engs:
            self.all_ops[e].extend(self.ops[e])
        if any(self.ops[e] for e in self.engs):
            self.emit()
        self.ops = {e: [] for e in self.engs}

    def emit(self):
        nc = self.nc
        self.n_blocks = getattr(self, "n_blocks", 0) + 1
        with nc.named_scope("blk%02d_%s" % (self.n_blocks, getattr(self, "scope_name", "x"))), nc.Block() as block:
            def mk(ename):
                def body(e):
                    for (waits, method, a, k, inc) in self.ops[ename]:
                        attach = None
                        if ATTACH_WAIT and method is not None and method != "dma_start" and waits:
                            attach = waits[-1]
                            waits = waits[:-1]
                        for (sem, val) in waits:
                            e.wait_ge(sem, val)
                        if method is None:
                            continue
                        ins = getattr(e, method)(*a, **k)
                        if attach is not None:
                            ins._wait_ge(attach[0], attach[1])
                        ins.then_inc(inc[0], inc[1])
                return body
            block.tensor(mk("tensor"))
            block.vector(mk("vector"))
            block.scalar(mk("scalar"))
            block.gpsimd(mk("gpsimd"))
            block.sync(mk("sync"))

    def close(self):
        self.stack.close()


def _const_table():
    idx = np.arange(128)
    P_, F_ = np.meshgrid(idx, idx, indexing="ij")
    segp = np.zeros_like(P_)
    segs_p, segs_f = P_ // 8, F_ // 8
    c = {}
    c["ident"] = (P_ == F_).astype(np.float32)
    c["ones"] = np.ones((128, 128), np.float32)
    c["notI"] = (P_ != F_).astype(np.float32)
    for x, same in (("p", np.ones_like(P_, bool)), ("s", segs_p == segs_f)):
        c["triT_" + x] = ((P_ <= F_) & same).astype(np.float32)
        c["sufT_" + x] = ((P_ > F_) & same).astype(np.float32)
        c["bigU_" + x] = np.where((F_ <= P_) & same, 0.0, BIG).astype(np.float32)
        c["nbigL_" + x] = np.where((F_ >= P_) & same, 0.0, -BIG).astype(np.float32)
        c["bigUs_" + x] = np.where((F_ < P_) & same, 0.0, BIG).astype(np.float32)
    lm = np.zeros((128, NLEV, 128), np.float32)
    for l in range(NLEV):
        s = 1 << l
        lm[:, l, :] = ((P_ // (2 * s) == F_ // (2 * s)) & (P_ % (2 * s) >= s) & (F_ % (2 * s) < s))
    c["lmask"] = lm.reshape(128, NLEV * 128)
    segF = np.zeros((128, NSEQ, 128), np.float32)
    for s in range(NSEQ):
        segF[:, s, 8 * s:8 * s + 8] = 1.0
    c["segF"] = segF.reshape(128, NSEQ * 128)
    c["segP"] = (idx[:, None] // 8 == np.arange(NSEQ)[None, :]).astype(np.float32)
    c["selL_s"] = (idx[:, None] == (8 * np.arange(NSEQ)[None, :] + 7)).astype(np.float32)
    c["selL_p"] = (idx[:, None] == 127).astype(np.float32)
    offs = {}
    cols = []
    o = 0
    for k, v in c.items():
        offs[k] = (o, v.shape[1])
        cols.append(v)
        o += v.shape[1]
    return np.ascontiguousarray(np.concatenate(cols, axis=1)), offs


_CONST, _COFF = _const_table()
NCONST = _CONST.shape[1]


def build_nc(upto=99, debug=False):
    nc = bass.Bass("TRN2", target_bir_lowering=False)
    P = Prog(nc)

    def din(name, shape, dt=F32):
        return Buf(nc.dram_tensor(name, list(shape), dt, kind="ExternalInput"), name)

    def dout(name, shape, dt=F32):
        return Buf(nc.dram_tensor(name, list(shape), dt, kind="ExternalOutput"), name)

    def dscr(name, shape, dt=F32):
        if debug:
            return dout(name, shape, dt)
        return P.dram(name, shape, dt)

    x_d = din("x", [TOK, D])
    w_in_d = din("w_in", [D, DIN])
    w2e_d = din("w2ext", [17, 1024])
    cwT_d = din("cwT", [128, 48, 4])
    alog_d = din("a_log", [1, 16])
    dtb_d = din("dt_bias", [1, 16])
    lng_d = din("ln_in_g", [1, D])
    fng_d = din("final_norm_g", [1, D])
    gna_d = din("gla_norm_g", [1, 512])
    gnb_d = din("gdn_norm_g", [1, 128])
    wba_d = din("w_br_a", [D, D])
    wbb_d = din("w_br_b", [D, D])
    wo_d = din("w_out", [D, D])
    sgla_d = din("state_gla", [NSEQ, 4, 256, 512])
    sgdn_d = din("state_gdn", [NSEQ, 16, 128, 128])
    sconv_d = din("state_conv", [NSEQ, 3, 6144])
    const_d = din("consts", [128, NCONST])

    y_d = dout("y", [TOK, D])
    glap_d = dout("gla_p", [4, 256, 512])
    gdnp_d = dout("gdn_p", [16, 128, 128])
    convp_d = dout("conv_p", [3, 6144])
    glas_d = dout("gla_s", [NSEQ, 4, 256, 512])
    gdns_d = dout("gdn_s", [NSEQ, 16, 128, 128])
    convs_d = dout("conv_s", [NSEQ, 3, 6144])
    outs_all = [y_d, glap_d, gdnp_d, convp_d, glas_d, gdns_d, convs_d]

    s_ka_tm = dscr("s_ka_tm", [TOK, 1024])
    s_va = dscr("s_va", [TOK, 2048], BF16)
    s_ga = dscr("s_ga", [TOK, 2048])
    s_bd = dscr("s_bd", [TOK, 32])
    s_gb = dscr("s_gb", [TOK, 2048])
    s_ma = dscr("s_ma", [TOK, 2048])
    s_mb = dscr("s_mb", [TOK, 2048])
    s_qkv_tm = dscr("s_qkv_tm", [256, 6144])
    s_qa_fm = dscr("s_qa_fm", [1024, TOK])
    s_ka_fm = dscr("s_ka_fm", [1024, TOK])
    s_lr_fm = dscr("s_lr_fm", [16, TOK])
    s_qkv_fm = dscr("s_qkv_fm", [6144, TOK])
    s_oaT = dscr("s_oaT", [2048, TOK], BF16)
    s_obT = dscr("s_obT", [2048, TOK], BF16)
    dbg = [s_ka_tm, s_va, s_ga, s_bd, s_gb, s_ma, s_mb, s_qkv_tm, s_qa_fm, s_ka_fm, s_lr_fm, s_qkv_fm, s_oaT, s_obT]

    cst = P.sb("cst", [128, NCONST], F32)
    P.dma("sync", cst[:], const_d[:])

    def C(name):
        o, n = _COFF[name]
        return cst[:, o:o + n]

    ident_bf = P.sb("ident_bf", [128, 128], BF16)
    P.op("vector", "tensor_copy", [ident_bf[:]], [cst[:]], out=ident_bf[:], in_=C("ident"))

    P.scope_name = "p12_gemm"
    P.push()
    gB = P.sb("gB", [128, D], F32)
    def bload(dst, src, n):
        P.dma("sync", dst, V(src.t.ap()[0:1, 0:n].partition_broadcast(128)[:, 0, :], [src]))

    bload(gB[:], lng_d, D)

    psA = [P.ps("psA%d" % i, [128, 512], F32) for i in range(4)]
    psT = [P.ps("psT%d" % i, [128, 1024], BF16) for i in range(2)]
    psM = [P.ps("psM%d" % i, [128, 512], F32) for i in range(2)]

    hT = P.sb("hT", [128, KT, TOK], BF16)
    xt = [P.sb("xt%d" % i, [128, D], F32) for i in range(2)]
    hb = [P.sb("hb%d" % i, [128, D], BF16) for i in range(2)]
    junk = P.sb("junk", [128, D], BF16)
    ss = [P.sb("ss%d" % i, [128, 1], F32) for i in range(2)]
    rs = [P.sb("rs%d" % i, [128, 1], F32) for i in range(2)]
    eps_t = P.sb("eps_t", [128, 1], F32)
    P.op("vector", "memset", [eps_t[:]], [], eps_t[:], EPS)
    for t in range(NT):
        b = t % 2
        P.dma("sync", xt[b][:], x_d[t * 128:(t + 1) * 128, :])
        P.op("scalar", "activation", [junk[:], ss[b][:]], [xt[b][:]], out=junk[:], in_=xt[b][:], func=AF.Square,
             accum_out=ss[b][:])
        P.op("scalar", "activation", [rs[b][:]], [ss[b][:], eps_t[:]], out=rs[b][:], in_=ss[b][:], func=AF.Sqrt,
             bias=eps_t[:], scale=1.0 / D)
        P.op("vector", "reciprocal", [rs[b][:]], [rs[b][:]], out=rs[b][:], in_=rs[b][:])
        P.op("vector", "scalar_tensor_tensor", [hb[b][:]], [xt[b][:], rs[b][:], gB[:]], out=hb[b][:], in0=xt[b][:],
             scalar=rs[b][:, 0:1], in1=gB[:], op0=ALU.mult, op1=ALU.mult)
        for g in range(2):
            pt = psT[g]
            for j in range(8):
                k = g * 8 + j
                P.op("tensor", "transpose", [pt[:]], [hb[b][:], ident_bf[:]], out=pt[:, j * 128:(j + 1) * 128],
                     in_=hb[b][:, k * 128:(k + 1) * 128], identity=ident_bf[:], same_ok=True)
            dst = hT[:, g * 8:(g + 1) * 8, t * 128:(t + 1) * 128]
            src = pt[:].re("p (a b) -> p a b", a=8)
            if g == 0:
                P.op("vector", "tensor_copy", [dst], [pt[:]], out=dst, in_=src)
            else:
                P.op("scalar", "copy", [dst], [pt[:]], out=dst, in_=src)

    wblk = [P.sb("wblk%d" % i, [128, KT, 512], BF16) for i in range(2)]
    stg = [P.sb("stg%d" % i, [128, 512], F32) for i in range(4)]
    stgb = [P.sb("stgb%d" % i, [128, 512], BF16) for i in range(2)]
    all_tiles = list(range(NT))
    groups = [
        (C_QA, 1024, s_qa_fm, None, None, False),
        (C_KA, 1024, s_ka_fm, s_ka_tm, all_tiles, False),
        (C_VA, 2048, None, s_va, all_tiles, True),
        (C_LR, 16, s_lr_fm, None, None, False),
        (C_GA, 2048, None, s_ga, all_tiles, False),
        (C_QKV, 6144, s_qkv_fm, s_qkv_tm, [NTP - 1, NT - 1], False),
        (C_BD, 32, None, s_bd, all_tiles, False),
        (C_GB, 2048, None, s_gb, all_tiles, False),
        (C_MA, 2048, None, s_ma, all_tiles, False),
        (C_MB, 2048, None, s_mb, all_tiles, False),
    ]
    blocks = []
    for (c0, n, fm, tm, tiles, tbf) in groups:
        for o in range(0, n, 512):
            blocks.append((c0 + o, min(512, n - o), o, fm, tm, tiles, tbf))
    w_view = w_in_d.t.ap().rearrange("(ko ki) c -> ki ko c", ki=128)
    cnt = {"ps": 0, "stg": 0, "stgb": 0, "ev": 0}

    def evac(dst, src):
        cnt["ev"] += 1
        if cnt["ev"] % 2:
            P.op("vector", "tensor_copy", [dst], [src], out=dst, in_=src)
        else:
            P.op("scalar", "copy", [dst], [src], out=dst, in_=src)

    tok_groups = [(0, 512), (512, 512), (1024, 512), (1536, 512), (2048, 128)]
    if upto >= 2:
        for bi, (cg, n, off, fm, tm, tiles, tbf) in enumerate(blocks):
            wb = wblk[bi % 2]
            for q4 in range(4):
                P.dma("gpsimd", wb[:, q4 * 4:(q4 + 1) * 4, 0:n], V(w_view[:, q4 * 4:(q4 + 1) * 4, cg:cg + n], [w_in_d]))
            if fm is not None:
                for sbo in range(0, n, 128):
                    m = min(128, n - sbo)
                    for (t0, tn) in tok_groups:
                        ps = psA[cnt["ps"] % 4]
                        cnt["ps"] += 1
                        for k in range(KT):
                            P.op("tensor", "matmul", [ps[:]], [wb[:], hT[:]], ps[0:m, 0:tn], wb[:, k, sbo:sbo + m],
                                 hT[:, k, t0:t0 + tn], start=(k == 0), stop=(k == KT - 1), same_ok=True)
                        st = stg[cnt["stg"] % 4]
                        cnt["stg"] += 1
                        evac(st[0:m, 0:tn], ps[0:m, 0:tn])
                        P.dma("sync", fm[off + sbo:off + sbo + m, t0:t0 + tn], st[0:m, 0:tn], semkey=st)
            if tm is not None:
                for t in tiles:
                    ps = psA[cnt["ps"] % 4]
                    cnt["ps"] += 1
                    for k in range(KT):
                        P.op("tensor", "matmul", [ps[:]], [wb[:], hT[:]], ps[:, 0:n], hT[:, k, t * 128:(t + 1) * 128],
                             wb[:, k, 0:n], start=(k == 0), stop=(k == KT - 1), same_ok=True)
                    if tbf:
                        st = stgb[cnt["stgb"] % 2]
                        cnt["stgb"] += 1
                    else:
                        st = stg[cnt["stg"] % 4]
                        cnt["stg"] += 1
                    evac(st[:, 0:n], ps[:, 0:n])
                    if tm is s_qkv_tm:
                        r0 = 0 if t == NTP - 1 else 128
                    else:
                        r0 = t * 128
                    P.dma("sync", tm[r0:r0 + 128, off:off + n], st[:, 0:n], semkey=st)

    P.pop()

    if upto >= 3:
        _phase_gla(P, C, locals())
    if upto >= 4:
        _phase_gdn(P, C, locals())
    if upto >= 5:
        _phase_post(P, C, locals())
    final = list(outs_all) + (dbg if debug else [])
    P.final_wait("sync", final + dbg)
    P.flush()
    P.close()
    print("ops", P.n_ops, "waits", P.n_wait)
    P.audit()
    return nc


def _phase_gla(P, C, L):
    d = L
    P.scope_name = "p3_gla"
    s_qa_fm, s_ka_fm, s_ka_tm, s_va, s_ga, s_lr_fm, s_oaT = (d[k] for k in
        ["s_qa_fm", "s_ka_fm", "s_ka_tm", "s_va", "s_ga", "s_lr_fm", "s_oaT"])
    w2e_d, gna_d, sgla_d, glap_d, glas_d, ident_bf = (d[k] for k in
        ["w2e_d", "gna_d", "sgla_d", "glap_d", "glas_d", "ident_bf"])
    P.push()
    w2e = P.sb("w2e", [17, 1024], F32)
    P.dma("sync", w2e[:], w2e_d[:])
    gnA = P.sb("gnA", [128, 512], F32)
    P.dma("sync", gnA[:], V(gna_d.t.ap()[0:1, :].partition_broadcast(128)[:, 0, :], [gna_d]))
    eps_t = P.sb("eps3", [128, 1], F32)
    P.op("vector", "memset", [eps_t[:]], [], eps_t[:], EPS)
    S = [P.sb("S%d" % h, [128, 2, 512], F32) for h in range(4)]
    Sbf = [P.sb("Sbf%d" % h, [128, 2, 512], BF16) for h in range(4)]
    for h in range(4):
        P.op("gpsimd", "memset", [S[h][:]], [], S[h][:], 0.0)
        P.op("gpsimd", "memset", [Sbf[h][:]], [], Sbf[h][:], 0.0)
    NB = 3
    lrT = [P.sb("lrT%d" % i, [17, 128], F32) for i in range(NB)]
    for i in range(NB):
        P.op("vector", "memset", [lrT[i][:]], [], lrT[i][:], 1.0)

    def mk(name, shape, dt):
        return [P.sb("%s%d" % (name, i), shape, dt) for i in range(NB)]
    qT = mk("qT", [128, 2, 128], F32); kT = mk("kT", [128, 2, 128], F32)
    ktok = mk("ktok", [128, 256], F32); vv = mk("vv", [128, 512], BF16); gate = mk("gate", [128, 512], F32)
    ee = mk("ee", [128, 256], F32); sp = mk("sp", [128, 256], F32)
    ebT = mk("ebT", [128, 2, 128], F32); enbT = mk("enbT", [128, 2, 128], F32); esuf = mk("esuf", [128, 256], F32)
    qd = mk("qd", [128, 2, 128], BF16); kd = mk("kd", [128, 2, 128], BF16); kend = mk("kend", [128, 256], BF16)
    attT = mk("attT", [128, 128], BF16)
    on = mk("on", [128, 512], F32); gs = mk("gs", [128, 512], F32); og = mk("og", [128, 512], BF16)
    oT = mk("oT", [128, 4, 128], BF16)
    junk = P.sb("junk3", [128, 512], BF16)
    ssq = mk("ssq", [128, 1], F32); rstd = mk("rstd", [128, 1], F32)
    qm = P.sb("qm", [128, 2, NSEQ, 128], BF16)
    kendm = P.sb("kendm", [128, NSEQ, 256], BF16)
    S0 = [P.sb("S0_%d" % i, [128, 2, 512], F32) for i in range(3)]
    S0b = [P.sb("S0b_%d" % i, [128, 2, 512], BF16) for i in range(3)]
    S1 = [P.sb("S1_%d" % i, [128, 2, 512], F32) for i in range(3)]
    gbanks = [P.ps("abank%d" % i, [128, 512], F32) for i in range(2 * NB)]

    def sub(bk, c0, n, name, pat=None, **kw):
        ap = gbanks[bk].t[:, c0:c0 + n]
        if pat:
            ap = ap.rearrange(pat, **kw)
        return SubBuf(gbanks[bk], ap, name)
    slot_ps = []
    for i in range(NB):
        A, B = 2 * i, 2 * i + 1
        slot_ps.append(dict(z=sub(A, 0, 256, "z%d" % i), bT=sub(A, 256, 256, "bT%d" % i, "p (a b) -> p a b", a=2),
                            bsuf=sub(A, 0, 256, "bsuf%d" % i), att=sub(A, 256, 128, "att%d" % i),
                            sn=sub(A, 0, 512, "sn%d" % i), o=sub(B, 0, 512, "o%d" % i)))
    tp_ps = P.ps("tp_ps", [128, 4, 128], BF16)
    it = 0
    nsn = 0
    for t in range(NT):
        samp = (t == NT - 1)
        x = "s" if samp else "p"
        triT, sufT = C("triT_" + x), C("sufT_" + x)
        tok = slice(t * 128, (t + 1) * 128)
        def gla_gen(h, b):
            pz = slot_ps[b]
            z_ps, bT_ps, bsuf_ps, att_ps, o_ps = pz['z'], pz['bT'], pz['bsuf'], pz['att'], pz['o']
            sn_ps = [pz['sn'], pz['sn']]
            P.dma("sync", lrT[b][0:16, :], s_lr_fm[:, tok])
            yield
            P.dma("sync", qT[b][:], s_qa_fm[h * 256:(h + 1) * 256, tok].re("(c p) t -> p c t", p=128))
            yield
            P.dma("sync", kT[b][:], s_ka_fm[h * 256:(h + 1) * 256, tok].re("(c p) t -> p c t", p=128))
            yield
            P.dma("sync", ktok[b][:], s_ka_tm[tok, h * 256:(h + 1) * 256])
            yield
            P.dma("sync", vv[b][:], s_va[tok, h * 512:(h + 1) * 512])
            yield
            P.dma("sync", gate[b][:], s_ga[tok, h * 512:(h + 1) * 512])
            yield
            P.op("tensor", "matmul", [z_ps[:]], [lrT[b][:], w2e[:]], z_ps[:], lrT[b][:], w2e[:, h * 256:(h + 1) * 256],
                 start=True, stop=True)
            yield
            P.op("scalar", "activation", [ee[b][:]], [z_ps[:]], out=ee[b][:], in_=z_ps[:], func=AF.Exp, scale=-1.0)
            yield
            P.op("scalar", "activation", [sp[b][:]], [ee[b][:]], out=sp[b][:], in_=ee[b][:], func=AF.Ln, bias=1.0, scale=1.0)
            yield
            for c in range(2):
                P.op("tensor", "matmul", [bT_ps[:]], [sp[b][:], triT], bT_ps[:, c, :], sp[b][:, c * 128:(c + 1) * 128], triT,
                     start=True, stop=True, same_ok=True)
                yield
            P.op("tensor", "matmul", [bsuf_ps[:]], [sp[b][:], sufT], bsuf_ps[:], sufT, sp[b][:], start=True, stop=True)
            yield
            P.op("scalar", "activation", [ebT[b][:]], [bT_ps[:]], out=ebT[b][:], in_=bT_ps[:], func=AF.Exp, scale=-1.0 / 16)
            yield
            P.op("scalar", "activation", [enbT[b][:]], [bT_ps[:]], out=enbT[b][:], in_=bT_ps[:], func=AF.Exp, scale=1.0 / 16)
            yield
            P.op("scalar", "activation", [esuf[b][:]], [bsuf_ps[:]], out=esuf[b][:], in_=bsuf_ps[:], func=AF.Exp, scale=-1.0 / 16)
            yield
            P.op("vector", "scalar_tensor_tensor", [qd[b][:]], [qT[b][:], ebT[b][:]], out=qd[b][:], in0=qT[b][:], scalar=0.0625,
                 in1=ebT[b][:], op0=ALU.mult, op1=ALU.mult)
            yield
            P.op("vector", "tensor_tensor", [kd[b][:]], [kT[b][:], enbT[b][:]], out=kd[b][:], in0=kT[b][:], in1=enbT[b][:], op=ALU.mult)
            yield
            P.op("gpsimd", "tensor_tensor", [kend[b][:]], [ktok[b][:], esuf[b][:]], out=kend[b][:], in0=ktok[b][:], in1=esuf[b][:],
                 op=ALU.mult)
            yield
            for c in range(2):
                P.op("tensor", "matmul", [att_ps[:]], [kd[b][:], qd[b][:]], att_ps[:], kd[b][:, c, :], qd[b][:, c, :],
                     start=(c == 0), stop=(c == 1), same_ok=True)
                yield
            P.op("vector", "tensor_tensor", [attT[b][:]], [att_ps[:], triT], out=attT[b][:], in0=att_ps[:], in1=triT, op=ALU.mult)
            yield
            P.op("tensor", "matmul", [o_ps[:]], [attT[b][:], vv[b][:]], o_ps[:], attT[b][:], vv[b][:], start=True, stop=False)
            yield
            if not samp:
                for c in range(2):
                    P.op("tensor", "matmul", [o_ps[:]], [qd[b][:], Sbf[h][:]], o_ps[:], qd[b][:, c, :], Sbf[h][:, c, :],
                         start=False, stop=(c == 1), same_ok=True)
                    yield
                for c in range(2):
                    sn = sn_ps[0]
                    P.op("tensor", "matmul", [sn[:]], [kend[b][:], vv[b][:]], sn[:], kend[b][:, c * 128:(c + 1) * 128], vv[b][:],
                         start=True, stop=True)
                    yield
                    P.op("vector", "scalar_tensor_tensor", [S[h][:]], [S[h][:], ebT[b][:], sn[:]], out=S[h][:, c, :],
                         in0=S[h][:, c, :], scalar=ebT[b][:, c, 127:128], in1=sn[:], op0=ALU.mult, op1=ALU.add)
                    yield
                P.op("scalar", "copy", [Sbf[h][:]], [S[h][:]], out=Sbf[h][:], in_=S[h][:])
                yield
                if t == NTP - 1:
                    P.dma("sync", glap_d[h].re("(c p) v -> p c v", p=128), S[h][:], semkey=S[h])
                    yield
            else:
                for c in range(2):
                    P.op("vector", "tensor_tensor", [qm[:]], [qd[b][:], C("segF")], out=qm[:, c, :, :],
                         in0=qd[b][:, c, :].un(1).bc([128, NSEQ, 128]), in1=C("segF").re("p (s t) -> p s t", s=NSEQ), op=ALU.mult)
                    yield
                P.op("gpsimd", "tensor_tensor", [kendm[:]], [kend[b][:], C("segP")], out=kendm[:],
                     in0=kend[b][:].un(1).bc([128, NSEQ, 256]), in1=C("segP").un(2).bc([128, NSEQ, 256]), op=ALU.mult)
                yield
                for s in range(NSEQ):
                    r = s % 3
                    P.dma("sync", S0[r][:], sgla_d[s, h].re("(c p) v -> p c v", p=128))
                    yield
                    P.op("scalar", "copy", [S0b[r][:]], [S0[r][:]], out=S0b[r][:], in_=S0[r][:])
                    yield
                    for c in range(2):
                        P.op("tensor", "matmul", [o_ps[:]], [qm[:], S0b[r][:]], o_ps[:], qm[:, c, s, :], S0b[r][:, c, :],
                             start=False, stop=(s == NSEQ - 1 and c == 1), same_ok=True)
                        yield
                    for c in range(2):
                        sn = sn_ps[0]
                        P.op("tensor", "matmul", [sn[:]], [kendm[:], vv[b][:]], sn[:], kendm[:, s, c * 128:(c + 1) * 128], vv[b][:],
                             start=True, stop=True)
                        yield
                        P.op("vector", "scalar_tensor_tensor", [S1[r][:]], [S0[r][:], ebT[b][:], sn[:]], out=S1[r][:, c, :],
                             in0=S0[r][:, c, :], scalar=ebT[b][:, c, 8 * s + 7:8 * s + 8], in1=sn[:], op0=ALU.mult, op1=ALU.add)
                        yield
                    P.dma("sync", glas_d[s, h].re("(c p) v -> p c v", p=128), S1[r][:], semkey=S1[r])
                    yield
            P.op("scalar", "activation", [junk[:], ssq[b][:]], [o_ps[:]], out=junk[:], in_=o_ps[:], func=AF.Square, accum_out=ssq[b][:])
            yield
            P.op("scalar", "activation", [rstd[b][:]], [ssq[b][:], eps_t[:]], out=rstd[b][:], in_=ssq[b][:], func=AF.Sqrt,
                 bias=eps_t[:], scale=1.0 / 512)
            yield
            P.op("vector", "reciprocal", [rstd[b][:]], [rstd[b][:]], out=rstd[b][:], in_=rstd[b][:])
            yield
            P.op("vector", "scalar_tensor_tensor", [on[b][:]], [o_ps[:], rstd[b][:], gnA[:]], out=on[b][:], in0=o_ps[:],
                 scalar=rstd[b][:, 0:1], in1=gnA[:], op0=ALU.mult, op1=ALU.mult)
            yield
            P.op("scalar", "activation", [gs[b][:]], [gate[b][:]], out=gs[b][:], in_=gate[b][:], func=AF.Silu)
            yield
            P.op("gpsimd", "tensor_tensor", [og[b][:]], [on[b][:], gs[b][:]], out=og[b][:], in0=on[b][:], in1=gs[b][:], op=ALU.mult)
            yield
            for j in range(4):
                P.op("tensor", "transpose", [tp_ps[:]], [og[b][:], ident_bf[:]], out=tp_ps[:, j, :], in_=og[b][:, j * 128:(j + 1) * 128],
                     identity=ident_bf[:], same_ok=True)
            P.op("vector", "tensor_copy", [oT[b][:]], [tp_ps[:]], out=oT[b][:], in_=tp_ps[:])
            yield
            P.dma("sync", s_oaT[h * 512:(h + 1) * 512, tok].re("(j p) t -> p j t", p=128), oT[b][:], semkey=oT[b])
            yield

        GI = 1 if samp else NB
        for h0_ in range(0, 4, GI):
            gens = [gla_gen(h, h - h0_) for h in range(h0_, min(4, h0_ + GI))]
            while gens:
                alive = []
                for g_ in gens:
                    try:
                        next(g_)
                        alive.append(g_)
                    except StopIteration:
                        pass
                gens = alive
    P.pop()


def _phase_gdn(P, C, L):
    d = L
    P.scope_name = "p4_gdn"
    s_qkv_fm, s_qkv_tm, s_bd, s_gb, s_obT = (d[k] for k in ["s_qkv_fm", "s_qkv_tm", "s_bd", "s_gb", "s_obT"])
    cwT_d, alog_d, dtb_d, gnb_d, sgdn_d, sconv_d, gdnp_d, gdns_d, convp_d, convs_d, ident_bf = (d[k] for k in
        ["cwT_d", "alog_d", "dtb_d", "gnb_d", "sgdn_d", "sconv_d", "gdnp_d", "gdns_d", "convp_d", "convs_d", "ident_bf"])
    P.push()

    def bl(name, src, n):
        t = P.sb(name, [128, n], F32)
        P.dma("sync", t[:], V(src.t.ap()[0:1, 0:n].partition_broadcast(128)[:, 0, :], [src]))
        return t
    gnB = bl("gnB", gnb_d, 128)
    alog_b = bl("alog_b", alog_d, 16)
    dtb_b = bl("dtb_b", dtb_d, 16)
    negA = P.sb("negA", [128, 16], F32)
    P.op("scalar", "activation", [negA[:]], [alog_b[:]], out=negA[:], in_=alog_b[:], func=AF.Exp)
    P.op("vector", "tensor_scalar", [negA[:]], [negA[:]], out=negA[:], in0=negA[:], scalar1=-1.0, scalar2=None, op0=ALU.mult)
    cw = P.sb("cw", [128, 48, 4], F32)
    P.dma("sync", cw[:], cwT_d[:])
    eps_t = P.sb("eps4", [128, 1], F32)
    P.op("vector", "memset", [eps_t[:]], [], eps_t[:], EPS)
    P.push()
    cvs = P.sb("cvs", [51, 6144], F32)
    P.dma("sync", cvs[48:51, :], s_qkv_tm[125:128, :])
    P.dma("sync", convp_d[:], cvs[48:51, :], semkey=cvs)
    for s in range(NSEQ):
        P.dma("sync", cvs[3 * s:3 * s + 3, :], s_qkv_tm[128 + 8 * s + 5:128 + 8 * s + 8, :])
    P.dma("sync", convs_d.ap().re("s r c -> (s r) c"), cvs[0:48, :], semkey=cvs)
    P.pop()
    sct = P.sb("sct", [48, 6144], F32)
    P.dma("sync", sct[:], sconv_d.ap().re("s r c -> (s r) c"))
    S = [P.sb("Sg%d" % i, [128, 128], F32) for i in range(16)]
    Sbf = [P.sb("Sgbf%d" % i, [128, 128], BF16) for i in range(16)]
    for i in range(16):
        P.op("gpsimd", "memset", [S[i][:]], [], S[i][:], 0.0)
        P.op("gpsimd", "memset", [Sbf[i][:]], [], Sbf[i][:], 0.0)
    prev3 = P.sb("prev3", [128, 16, 3, 3], F32)
    P.op("gpsimd", "memset", [prev3[:]], [], prev3[:], 0.0)
    bd = P.sb("bd", [128, 32], F32)
    beta = P.sb("beta", [128, 16], F32); tmp16 = P.sb("tmp16", [128, 16], F32); g = P.sb("g", [128, 16], F32)
    gh = P.sb("gh", [128, 16], F32); eg = P.sb("eg", [128, 16], F32); egsuf = P.sb("egsuf", [128, 16], F32)
    bq = P.sb("bq", [128, 16], F32)
    Rl = P.sb("Rl", [128, NSEQ, 16], F32); egl = P.sb("egl", [128, NSEQ, 16], F32)
    diag = P.sb("diag", [128, 16, 128], F32)
    M1 = diag
    Ds = P.sb("Ds", [128, 16, 128], F32); DT = P.sb("DT", [128, 16, 128], F32); egB = P.sb("egB", [128, 16, 128], F32)
    gbt = P.sb("gbt", [128, 2048], F32); gsb = gbt
    ogall = P.sb("ogall", [128, 2048], BF16)
    oTs = [P.sb("oTs%d" % i, [128, 8, 128], BF16) for i in range(2)]
    NB = int(os.environ.get("GDN_G", "4"))

    def mk(name, shape, dt):
        return [P.sb("%s%d" % (name, i), shape, dt) for i in range(NB)]
    ext = mk("ext", [128, 3, 176], F32)
    cacc = mk("cacc", [128, 3, 128], F32); act = mk("act", [128, 3, 128], F32)
    sq = mk("sq", [128, 2, 128], F32); rn = mk("rn", [128, 2, 128], F32)
    qn = mk("qn", [128, 128], BF16); qg = mk("qg", [128, 128], BF16); knf = mk("knf", [128, 128], F32); kn = mk("kn", [128, 128], BF16)
    rhsv = mk("rhsv", [128, 128], BF16); rhsw = mk("rhsw", [128, 128], BF16); kend = mk("kendg", [128, 128], BF16)
    mm = mk("mm", [128, 128], F32); Cl = mk("Cl", [128, NLEV, 128], F32)
    U = mk("U", [128, 128], F32); X = mk("X", [128, 128], F32); Psb = mk("Psb", [128, 128], F32); Ubf = mk("Ubf", [128, 128], BF16)
    nwT = mk("nwT", [128, 128], BF16); vnew = mk("vnew", [128, 128], BF16); qkT = mk("qkT", [128, 128], BF16)
    on = mk("ong", [128, 128], F32); ssq = mk("ssqg", [128, 1], F32); rstd = mk("rstdg", [128, 1], F32)
    junk = P.sb("junk4", [128, 128], BF16)
    nwTm = P.sb("nwTm", [128, NSEQ, 128], BF16); qgm = P.sb("qgm", [128, NSEQ, 128], BF16); kendm = P.sb("kendmg", [128, NSEQ, 128], BF16)
    S0h = [P.sb("S0h", [128, NSEQ, 128], F32)] * NB; S0hb = [P.sb("S0hb", [128, NSEQ, 128], BF16)] * NB
    S1h = [P.sb("S1h", [128, NSEQ, 128], F32)] * NB
    E1 = S1h[0]
    banks = [P.ps("gbank%d" % i, [128, 512], F32) for i in range(7)]

    def sub(bk, c0, n, name, pat=None, **kw):
        ap = banks[bk].t[:, c0:c0 + n]
        if pat:
            ap = ap.rearrange(pat, **kw)
        return SubBuf(banks[bk], ap, name)
    CB = NB
    ghB_half = [sub(CB, 0, 256, "ghB_ps0", "p (a b) -> p a b", a=2)] * 2
    sm_ps = sub(CB, 256, 32, "sm_ps")
    glB_ps = sub(CB, 288, 128, "glB_ps", "p (a b) -> p a b", b=16)
    SN = [NB + 1, NB + 2]
    sn_bank = [sub(SN[i], 0, 512, "sng_ps%d" % i, "p (a b) -> p a b", a=4) for i in range(2)]
    tp_ps = P.ps("tpg_ps", [128, 8, 128], BF16)
    slot_ps = []
    for i in range(NB):
        slot_ps.append(dict(
            ss=sub(i, 0, 256, "ss%d" % i), tk=sub(i, 256, 256, "tk%d" % i, "p (a b) -> p a b", a=2), tpc=sub(i, 0, 48, "tpc%d" % i),
            kk=sub(i, 0, 128, "kk%d" % i), lp=sub(i, 128, 128, "lp%d" % i), lu=sub(i, 256, 128, "lu%d" % i), lx=sub(i, 384, 128, "lx%d" % i),
            sn=sub(i, 0, 512, "sn%d" % i, "p (a b) -> p a b", a=4), wT=sub(i, 0, 128, "wT%d" % i), vn=sub(i, 128, 128, "vn%d" % i), qk=sub(i, 256, 128, "qk%d" % i), o=sub(i, 384, 128, "o%d" % i)))
    ident, ones = C("ident"), C("ones")
    it = 0
    G_T = [int(v) for v in os.environ.get("GDN_T", ",".join(str(i) for i in range(NT))).split(",") if v != ""]
    G_H = int(os.environ.get("GDN_H", "16"))
    G_A = int(os.environ.get("GDN_A", "1000"))
    for t in G_T:
        samp = (t == NT - 1)
        x = "s" if samp else "p"
        nseg = NSEQ if samp else 1
        nlev = 3 if samp else NLEV
        triT, sufT = C("triT_" + x), C("sufT_" + x)
        tok = slice(t * 128, (t + 1) * 128)
        _ac = [0]

        def _A():
            _ac[0] += 1
            return _ac[0] <= G_A
        if _A():
            P.dma("sync", bd[:], s_bd[tok, :])
        if _A():
            P.dma("sync", gbt[:], s_gb[tok, :])
        if _A():
            P.op("scalar", "activation", [gsb[:]], [gbt[:]], out=gsb[:], in_=gbt[:], func=AF.Silu)
        if _A():
            P.op("scalar", "activation", [beta[:]], [bd[:]], out=beta[:], in_=bd[:, 0:16], func=AF.Sigmoid)
        if _A():
            P.op("vector", "tensor_tensor", [tmp16[:]], [bd[:], dtb_b[:]], out=tmp16[:], in0=bd[:, 16:32], in1=dtb_b[:], op=ALU.add)
        if _A():
            P.op("scalar", "activation", [tmp16[:]], [tmp16[:]], out=tmp16[:], in_=tmp16[:], func=AF.Exp)
        if _A():
            P.op("scalar", "activation", [tmp16[:]], [tmp16[:]], out=tmp16[:], in_=tmp16[:], func=AF.Ln, bias=1.0, scale=1.0)
        if _A():
            P.op("vector", "tensor_tensor", [g[:]], [tmp16[:], negA[:]], out=g[:], in0=tmp16[:], in1=negA[:], op=ALU.mult)
        if _A():
            P.op("tensor", "matmul", [sm_ps[:]], [g[:], triT], sm_ps[:, 0:16], triT, g[:], start=True, stop=True)
        if _A():
            P.op("tensor", "matmul", [sm_ps[:]], [g[:], sufT], sm_ps[:, 16:32], sufT, g[:], start=True, stop=True, same_ok=True)
        if _A():
            P.op("vector", "tensor_copy", [gh[:]], [sm_ps[:]], out=gh[:], in_=sm_ps[:, 0:16])
        if _A():
            P.op("scalar", "activation", [eg[:]], [sm_ps[:]], out=eg[:], in_=sm_ps[:, 0:16], func=AF.Exp)
        if _A():
            P.op("scalar", "activation", [egsuf[:]], [sm_ps[:]], out=egsuf[:], in_=sm_ps[:, 16:32], func=AF.Exp)
        if _A():
            P.op("vector", "tensor_tensor", [bq[:]], [beta[:], eg[:]], out=bq[:], in0=beta[:], in1=eg[:], op=ALU.mult)
        selL = C("selL_" + x)
        if _A():
            P.op("vector", "tensor_tensor", [Rl[:]], [gh[:], selL], out=Rl[:, 0:nseg, :], in0=gh[:].un(1).bc([128, nseg, 16]),
                 in1=selL.un(2).bc([128, nseg, 16]), op=ALU.mult)
        for s0 in range(0, nseg, 8):
            sn_ = min(8, nseg - s0)
            P.op("tensor", "matmul", [glB_ps[:]], [Rl[:], ones], glB_ps[:, 0:sn_, :], ones, Rl[:, s0:s0 + sn_, :], start=True, stop=True)
            P.op("scalar", "activation", [egl[:]], [glB_ps[:]], out=egl[:, s0:s0 + sn_, :], in_=glB_ps[:, 0:sn_, :], func=AF.Exp)
        if _A():
            P.op("gpsimd", "tensor_tensor", [diag[:]], [gh[:], ident], out=diag[:], in0=ident.un(1).bc([128, 16, 128]),
                 in1=gh[:].un(2).bc([128, 16, 128]), op=ALU.mult)
        for q8 in range(8):
            hs = slice(q8 * 2, q8 * 2 + 2)
            gps = ghB_half[q8 % 2]
            if _A():
                P.op("tensor", "matmul", [gps[:]], [diag[:], ones], gps[:], ones, diag[:, hs, :], start=True, stop=True)
            if _A():
                P.op("vector", "tensor_tensor", [E1[:]], [gps[:], gh[:]], out=E1[:, hs, :], in0=gps[:],
                     in1=gh[:, hs].un(2).bc([128, 2, 128]), op=ALU.subtract)
            if _A():
                P.op("scalar", "activation", [egB[:]], [gps[:]], out=egB[:, hs, :], in_=gps[:], func=AF.Exp)
        if _A():
            P.op("gpsimd", "tensor_tensor", [M1[:]], [E1[:], C("bigUs_" + x)], out=M1[:], in0=E1[:],
                 in1=C("bigUs_" + x).un(1).bc([128, 16, 128]), op=ALU.add)
        if _A():
            P.op("scalar", "activation", [Ds[:]], [M1[:]], out=Ds[:], in_=M1[:], func=AF.Exp, scale=-1.0)
        if _A():
            P.op("gpsimd", "tensor_tensor", [M1[:]], [E1[:], C("nbigL_" + x)], out=M1[:], in0=E1[:],
                 in1=C("nbigL_" + x).un(1).bc([128, 16, 128]), op=ALU.add)
        if _A():
            P.op("scalar", "activation", [DT[:]], [M1[:]], out=DT[:], in_=M1[:], func=AF.Exp)
        def head_gen(h, b):
            pz = slot_ps[b]
            tpc_ps, ss_ps, tk_ps, kk_ps, lp_ps, lu_ps, lx_ps = pz['tpc'], pz['ss'], pz['tk'], pz['kk'], pz['lp'], pz['lu'], pz['lx']
            wT_ps, vn_ps, qk_ps, o_ps = pz['wT'], pz['vn'], pz['qk'], pz['o']
            sn_ps = sn_bank[h % 2] if samp else pz['sn']
            if not samp:
                ev = ext[b][:, :, 0:131]
                cur = ev[:, :, 3:131]
                P.op("gpsimd", "tensor_copy", [ext[b][:]], [prev3[:]], out=ev[:, :, 0:3], in_=prev3[:, h, :, :])
                yield
                for p3 in range(3):
                    r0 = p3 * 2048 + h * 128
                    P.dma("sync", ev[:, p3, 3:131], s_qkv_fm[r0:r0 + 128, tok])
                    yield
                P.op("gpsimd", "tensor_copy", [prev3[:]], [ext[b][:]], out=prev3[:, h, :, :], in_=ev[:, :, 128:131])
                yield
                taps = [ev[:, :, i:i + 128] for i in range(4)]
                accv = cacc[b][:]
            else:
                ev = ext[b][:].re("p a (s w) -> p a s w", w=11)
                for p3 in range(3):
                    r0 = p3 * 2048 + h * 128
                    P.dma("sync", ev[:, p3, :, 3:11], s_qkv_fm[r0:r0 + 128, tok].re("p (s w) -> p s w", w=8))
                    yield
                    P.op("tensor", "transpose", [tpc_ps[:]], [sct[:], ident], out=tpc_ps[:], in_=sct[:, r0:r0 + 128],
                         identity=ident[0:48, 0:48])
                    yield
                    P.op("vector", "tensor_copy", [ext[b][:]], [tpc_ps[:]], out=ev[:, p3, :, 0:3],
                         in_=tpc_ps[:].re("p (s r) -> p s r", r=3))
                    yield
                taps = None
                accv = cacc[b][:].re("p a (s w) -> p a s w", w=8)
            for p3 in range(3):
                blk = p3 * 16 + h
                if not samp:
                    tp = [ev[:, p3, i:i + 128] for i in range(4)]
                    av = cacc[b][:, p3, :]
                else:
                    tp = [ev[:, p3, :, i:i + 8] for i in range(4)]
                    av = accv[:, p3, :, :]
                P.op("gpsimd", "tensor_scalar", [cacc[b][:]], [ext[b][:], cw[:]], out=av, in0=tp[0], scalar1=cw[:, blk, 0:1],
                     scalar2=None, op0=ALU.mult)
                yield
                for i in range(1, 4):
                    P.op("vector", "scalar_tensor_tensor", [cacc[b][:]], [ext[b][:], cw[:], cacc[b][:]], out=av, in0=tp[i],
                         scalar=cw[:, blk, i:i + 1], in1=av, op0=ALU.mult, op1=ALU.add)
                    yield
            P.op("scalar", "activation", [act[b][:]], [cacc[b][:]], out=act[b][:], in_=cacc[b][:], func=AF.Silu)
            yield
            P.op("gpsimd", "tensor_tensor", [sq[b][:]], [act[b][:]], out=sq[b][:], in0=act[b][:, 0:2, :], in1=act[b][:, 0:2, :], op=ALU.mult)
            yield
            P.op("tensor", "matmul", [ss_ps[:]], [sq[b][:], ones], ss_ps[:], ones, sq[b][:].re("p a t -> p (a t)"), start=True, stop=True)
            yield
            P.op("scalar", "activation", [rn[b][:]], [ss_ps[:], eps_t[:]], out=rn[b][:].re("p a t -> p (a t)"), in_=ss_ps[:], func=AF.Sqrt,
                 bias=eps_t[:], scale=1.0)
            yield
            P.op("vector", "reciprocal", [rn[b][:]], [rn[b][:]], out=rn[b][:], in_=rn[b][:])
            yield
            P.op("vector", "scalar_tensor_tensor", [qn[b][:]], [act[b][:], rn[b][:]], out=qn[b][:], in0=act[b][:, 0, :], scalar=128.0 ** -0.5,
                 in1=rn[b][:, 0, :], op0=ALU.mult, op1=ALU.mult)
            yield
            P.op("vector", "tensor_tensor", [knf[b][:]], [act[b][:], rn[b][:]], out=knf[b][:], in0=act[b][:, 1, :], in1=rn[b][:, 1, :], op=ALU.mult)
            yield
            P.op("gpsimd", "tensor_copy", [kn[b][:]], [knf[b][:]], out=kn[b][:], in_=knf[b][:])
            yield
            P.op("gpsimd", "tensor_tensor", [qg[b][:]], [qn[b][:], egB[:]], out=qg[b][:], in0=qn[b][:], in1=egB[:, h, :], op=ALU.mult)
            yield
            P.op("tensor", "transpose", [tk_ps[:]], [knf[b][:], ident], out=tk_ps[:, 0, :], in_=knf[b][:], identity=ident)
            yield
            P.op("tensor", "transpose", [tk_ps[:]], [act[b][:], ident], out=tk_ps[:, 1, :], in_=act[b][:, 2, :], identity=ident, same_ok=True)
            yield
            P.op("vector", "tensor_scalar", [rhsw[b][:]], [tk_ps[:], bq[:]], out=rhsw[b][:], in0=tk_ps[:, 0, :], scalar1=bq[:, h:h + 1],
                 scalar2=None, op0=ALU.mult)
            yield
            P.op("vector", "tensor_scalar", [kend[b][:]], [tk_ps[:], egsuf[:]], out=kend[b][:], in0=tk_ps[:, 0, :], scalar1=egsuf[:, h:h + 1],
                 scalar2=None, op0=ALU.mult)
            yield
            P.op("vector", "tensor_scalar", [rhsv[b][:]], [tk_ps[:], beta[:]], out=rhsv[b][:], in0=tk_ps[:, 1, :], scalar1=beta[:, h:h + 1],
                 scalar2=None, op0=ALU.mult)
            yield
            P.op("tensor", "matmul", [kk_ps[:]], [kn[b][:]], kk_ps[:], kn[b][:], kn[b][:], start=True, stop=True)
            yield
            P.op("vector", "scalar_tensor_tensor", [mm[b][:]], [kk_ps[:], beta[:], Ds[:]], out=mm[b][:], in0=kk_ps[:], scalar=beta[:, h:h + 1],
                 in1=Ds[:, h, :], op0=ALU.mult, op1=ALU.mult)
            yield
            P.op("gpsimd", "tensor_tensor", [Cl[b][:]], [mm[b][:], C("lmask")], out=Cl[b][:, 0:nlev, :],
                 in0=mm[b][:].un(1).bc([128, nlev, 128]), in1=C("lmask").re("p (l f) -> p l f", l=NLEV)[:, 0:nlev, :], op=ALU.mult)
            yield
            P.op("vector", "tensor_copy", [U[b][:]], [ident], out=U[b][:], in_=ident)
            yield
            P.op("gpsimd", "tensor_copy", [X[b][:]], [ident], out=X[b][:], in_=ident)
            yield
            for l in range(nlev):
                P.op("tensor", "matmul", [lp_ps[:]], [Cl[b][:], U[b][:]], lp_ps[:], Cl[b][:, l, :], U[b][:], start=True, stop=True)
                yield
                P.op("scalar", "copy", [Psb[b][:]], [lp_ps[:]], out=Psb[b][:], in_=lp_ps[:])
                yield
                P.op("tensor", "matmul", [lu_ps[:]], [X[b][:], Psb[b][:]], lu_ps[:], X[b][:], Psb[b][:], start=True, stop=True)
                yield
                P.op("vector", "tensor_tensor", [U[b][:]], [U[b][:], lu_ps[:]], out=U[b][:], in0=U[b][:], in1=lu_ps[:], op=ALU.subtract)
                yield
                if l < nlev - 1:
                    P.op("tensor", "transpose", [lx_ps[:]], [U[b][:], ident], out=lx_ps[:], in_=U[b][:], identity=ident)
                    yield
                    P.op("scalar", "copy", [X[b][:]], [lx_ps[:]], out=X[b][:], in_=lx_ps[:])
                    yield
            P.op("scalar", "copy", [Ubf[b][:]], [U[b][:]], out=Ubf[b][:], in_=U[b][:])
            yield
            P.op("tensor", "matmul", [wT_ps[:]], [rhsw[b][:], Ubf[b][:]], wT_ps[:], rhsw[b][:], Ubf[b][:], start=True, stop=True)
            yield
            P.op("scalar", "mul", [nwT[b][:]], [wT_ps[:]], out=nwT[b][:], in_=wT_ps[:], mul=-1.0)
            yield
            P.op("tensor", "matmul", [vn_ps[:]], [Ubf[b][:], rhsv[b][:]], vn_ps[:], Ubf[b][:], rhsv[b][:], start=True, stop=False)
            yield
            if not samp:
                P.op("tensor", "matmul", [vn_ps[:]], [nwT[b][:], Sbf[h][:]], vn_ps[:], nwT[b][:], Sbf[h][:], start=False, stop=True, same_ok=True)
                yield
            else:
                P.dma("sync", S0h[b][:], sgdn_d[:, h].re("s k v -> k s v"))
                yield
                P.op("gpsimd", "tensor_copy", [S0hb[b][:]], [S0h[b][:]], out=S0hb[b][:], in_=S0h[b][:])
                yield
                segF3 = C("segF").re("p (s t) -> p s t", s=NSEQ)
                P.op("vector", "tensor_tensor", [nwTm[:]], [nwT[b][:], C("segF")], out=nwTm[:], in0=nwT[b][:].un(1).bc([128, NSEQ, 128]),
                     in1=segF3, op=ALU.mult)
                yield
                P.op("gpsimd", "tensor_tensor", [qgm[:]], [qg[b][:], C("segF")], out=qgm[:], in0=qg[b][:].un(1).bc([128, NSEQ, 128]),
                     in1=segF3, op=ALU.mult)
                yield
                P.op("gpsimd", "tensor_tensor", [kendm[:]], [kend[b][:], C("segP")], out=kendm[:], in0=kend[b][:].un(1).bc([128, NSEQ, 128]),
                     in1=C("segP").un(2).bc([128, NSEQ, 128]), op=ALU.mult)
                yield
                for s in range(NSEQ):
                    P.op("tensor", "matmul", [vn_ps[:]], [nwTm[:], S0hb[b][:]], vn_ps[:], nwTm[:, s, :], S0hb[b][:, s, :],
                         start=False, stop=(s == NSEQ - 1), same_ok=True)
                    yield
            P.op("scalar", "copy", [vnew[b][:]], [vn_ps[:]], out=vnew[b][:], in_=vn_ps[:])
            yield
            P.op("tensor", "matmul", [qk_ps[:]], [kn[b][:], qn[b][:]], qk_ps[:], kn[b][:], qn[b][:], start=True, stop=True)
            yield
            P.op("vector", "tensor_tensor", [qkT[b][:]], [qk_ps[:], DT[:]], out=qkT[b][:], in0=qk_ps[:], in1=DT[:, h, :], op=ALU.mult)
            yield
            P.op("tensor", "matmul", [o_ps[:]], [qkT[b][:], vnew[b][:]], o_ps[:], qkT[b][:], vnew[b][:], start=True, stop=False)
            yield
            if not samp:
                P.op("tensor", "matmul", [o_ps[:]], [qg[b][:], Sbf[h][:]], o_ps[:], qg[b][:], Sbf[h][:], start=False, stop=True, same_ok=True)
                yield
            else:
                for s in range(NSEQ):
                    P.op("tensor", "matmul", [o_ps[:]], [qgm[:], S0hb[b][:]], o_ps[:], qgm[:, s, :], S0hb[b][:, s, :],
                         start=False, stop=(s == NSEQ - 1), same_ok=True)
                    yield
            if not samp:
                P.op("tensor", "matmul", [sn_ps[:]], [kend[b][:], vnew[b][:]], sn_ps[:, 0, :], kend[b][:], vnew[b][:], start=True, stop=True)
                yield
                P.op("vector", "scalar_tensor_tensor", [S[h][:]], [S[h][:], egl[:], sn_ps[:]], out=S[h][:], in0=S[h][:],
                     scalar=egl[:, 0, h:h + 1], in1=sn_ps[:, 0, :], op0=ALU.mult, op1=ALU.add)
                yield
                P.op("scalar", "copy", [Sbf[h][:]], [S[h][:]], out=Sbf[h][:], in_=S[h][:])
                yield
            else:
                for s4 in range(4):
                    for j in range(4):
                        s = s4 * 4 + j
                        P.op("tensor", "matmul", [sn_ps[:]], [kendm[:], vnew[b][:]], sn_ps[:, j, :], kendm[:, s, :], vnew[b][:],
                             start=True, stop=True, same_ok=(j > 0))
                        yield
                    for j in range(4):
                        s = s4 * 4 + j
                        P.op("vector", "scalar_tensor_tensor", [S1h[b][:]], [S0h[b][:], egl[:], sn_ps[:]], out=S1h[b][:, s, :],
                             in0=S0h[b][:, s, :], scalar=egl[:, s, h:h + 1], in1=sn_ps[:, j, :], op0=ALU.mult, op1=ALU.add)
                        yield
                P.dma("sync", gdns_d[:, h].re("s k v -> k s v"), S1h[b][:], semkey=S1h[b])
                yield
            P.op("scalar", "activation", [junk[:], ssq[b][:]], [o_ps[:]], out=junk[:], in_=o_ps[:], func=AF.Square, accum_out=ssq[b][:])
            yield
            P.op("scalar", "activation", [rstd[b][:]], [ssq[b][:], eps_t[:]], out=rstd[b][:], in_=ssq[b][:], func=AF.Sqrt,
                 bias=eps_t[:], scale=1.0 / 128)
            yield
            P.op("vector", "reciprocal", [rstd[b][:]], [rstd[b][:]], out=rstd[b][:], in_=rstd[b][:])
            yield
            P.op("vector", "scalar_tensor_tensor", [on[b][:]], [o_ps[:], rstd[b][:], gnB[:]], out=on[b][:], in0=o_ps[:],
                 scalar=rstd[b][:, 0:1], in1=gnB[:], op0=ALU.mult, op1=ALU.mult)
            yield
            P.op("gpsimd", "tensor_tensor", [ogall[:]], [on[b][:], gsb[:]], out=ogall[:, h * 128:(h + 1) * 128], in0=on[b][:],
                 in1=gsb[:, h * 128:(h + 1) * 128], op=ALU.mult)
            yield

        GI = 1 if samp else NB
        for h0_ in range(0, G_H, GI):
            gens = [head_gen(h, h - h0_) for h in range(h0_, min(G_H, h0_ + GI))]
            while gens:
                alive = []
                for g_ in gens:
                    try:
                        next(g_)
                        alive.append(g_)
                    except StopIteration:
                        pass
                gens = alive
        if t == NTP - 1:
            for hh in range(G_H):
                P.dma("sync", gdnp_d[hh], S[hh][:], semkey=S[hh])
        for g8 in (range(2) if G_H > 0 else []):
            for j in range(8):
                hh = g8 * 8 + j
                P.op("tensor", "transpose", [tp_ps[:]], [ogall[:], ident_bf[:]], out=tp_ps[:, j, :], in_=ogall[:, hh * 128:(hh + 1) * 128],
                     identity=ident_bf[:], same_ok=(j > 0))
            P.op("vector", "tensor_copy", [oTs[g8][:]], [tp_ps[:]], out=oTs[g8][:], in_=tp_ps[:])
            P.dma("sync", s_obT[g8 * 1024:(g8 + 1) * 1024, tok].re("(j p) t -> p j t", p=128), oTs[g8][:], semkey=oTs[g8])
    P.pop()


def _phase_post(P, C, L):
    d = L
    P.scope_name = "p5_post"
    s_oaT, s_obT, s_ma, s_mb, x_d, y_d, wba_d, wbb_d, wo_d, fng_d, ident_bf = (d[k] for k in
        ["s_oaT", "s_obT", "s_ma", "s_mb", "x_d", "y_d", "wba_d", "wbb_d", "wo_d", "fng_d", "ident_bf"])
    P.push()
    gfin = P.sb("gfin", [128, D], F32)
    P.dma("sync", gfin[:], V(fng_d.t.ap()[0:1, :].partition_broadcast(128)[:, 0, :], [fng_d]))
    eps_t = P.sb("eps5", [128, 1], F32)
    P.op("vector", "memset", [eps_t[:]], [], eps_t[:], EPS)
    pA = [P.ps("pp%d" % i, [128, 512], F32) for i in range(4)]
    tp_ps = P.ps("tp5", [128, 4, 128], BF16)
    halves = [list(range(0, 9)), list(range(9, NT))]
    wviews = {id(w): w.t.ap().rearrange("(ko ki) c -> ki ko c", ki=128) for w in (wba_d, wbb_d, wo_d)}

    def wload(dst, w, c0, n):
        for q4 in range(4):
            P.dma("gpsimd", dst[:, q4 * 4:(q4 + 1) * 4, 0:n], V(wviews[id(w)][:, q4 * 4:(q4 + 1) * 4, c0:c0 + n], [w]))
    for hi, tiles in enumerate(halves):
        nt = len(tiles)
        sfx = "_h%d" % hi
        t0 = tiles[0] * 128
        ntok = nt * 128
        P.push()
        mT = P.sb("mT" + sfx, [128, KT, 1152], BF16)
        P.push()
        oaT = P.sb("oaT" + sfx, [128, KT, 1152], BF16)
        obT = P.sb("obT" + sfx, [128, KT, 1152], BF16)
        for q4 in range(4):
            ks = slice(q4 * 4, q4 * 4 + 4)
            P.dma("sync", oaT[:, ks, 0:ntok], s_oaT[q4 * 512:(q4 + 1) * 512, t0:t0 + ntok].re("(k p) t -> p k t", p=128))
            P.dma("sync", obT[:, ks, 0:ntok], s_obT[q4 * 512:(q4 + 1) * 512, t0:t0 + ntok].re("(k p) t -> p k t", p=128))
        wa = P.sb("wa" + sfx, [128, KT, 512], BF16)
        wb = P.sb("wb" + sfx, [128, KT, 512], BF16)
        mat = [P.sb("mat%d" % i + sfx, [128, 512], F32) for i in range(2)]
        mbt = [P.sb("mbt%d" % i + sfx, [128, 512], F32) for i in range(2)]
        t1 = [P.sb("t1_%d" % i + sfx, [128, 512], F32) for i in range(2)]
        t2 = [P.sb("t2_%d" % i + sfx, [128, 512], F32) for i in range(2)]
        mg = [P.sb("mg%d" % i + sfx, [128, 512], BF16) for i in range(2)]
        it = 0
        for j in range(4):
            wload(wa, wba_d, j * 512, 512)
            wload(wb, wbb_d, j * 512, 512)
            for ti, t in enumerate(tiles):
                b = it % 2
                it += 1
                ya, yb = pA[2 * b], pA[2 * b + 1]
                lt = slice(ti * 128, (ti + 1) * 128)
                for k in range(KT):
                    P.op("tensor", "matmul", [ya[:]], [oaT[:], wa[:]], ya[:], oaT[:, k, lt], wa[:, k, :], start=(k == 0), stop=(k == KT - 1), same_ok=True)
                for k in range(KT):
                    P.op("tensor", "matmul", [yb[:]], [obT[:], wb[:]], yb[:], obT[:, k, lt], wb[:, k, :], start=(k == 0), stop=(k == KT - 1), same_ok=True)
                P.dma("sync", mat[b][:], s_ma[t * 128:(t + 1) * 128, j * 512:(j + 1) * 512])
                P.dma("sync", mbt[b][:], s_mb[t * 128:(t + 1) * 128, j * 512:(j + 1) * 512])
                P.op("scalar", "activation", [mat[b][:]], [mat[b][:]], out=mat[b][:], in_=mat[b][:], func=AF.Sigmoid)
                P.op("scalar", "activation", [mbt[b][:]], [mbt[b][:]], out=mbt[b][:], in_=mbt[b][:], func=AF.Sigmoid)
                P.op("vector", "tensor_tensor", [t1[b][:]], [ya[:], mat[b][:]], out=t1[b][:], in0=ya[:], in1=mat[b][:], op=ALU.mult)
                P.op("vector", "tensor_tensor", [t2[b][:]], [yb[:], mbt[b][:]], out=t2[b][:], in0=yb[:], in1=mbt[b][:], op=ALU.mult)
                P.op("gpsimd", "tensor_tensor", [mg[b][:]], [t1[b][:], t2[b][:]], out=mg[b][:], in0=t1[b][:], in1=t2[b][:], op=ALU.add)
                for c in range(4):
                    P.op("tensor", "transpose", [tp_ps[:]], [mg[b][:], ident_bf[:]], out=tp_ps[:, c, :], in_=mg[b][:, c * 128:(c + 1) * 128],
                         identity=ident_bf[:], same_ok=(c > 0))
                P.op("scalar", "copy", [mT[:]], [tp_ps[:]], out=mT[:, 4 * j:4 * j + 4, lt], in_=tp_ps[:])
        P.pop()
        P.push()
        wo = P.sb("wo" + sfx, [128, KT, D], BF16)
        for j in range(4):
            for q4 in range(4):
                P.dma("gpsimd", wo[:, q4 * 4:(q4 + 1) * 4, j * 512:(j + 1) * 512],
                      V(wviews[id(wo_d)][:, q4 * 4:(q4 + 1) * 4, j * 512:(j + 1) * 512], [wo_d]))
        xt = [P.sb("x5_%d" % i + sfx, [128, D], F32) for i in range(2)]
        res = [P.sb("res%d" % i + sfx, [128, D], F32) for i in range(2)]
        yo = [P.sb("yo%d" % i + sfx, [128, D], F32) for i in range(2)]
        junk = P.sb("junk5" + sfx, [128, D], BF16)
        ssq = [P.sb("ssq5_%d" % i + sfx, [128, 1], F32) for i in range(2)]
        for ti, t in enumerate(tiles):
            b = ti % 2
            lt = slice(ti * 128, (ti + 1) * 128)
            P.dma("sync", xt[b][:], x_d[t * 128:(t + 1) * 128, :])
            for j in range(4):
                ps = pA[j]
                for k in range(KT):
                    P.op("tensor", "matmul", [ps[:]], [mT[:], wo[:]], ps[:], mT[:, k, lt], wo[:, k, j * 512:(j + 1) * 512],
                         start=(k == 0), stop=(k == KT - 1), same_ok=True)
                P.op("vector", "tensor_tensor", [res[b][:]], [ps[:], xt[b][:]], out=res[b][:, j * 512:(j + 1) * 512], in0=ps[:],
                     in1=xt[b][:, j * 512:(j + 1) * 512], op=ALU.add)
            P.op("scalar", "activation", [junk[:], ssq[b][:]], [res[b][:]], out=junk[:], in_=res[b][:], func=AF.Square, accum_out=ssq[b][:])
            P.op("scalar", "activation", [ssq[b][:]], [ssq[b][:], eps_t[:]], out=ssq[b][:], in_=ssq[b][:], func=AF.Sqrt, bias=eps_t[:], scale=1.0 / D)
            P.op("vector", "reciprocal", [ssq[b][:]], [ssq[b][:]], out=ssq[b][:], in_=ssq[b][:])
            P.op("vector", "scalar_tensor_tensor", [yo[b][:]], [res[b][:], ssq[b][:], gfin[:]], out=yo[b][:], in0=res[b][:], scalar=ssq[b][:, 0:1],
                 in1=gfin[:], op0=ALU.mult, op1=ALU.mult)
            P.dma("sync", y_d[t * 128:(t + 1) * 128, :], yo[b][:], semkey=yo[b])
        P.pop()
        P.pop()
    P.pop()

def _core_inputs(c, inp):
    b = c // 2
    xs = inp["x_sample"][16 * c:16 * c + 16].reshape(128, D)
    x = np.concatenate([inp["x_prompt"][b], xs], axis=0)
    w2ext = np.concatenate([inp["w_alpha2"][0], inp["b_alpha"][0][None, :]], axis=0)
    cwT = np.ascontiguousarray(inp["conv_w"][0].T.reshape(48, 128, 4).transpose(1, 0, 2))
    return {
        "x": np.ascontiguousarray(x, dtype=np.float32),
        "w_in": np.ascontiguousarray(inp["w_in"][0]),
        "w2ext": np.ascontiguousarray(w2ext),
        "cwT": cwT,
        "a_log": np.ascontiguousarray(inp["a_log"][0][None, :]),
        "dt_bias": np.ascontiguousarray(inp["dt_bias"][0][None, :]),
        "ln_in_g": np.ascontiguousarray(inp["ln_in_g"][0][None, :]),
        "final_norm_g": np.ascontiguousarray(inp["final_norm_g"][None, :]),
        "gla_norm_g": np.ascontiguousarray(inp["gla_norm_g"][0][None, :]),
        "gdn_norm_g": np.ascontiguousarray(inp["gdn_norm_g"][0][None, :]),
        "w_br_a": np.ascontiguousarray(inp["w_br_a"][0]),
        "w_br_b": np.ascontiguousarray(inp["w_br_b"][0]),
        "w_out": np.ascontiguousarray(inp["w_out"][0]),
        "state_gla": np.ascontiguousarray(inp["state_gla"][0, 16 * c:16 * c + 16]),
        "state_gdn": np.ascontiguousarray(inp["state_gdn"][0, 16 * c:16 * c + 16]),
        "state_conv": np.ascontiguousarray(inp["state_conv"][0, 16 * c:16 * c + 16]),
        "consts": _CONST,
    }


def kernel(**inputs):
    inp = {k: np.asarray(v) for k, v in inputs.items()}
    nc = build_nc()
    in_maps = [_core_inputs(c, inp) for c in range(8)]
    res = run_bass_kernel_spmd(nc, in_maps, core_ids=list(range(8)))
    r = res.results
    y_prompt = np.stack([r[2 * b]["y"][:NTP * 128] for b in range(4)], axis=0)
    y_sample = np.concatenate([r[c]["y"][NTP * 128:].reshape(16, 8, D) for c in range(8)], axis=0)
    gla_p = np.stack([r[2 * b]["gla_p"] for b in range(4)], axis=0)[None]
    gdn_p = np.stack([r[2 * b]["gdn_p"] for b in range(4)], axis=0)[None]
    conv_p = np.stack([r[2 * b]["conv_p"] for b in range(4)], axis=0)[None]
    gla_s = np.concatenate([r[c]["gla_s"] for c in range(8)], axis=0)[None]
    gdn_s = np.concatenate([r[c]["gdn_s"] for c in range(8)], axis=0)[None]
    conv_s = np.concatenate([r[c]["conv_s"] for c in range(8)], axis=0)[None]
    return (y_prompt.astype(np.float32), y_sample.astype(np.float32), gla_p.astype(np.float32),
            gdn_p.astype(np.float32), conv_p.astype(np.float32), gla_s.astype(np.float32),
            gdn_s.astype(np.float32), conv_s.astype(np.float32))
```

```python
import contextlib
import os
import numpy as np
import concourse.bass as bass
import concourse.mybir as mybir
from concourse.bass_utils import run_bass_kernel_spmd

F32 = mybir.dt.float32
BF16 = mybir.dt.bfloat16
AF = mybir.ActivationFunctionType
ALU = mybir.AluOpType

D = 2048
KT = 16
NTP = 16
NT = 17
TOK = NT * 128
NSEQ = 16
DIN = 18480
C_QA, C_KA, C_VA, C_LR, C_GA, C_QKV, C_BD, C_GB, C_MA, C_MB = 0, 1024, 2048, 4096, 4112, 6160, 12304, 12336, 14384, 16432
EPS = 1e-6
BIG = 3.0e4
NLEV = 7
ATTACH_WAIT = os.environ.get("ATTACH_WAIT", "1") == "1"


class Bank:
    def __init__(self):
        self.reads = {}


class Buf:
    def __init__(self, t, name, bank=None):
        self.t = t
        self.name = name
        self.last_write = None
        self.reads = []
        self.bank = bank

    def __getitem__(self, idx):
        return V(self.t[idx], [self])

    def ap(self):
        return V(self.t.ap() if hasattr(self.t, "ap") and callable(getattr(self.t, "ap")) else self.t, [self])


class SubBuf:
    def __init__(self, parent, ap, name):
        self.parent = parent
        self.t = ap
        self.name = name

    def __getitem__(self, idx):
        return V(self.t[idx], [self.parent])


class V:
    def __init__(self, ap, keys):
        self.ap = ap
        self.keys = keys

    def __getitem__(self, idx):
        return V(self.ap[idx], self.keys)

    def re(self, pattern, **kw):
        return V(self.ap.rearrange(pattern, **kw), self.keys)

    def bc(self, shape):
        return V(self.ap.to_broadcast(list(shape)), self.keys)

    def un(self, axis):
        return V(self.ap.unsqueeze(axis), self.keys)

    def bitcast(self, dt):
        return V(self.ap.bitcast(dt), self.keys)


def _ap(x):
    return x.ap if isinstance(x, V) else x


class Prog:
    def __init__(self, nc):
        self.nc = nc
        self.stack = contextlib.ExitStack()
        self.engs = ["tensor", "vector", "scalar", "gpsimd", "sync"]
        self.ops = {e: [] for e in self.engs}
        self.sem = {}
        self.cnt = {e: 0 for e in self.engs}
        self.seen = {e: {} for e in self.engs}
        for e in self.engs:
            self.sem[e] = self.stack.enter_context(nc.semaphore("s_" + e))
        self.dma_sems = {}
        self.n_wait = 0
        self.n_ops = 0
        self.scopes = [self.stack]
        self.all_ops = {e: [] for e in self.engs}

    def push(self):
        self.scopes.append(contextlib.ExitStack())

    def pop(self):
        self.flush()
        self.scopes.pop().close()

    def sb(self, name, shape, dt):
        t = self.scopes[-1].enter_context(self.nc.sbuf_tensor(name, list(shape), dt))
        return Buf(t, name)

    def ps(self, name, shape, dt=F32):
        t = self.scopes[-1].enter_context(self.nc.psum_tensor(name, list(shape), dt))
        return Buf(t, name, bank=Bank())

    def dram(self, name, shape, dt, kind="Internal"):
        t = self.nc.dram_tensor(name, list(shape), dt, kind=kind)
        return Buf(t, name)

    def _collect(self, eng, ins, outs, same_ok):
        need = {}

        def add(sig):
            if sig is None:
                return
            sem, val, seng = sig
            if seng == eng and same_ok:
                return
            k = id(sem)
            if k not in need or need[k][1] < val:
                need[k] = (sem, val)
        for v in ins:
            if isinstance(v, V):
                for key in v.keys:
                    add(key.last_write)
                    if key.bank is not None:
                        for oe, sig in key.bank.reads.items():
                            if oe != eng:
                                add(sig)
        for v in outs:
            if isinstance(v, V):
                for key in v.keys:
                    add(key.last_write)
                    for r in key.reads:
                        add(r)
        waits = []
        for k, (sem, val) in need.items():
            if self.seen[eng].get(k, 0) >= val:
                continue
            self.seen[eng][k] = val
            waits.append((sem, val))
        return waits

    def _record(self, sig, ins, outs):
        for v in ins:
            if isinstance(v, V):
                for key in v.keys:
                    if key.bank is not None and sig[2] != "tensor":
                        key.bank.reads[sig[2]] = sig
                    key.reads.append(sig)
                    if len(key.reads) > 64:
                        best = {}
                        for s in key.reads:
                            if id(s[0]) not in best or best[id(s[0])][1] < s[1]:
                                best[id(s[0])] = s
                        key.reads = list(best.values())
        for v in outs:
            if isinstance(v, V):
                for key in v.keys:
                    key.last_write = sig
                    key.reads = []

    def op(self, eng, method, outs, ins, *args, same_ok=False, **kw):
        waits = self._collect(eng, ins, outs, same_ok)
        self.cnt[eng] += 1
        sig = (self.sem[eng], self.cnt[eng], eng)
        a2 = [_ap(a) for a in args]
        k2 = {k: _ap(v) for k, v in kw.items()}
        self.ops[eng].append((waits, method, a2, k2, (self.sem[eng], 1)))
        self.n_wait += len(waits)
        self.n_ops += 1
        self._record(sig, ins, outs)

    def dma(self, q, out, in_, semkey=None, **kw):
        waits = self._collect(q, [in_], [out], False)
        if semkey is None:
            semkey = out.keys[0]
        k = id(semkey)
        if k not in self.dma_sems:
            s = self.stack.enter_context(self.nc.semaphore("d_" + semkey.name))
            self.dma_sems[k] = [s, 0]
        ent = self.dma_sems[k]
        ent[1] += 16
        sig = (ent[0], ent[1], "dma")
        self.ops[q].append((waits, "dma_start", [], dict(out=_ap(out), in_=_ap(in_), **kw), (ent[0], 16)))
        self.n_wait += len(waits)
        self.n_ops += 1
        self._record(sig, [in_], [out])

    def final_wait(self, eng, bufs):
        need = {}
        for b in bufs:
            for sig in [b.last_write] + b.reads:
                if sig is None:
                    continue
                sem, val, _ = sig
                k = id(sem)
                if k not in need or need[k][1] < val:
                    need[k] = (sem, val)
        self.ops[eng].append((list(need.values()), None, None, None, None))

    def audit(self):
        final = {id(self.sem[e]): (self.cnt[e], "eng:" + e) for e in self.engs}
        for k, (sm, c) in self.dma_sems.items():
            final[id(sm)] = (c, "dma:" + sm.name)
        bad = 0
        mx = 0
        for e in self.engs:
            for (waits, method, a, k, inc) in self.all_ops[e]:
                for (sm, val) in waits:
                    mx = max(mx, val)
                    if id(sm) not in final or final[id(sm)][0] < val:
                        bad += 1
                        if bad < 10:
                            print("AUDIT: unsatisfiable wait on", e, getattr(sm, "name", sm), val, final.get(id(sm)))
        print("AUDIT: bad waits", bad, "max wait value", mx, "engine counts", dict(self.cnt),
              "max dma sem", max([c for (_, c) in self.dma_sems.values()] + [0]), "n dma sems", len(self.dma_sems))
        return bad

    def flush(self):
        for e in self.engs:
            self.all_ops[e].extend(self.ops[e])
        if any(self.ops[e] for e in self.engs):
            self.emit()
        self.ops = {e: [] for e in self.engs}

    def emit(self):
        nc = self.nc
        self.n_blocks = getattr(self, "n_blocks", 0) + 1
        with nc.named_scope("blk%02d_%s" % (self.n_blocks, getattr(self, "scope_name", "x"))), nc.Block() as block:
            def mk(ename):
                def body(e):
                    for (waits, method, a, k, inc) in self.ops[ename]:
                        attach = None
                        if ATTACH_WAIT and method is not None and method != "dma_start" and waits:
                            attach = waits[-1]
                            waits = waits[:-1]
                        for (sem, val) in waits:
                            e.wait_ge(sem, val)
                        if method is None:
                            continue
                        ins = getattr(e, method)(*a, **k)
                        if attach is not None:
                            ins._wait_ge(attach[0], attach[1])
                        ins.then_inc(inc[0], inc[1])
                return body
            block.tensor(mk("tensor"))
            block.vector(mk("vector"))
            block.scalar(mk("scalar"))
            block.gpsimd(mk("gpsimd"))
            block.sync(mk("sync"))

    def close(self):
        self.stack.close()


def _const_table():
    idx = np.arange(128)
    P_, F_ = np.meshgrid(idx, idx, indexing="ij")
    segp = np.zeros_like(P_)
    segs_p, segs_f = P_ // 8, F_ // 8
    c = {}
    c["ident"] = (P_ == F_).astype(np.float32)
    c["ones"] = np.ones((128, 128), np.float32)
    c["notI"] = (P_ != F_).astype(np.float32)
    for x, same in (("p", np.ones_like(P_, bool)), ("s", segs_p == segs_f)):
        c["triT_" + x] = ((P_ <= F_) & same).astype(np.float32)
        c["sufT_" + x] = ((P_ > F_) & same).astype(np.float32)
        c["bigU_" + x] = np.where((F_ <= P_) & same, 0.0, BIG).astype(np.float32)
        c["nbigL_" + x] = np.where((F_ >= P_) & same, 0.0, -BIG).astype(np.float32)
        c["bigUs_" + x] = np.where((F_ < P_) & same, 0.0, BIG).astype(np.float32)
    lm = np.zeros((128, NLEV, 128), np.float32)
    for l in range(NLEV):
        s = 1 << l
        lm[:, l, :] = ((P_ // (2 * s) == F_ // (2 * s)) & (P_ % (2 * s) >= s) & (F_ % (2 * s) < s))
    c["lmask"] = lm.reshape(128, NLEV * 128)
    segF = np.zeros((128, NSEQ, 128), np.float32)
    for s in range(NSEQ):
        segF[:, s, 8 * s:8 * s + 8] = 1.0
    c["segF"] = segF.reshape(128, NSEQ * 128)
    c["segP"] = (idx[:, None] // 8 == np.arange(NSEQ)[None, :]).astype(np.float32)
    c["selL_s"] = (idx[:, None] == (8 * np.arange(NSEQ)[None, :] + 7)).astype(np.float32)
    c["selL_p"] = (idx[:, None] == 127).astype(np.float32)
    offs = {}
    cols = []
    o = 0
    for k, v in c.items():
        offs[k] = (o, v.shape[1])
        cols.append(v)
        o += v.shape[1]
    return np.ascontiguousarray(np.concatenate(cols, axis=1)), offs


_CONST, _COFF = _const_table()
NCONST = _CONST.shape[1]


def build_nc(upto=99, debug=False):
    nc = bass.Bass("TRN2", target_bir_lowering=False)
    P = Prog(nc)

    def din(name, shape, dt=F32):
        return Buf(nc.dram_tensor(name, list(shape), dt, kind="ExternalInput"), name)

    def dout(name, shape, dt=F32):
        return Buf(nc.dram_tensor(name, list(shape), dt, kind="ExternalOutput"), name)

    def dscr(name, shape, dt=F32):
        if debug:
            return dout(name, shape, dt)
        return P.dram(name, shape, dt)

    x_d = din("x", [TOK, D])
    w_in_d = din("w_in", [D, DIN])
    w2e_d = din("w2ext", [17, 1024])
    cwT_d = din("cwT", [128, 48, 4])
    alog_d = din("a_log", [1, 16])
    dtb_d = din("dt_bias", [1, 16])
    lng_d = din("ln_in_g", [1, D])
    fng_d = din("final_norm_g", [1, D])
    gna_d = din("gla_norm_g", [1, 512])
    gnb_d = din("gdn_norm_g", [1, 128])
    wba_d = din("w_br_a", [D, D])
    wbb_d = din("w_br_b", [D, D])
    wo_d = din("w_out", [D, D])
    sgla_d = din("state_gla", [NSEQ, 4, 256, 512])
    sgdn_d = din("state_gdn", [NSEQ, 16, 128, 128])
    sconv_d = din("state_conv", [NSEQ, 3, 6144])
    const_d = din("consts", [128, NCONST])

    y_d = dout("y", [TOK, D])
    glap_d = dout("gla_p", [4, 256, 512])
    gdnp_d = dout("gdn_p", [16, 128, 128])
    convp_d = dout("conv_p", [3, 6144])
    glas_d = dout("gla_s", [NSEQ, 4, 256, 512])
    gdns_d = dout("gdn_s", [NSEQ, 16, 128, 128])
    convs_d = dout("conv_s", [NSEQ, 3, 6144])
    outs_all = [y_d, glap_d, gdnp_d, convp_d, glas_d, gdns_d, convs_d]

    s_ka_tm = dscr("s_ka_tm", [TOK, 1024])
    s_va = dscr("s_va", [TOK, 2048], BF16)
    s_ga = dscr("s_ga", [TOK, 2048])
    s_bd = dscr("s_bd", [TOK, 32])
    s_gb = dscr("s_gb", [TOK, 2048])
    s_ma = dscr("s_ma", [TOK, 2048])
    s_mb = dscr("s_mb", [TOK, 2048])
    s_qkv_tm = dscr("s_qkv_tm", [256, 6144])
    s_qa_fm = dscr("s_qa_fm", [1024, TOK])
    s_ka_fm = dscr("s_ka_fm", [1024, TOK])
    s_lr_fm = dscr("s_lr_fm", [16, TOK])
    s_qkv_fm = dscr("s_qkv_fm", [6144, TOK])
    s_oaT = dscr("s_oaT", [2048, TOK], BF16)
    s_obT = dscr("s_obT", [2048, TOK], BF16)
    dbg = [s_ka_tm, s_va, s_ga, s_bd, s_gb, s_ma, s_mb, s_qkv_tm, s_qa_fm, s_ka_fm, s_lr_fm, s_qkv_fm, s_oaT, s_obT]

    cst = P.sb("cst", [128, NCONST], F32)
    P.dma("sync", cst[:], const_d[:])

    def C(name):
        o, n = _COFF[name]
        return cst[:, o:o + n]

    ident_bf = P.sb("ident_bf", [128, 128], BF16)
    P.op("vector", "tensor_copy", [ident_bf[:]], [cst[:]], out=ident_bf[:], in_=C("ident"))

    P.scope_name = "p12_gemm"
    P.push()
    gB = P.sb("gB", [128, D], F32)
    def bload(dst, src, n):
        P.dma("sync", dst, V(src.t.ap()[0:1, 0:n].partition_broadcast(128)[:, 0, :], [src]))

    bload(gB[:], lng_d, D)

    psA = [P.ps("psA%d" % i, [128, 512], F32) for i in range(4)]
    psT = [P.ps("psT%d" % i, [128, 1024], BF16) for i in range(2)]
    psM = [P.ps("psM%d" % i, [128, 512], F32) for i in range(2)]

    hT = P.sb("hT", [128, KT, TOK], BF16)
    xt = [P.sb("xt%d" % i, [128, D], F32) for i in range(2)]
    hb = [P.sb("hb%d" % i, [128, D], BF16) for i in range(2)]
    junk = P.sb("junk", [128, D], BF16)
    ss = [P.sb("ss%d" % i, [128, 1], F32) for i in range(2)]
    rs = [P.sb("rs%d" % i, [128, 1], F32) for i in range(2)]
    eps_t = P.sb("eps_t", [128, 1], F32)
    P.op("vector", "memset", [eps_t[:]], [], eps_t[:], EPS)
    for t in range(NT):
        b = t % 2
        P.dma("sync", xt[b][:], x_d[t * 128:(t + 1) * 128, :])
        P.op("scalar", "activation", [junk[:], ss[b][:]], [xt[b][:]], out=junk[:], in_=xt[b][:], func=AF.Square,
             accum_out=ss[b][:])
        P.op("scalar", "activation", [rs[b][:]], [ss[b][:], eps_t[:]], out=rs[b][:], in_=ss[b][:], func=AF.Sqrt,
             bias=eps_t[:], scale=1.0 / D)
        P.op("vector", "reciprocal", [rs[b][:]], [rs[b][:]], out=rs[b][:], in_=rs[b][:])
        P.op("vector", "scalar_tensor_tensor", [hb[b][:]], [xt[b][:], rs[b][:], gB[:]], out=hb[b][:], in0=xt[b][:],
             scalar=rs[b][:, 0:1], in1=gB[:], op0=ALU.mult, op1=ALU.mult)
        for g in range(2):
            pt = psT[g]
            for j in range(8):
                k = g * 8 + j
                P.op("tensor", "transpose", [pt[:]], [hb[b][:], ident_bf[:]], out=pt[:, j * 128:(j + 1) * 128],
                     in_=hb[b][:, k * 128:(k + 1) * 128], identity=ident_bf[:], same_ok=True)
            dst = hT[:, g * 8:(g + 1) * 8, t * 128:(t + 1) * 128]
            src = pt[:].re("p (a b) -> p a b", a=8)
            if g == 0:
                P.op("vector", "tensor_copy", [dst], [pt[:]], out=dst, in_=src)
            else:
                P.op("scalar", "copy", [dst], [pt[:]], out=dst, in_=src)

    wblk = [P.sb("wblk%d" % i, [128, KT, 512], BF16) for i in range(2)]
    stg = [P.sb("stg%d" % i, [128, 512], F32) for i in range(4)]
    stgb = [P.sb("stgb%d" % i, [128, 512], BF16) for i in range(2)]
    all_tiles = list(range(NT))
    groups = [
        (C_QA, 1024, s_qa_fm, None, None, False),
        (C_KA, 1024, s_ka_fm, s_ka_tm, all_tiles, False),
        (C_VA, 2048, None, s_va, all_tiles, True),
        (C_LR, 16, s_lr_fm, None, None, False),
        (C_GA, 2048, None, s_ga, all_tiles, False),
        (C_QKV, 6144, s_qkv_fm, s_qkv_tm, [NTP - 1, NT - 1], False),
        (C_BD, 32, None, s_bd, all_tiles, False),
        (C_GB, 2048, None, s_gb, all_tiles, False),
        (C_MA, 2048, None, s_ma, all_tiles, False),
        (C_MB, 2048, None, s_mb, all_tiles, False),
    ]
    blocks = []
    for (c0, n, fm, tm, tiles, tbf) in groups:
        for o in range(0, n, 512):
            blocks.append((c0 + o, min(512, n - o), o, fm, tm, tiles, tbf))
    w_view = w_in_d.t.ap().rearrange("(ko ki) c -> ki ko c", ki=128)
    cnt = {"ps": 0, "stg": 0, "stgb": 0, "ev": 0}

    def evac(dst, src):
        cnt["ev"] += 1
        if cnt["ev"] % 2:
            P.op("vector", "tensor_copy", [dst], [src], out=dst, in_=src)
        else:
            P.op("scalar", "copy", [dst], [src], out=dst, in_=src)

    tok_groups = [(0, 512), (512, 512), (1024, 512), (1536, 512), (2048, 128)]
    if upto >= 2:
        for bi, (cg, n, off, fm, tm, tiles, tbf) in enumerate(blocks):
            wb = wblk[bi % 2]
            for q4 in range(4):
                P.dma("gpsimd", wb[:, q4 * 4:(q4 + 1) * 4, 0:n], V(w_view[:, q4 * 4:(q4 + 1) * 4, cg:cg + n], [w_in_d]))
            if fm is not None:
                for sbo in range(0, n, 128):
                    m = min(128, n - sbo)
                    for (t0, tn) in tok_groups:
                        ps = psA[cnt["ps"] % 4]
                        cnt["ps"] += 1
                        for k in range(KT):
                            P.op("tensor", "matmul", [ps[:]], [wb[:], hT[:]], ps[0:m, 0:tn], wb[:, k, sbo:sbo + m],
                                 hT[:, k, t0:t0 + tn], start=(k == 0), stop=(k == KT - 1), same_ok=True)
                        st = stg[cnt["stg"] % 4]
                        cnt["stg"] += 1
                        evac(st[0:m, 0:tn], ps[0:m, 0:tn])
                        P.dma("sync", fm[off + sbo:off + sbo + m, t0:t0 + tn], st[0:m, 0:tn], semkey=st)
            if tm is not None:
                for t in tiles:
                    ps = psA[cnt["ps"] % 4]
                    cnt["ps"] += 1
                    for k in range(KT):
                        P.op("tensor", "matmul", [ps[:]], [wb[:], hT[:]], ps[:, 0:n], hT[:, k, t * 128:(t + 1) * 128],
                             wb[:, k, 0:n], start=(k == 0), stop=(k == KT - 1), same_ok=True)
                    if tbf:
                        st = stgb[cnt["stgb"] % 2]
                        cnt["stgb"] += 1
                    else:
                        st = stg[cnt["stg"] % 4]
                        cnt["stg"] += 1
                    evac(st[:, 0:n], ps[:, 0:n])
                    if tm is s_qkv_tm:
                        r0 = 0 if t == NTP - 1 else 128
                    else:
                        r0 = t * 128
                    P.dma("sync", tm[r0:r0 + 128, off:off + n], st[:, 0:n], semkey=st)

    P.pop()

    if upto >= 3:
        _phase_gla(P, C, locals())
    if upto >= 4:
        _phase_gdn(P, C, locals())
    if upto >= 5:
        _phase_post(P, C, locals())
    final = list(outs_all) + (dbg if debug else [])
    P.final_wait("sync", final + dbg)
    P.flush()
    P.close()
    print("ops", P.n_ops, "waits", P.n_wait)
    P.audit()
    return nc


def _phase_gla(P, C, L):
    d = L
    P.scope_name = "p3_gla"
    s_qa_fm, s_ka_fm, s_ka_tm, s_va, s_ga, s_lr_fm, s_oaT = (d[k] for k in
        ["s_qa_fm", "s_ka_fm", "s_ka_tm", "s_va", "s_ga", "s_lr_fm", "s_oaT"])
    w2e_d, gna_d, sgla_d, glap_d, glas_d, ident_bf = (d[k] for k in
        ["w2e_d", "gna_d", "sgla_d", "glap_d", "glas_d", "ident_bf"])
    P.push()
    w2e = P.sb("w2e", [17, 1024], F32)
    P.dma("sync", w2e[:], w2e_d[:])
    gnA = P.sb("gnA", [128, 512], F32)
    P.dma("sync", gnA[:], V(gna_d.t.ap()[0:1, :].partition_broadcast(128)[:, 0, :], [gna_d]))
    eps_t = P.sb("eps3", [128, 1], F32)
    P.op("vector", "memset", [eps_t[:]], [], eps_t[:], EPS)
    S = [P.sb("S%d" % h, [128, 2, 512], F32) for h in range(4)]
    Sbf = [P.sb("Sbf%d" % h, [128, 2, 512], BF16) for h in range(4)]
    for h in range(4):
        P.op("gpsimd", "memset", [S[h][:]], [], S[h][:], 0.0)
        P.op("gpsimd", "memset", [Sbf[h][:]], [], Sbf[h][:], 0.0)
    NB = 3
    lrT = [P.sb("lrT%d" % i, [17, 128], F32) for i in range(NB)]
    for i in range(NB):
        P.op("vector", "memset", [lrT[i][:]], [], lrT[i][:], 1.0)

    def mk(name, shape, dt):
        return [P.sb("%s%d" % (name, i), shape, dt) for i in range(NB)]
    qT = mk("qT", [128, 2, 128], F32); kT = mk("kT", [128, 2, 128], F32)
    ktok = mk("ktok", [128, 256], F32); vv = mk("vv", [128, 512], BF16); gate = mk("gate", [128, 512], F32)
    ee = mk("ee", [128, 256], F32); sp = mk("sp", [128, 256], F32)
    ebT = mk("ebT", [128, 2, 128], F32); enbT = mk("enbT", [128, 2, 128], F32); esuf = mk("esuf", [128, 256], F32)
    qd = mk("qd", [128, 2, 128], BF16); kd = mk("kd", [128, 2, 128], BF16); kend = mk("kend", [128, 256], BF16)
    attT = mk("attT", [128, 128], BF16)
    on = mk("on", [128, 512], F32); gs = mk("gs", [128, 512], F32); og = mk("og", [128, 512], BF16)
    oT = mk("oT", [128, 4, 128], BF16)
    junk = P.sb("junk3", [128, 512], BF16)
    ssq = mk("ssq", [128, 1], F32); rstd = mk("rstd", [128, 1], F32)
    qm = P.sb("qm", [128, 2, NSEQ, 128], BF16)
    kendm = P.sb("kendm", [128, NSEQ, 256], BF16)
    S0 = [P.sb("S0_%d" % i, [128, 2, 512], F32) for i in range(3)]
    S0b = [P.sb("S0b_%d" % i, [128, 2, 512], BF16) for i in range(3)]
    S1 = [P.sb("S1_%d" % i, [128, 2, 512], F32) for i in range(3)]
    gbanks = [P.ps("abank%d" % i, [128, 512], F32) for i in range(2 * NB)]

    def sub(bk, c0, n, name, pat=None, **kw):
        ap = gbanks[bk].t[:, c0:c0 + n]
        if pat:
            ap = ap.rearrange(pat, **kw)
        return SubBuf(gbanks[bk], ap, name)
    slot_ps = []
    for i in range(NB):
        A, B = 2 * i, 2 * i + 1
        slot_ps.append(dict(z=sub(A, 0, 256, "z%d" % i), bT=sub(A, 256, 256, "bT%d" % i, "p (a b) -> p a b", a=2),
                            bsuf=sub(A, 0, 256, "bsuf%d" % i), att=sub(A, 256, 128, "att%d" % i),
                            sn=sub(A, 0, 512, "sn%d" % i), o=sub(B, 0, 512, "o%d" % i)))
    tp_ps = P.ps("tp_ps", [128, 4, 128], BF16)
    it = 0
    nsn = 0
    for t in range(NT):
        samp = (t == NT - 1)
        x = "s" if samp else "p"
        triT, sufT = C("triT_" + x), C("sufT_" + x)
        tok = slice(t * 128, (t + 1) * 128)
        def gla_gen(h, b):
            pz = slot_ps[b]
            z_ps, bT_ps, bsuf_ps, att_ps, o_ps = pz['z'], pz['bT'], pz['bsuf'], pz['att'], pz['o']
            sn_ps = [pz['sn'], pz['sn']]
            P.dma("sync", lrT[b][0:16, :], s_lr_fm[:, tok])
            yield
            P.dma("sync", qT[b][:], s_qa_fm[h * 256:(h + 1) * 256, tok].re("(c p) t -> p c t", p=128))
            yield
            P.dma("sync", kT[b][:], s_ka_fm[h * 256:(h + 1) * 256, tok].re("(c p) t -> p c t", p=128))
            yield
            P.dma("sync", ktok[b][:], s_ka_tm[tok, h * 256:(h + 1) * 256])
            yield
            P.dma("sync", vv[b][:], s_va[tok, h * 512:(h + 1) * 512])
            yield
            P.dma("sync", gate[b][:], s_ga[tok, h * 512:(h + 1) * 512])
            yield
            P.op("tensor", "matmul", [z_ps[:]], [lrT[b][:], w2e[:]], z_ps[:], lrT[b][:], w2e[:, h * 256:(h + 1) * 256],
                 start=True, stop=True)
            yield
            P.op("scalar", "activation", [ee[b][:]], [z_ps[:]], out=ee[b][:], in_=z_ps[:], func=AF.Exp, scale=-1.0)
            yield
            P.op("scalar", "activation", [sp[b][:]], [ee[b][:]], out=sp[b][:], in_=ee[b][:], func=AF.Ln, bias=1.0, scale=1.0)
            yield
            for c in range(2):
                P.op("tensor", "matmul", [bT_ps[:]], [sp[b][:], triT], bT_ps[:, c, :], sp[b][:, c * 128:(c + 1) * 128], triT,
                     start=True, stop=True, same_ok=True)
                yield
            P.op("tensor", "matmul", [bsuf_ps[:]], [sp[b][:], sufT], bsuf_ps[:], sufT, sp[b][:], start=True, stop=True)
            yield
            P.op("scalar", "activation", [ebT[b][:]], [bT_ps[:]], out=ebT[b][:], in_=bT_ps[:], func=AF.Exp, scale=-1.0 / 16)
            yield
            P.op("scalar", "activation", [enbT[b][:]], [bT_ps[:]], out=enbT[b][:], in_=bT_ps[:], func=AF.Exp, scale=1.0 / 16)
            yield
            P.op("scalar", "activation", [esuf[b][:]], [bsuf_ps[:]], out=esuf[b][:], in_=bsuf_ps[:], func=AF.Exp, scale=-1.0 / 16)
            yield
            P.op("vector", "scalar_tensor_tensor", [qd[b][:]], [qT[b][:], ebT[b][:]], out=qd[b][:], in0=qT[b][:], scalar=0.0625,
                 in1=ebT[b][:], op0=ALU.mult, op1=ALU.mult)
            yield
            P.op("vector", "tensor_tensor", [kd[b][:]], [kT[b][:], enbT[b][:]], out=kd[b][:], in0=kT[b][:], in1=enbT[b][:], op=ALU.mult)
            yield
            P.op("gpsimd", "tensor_tensor", [kend[b][:]], [ktok[b][:], esuf[b][:]], out=kend[b][:], in0=ktok[b][:], in1=esuf[b][:],
                 op=ALU.mult)
            yield
            for c in range(2):
                P.op("tensor", "matmul", [att_ps[:]], [kd[b][:], qd[b][:]], att_ps[:], kd[b][:, c, :], qd[b][:, c, :],
                     start=(c == 0), stop=(c == 1), same_ok=True)
                yield
            P.op("vector", "tensor_tensor", [attT[b][:]], [att_ps[:], triT], out=attT[b][:], in0=att_ps[:], in1=triT, op=ALU.mult)
            yield
            P.op("tensor", "matmul", [o_ps[:]], [attT[b][:], vv[b][:]], o_ps[:], attT[b][:], vv[b][:], start=True, stop=False)
            yield
            if not samp:
                for c in range(2):
                    P.op("tensor", "matmul", [o_ps[:]], [qd[b][:], Sbf[h][:]], o_ps[:], qd[b][:, c, :], Sbf[h][:, c, :],
                         start=False, stop=(c == 1), same_ok=True)
                    yield
                for c in range(2):
                    sn = sn_ps[0]
                    P.op("tensor", "matmul", [sn[:]], [kend[b][:], vv[b][:]], sn[:], kend[b][:, c * 128:(c + 1) * 128], vv[b][:],
                         start=True, stop=True)
                    yield
                    P.op("vector", "scalar_tensor_tensor", [S[h][:]], [S[h][:], ebT[b][:], sn[:]], out=S[h][:, c, :],
                         in0=S[h][:, c, :], scalar=ebT[b][:, c, 127:128], in1=sn[:], op0=ALU.mult, op1=ALU.add)
                    yield
                P.op("scalar", "copy", [Sbf[h][:]], [S[h][:]], out=Sbf[h][:], in_=S[h][:])
                yield
                if t == NTP - 1:
                    P.dma("sync", glap_d[h].re("(c p) v -> p c v", p=128), S[h][:], semkey=S[h])
                    yield
            else:
                for c in range(2):
                    P.op("vector", "tensor_tensor", [qm[:]], [qd[b][:], C("segF")], out=qm[:, c, :, :],
                         in0=qd[b][:, c, :].un(1).bc([128, NSEQ, 128]), in1=C("segF").re("p (s t) -> p s t", s=NSEQ), op=ALU.mult)
                    yield
                P.op("gpsimd", "tensor_tensor", [kendm[:]], [kend[b][:], C("segP")], out=kendm[:],
                     in0=kend[b][:].un(1).bc([128, NSEQ, 256]), in1=C("segP").un(2).bc([128, NSEQ, 256]), op=ALU.mult)
                yield
                for s in range(NSEQ):
                    r = s % 3
                    P.dma("sync", S0[r][:], sgla_d[s, h].re("(c p) v -> p c v", p=128))
                    yield
                    P.op("scalar", "copy", [S0b[r][:]], [S0[r][:]], out=S0b[r][:], in_=S0[r][:])
                    yield
                    for c in range(2):
                        P.op("tensor", "matmul", [o_ps[:]], [qm[:], S0b[r][:]], o_ps[:], qm[:, c, s, :], S0b[r][:, c, :],
                             start=False, stop=(s == NSEQ - 1 and c == 1), same_ok=True)
                        yield
                    for c in range(2):
                        sn = sn_ps[0]
                        P.op("tensor", "matmul", [sn[:]], [kendm[:], vv[b][:]], sn[:], kendm[:, s, c * 128:(c + 1) * 128], vv[b][:],
                             start=True, stop=True)
                        yield
                        P.op("vector", "scalar_tensor_tensor", [S1[r][:]], [S0[r][:], ebT[b][:], sn[:]], out=S1[r][:, c, :],
                             in0=S0[r][:, c, :], scalar=ebT[b][:, c, 8 * s + 7:8 * s + 8], in1=sn[:], op0=ALU.mult, op1=ALU.add)
                        yield
                    P.dma("sync", glas_d[s, h].re("(c p) v -> p c v", p=128), S1[r][:], semkey=S1[r])
                    yield
            P.op("scalar", "activation", [junk[:], ssq[b][:]], [o_ps[:]], out=junk[:], in_=o_ps[:], func=AF.Square, accum_out=ssq[b][:])
            yield
            P.op("scalar", "activation", [rstd[b][:]], [ssq[b][:], eps_t[:]], out=rstd[b][:], in_=ssq[b][:], func=AF.Sqrt,
                 bias=eps_t[:], scale=1.0 / 512)
            yield
            P.op("vector", "reciprocal", [rstd[b][:]], [rstd[b][:]], out=rstd[b][:], in_=rstd[b][:])
            yield
            P.op("vector", "scalar_tensor_tensor", [on[b][:]], [o_ps[:], rstd[b][:], gnA[:]], out=on[b][:], in0=o_ps[:],
                 scalar=rstd[b][:, 0:1], in1=gnA[:], op0=ALU.mult, op1=ALU.mult)
            yield
            P.op("scalar", "activation", [gs[b][:]], [gate[b][:]], out=gs[b][:], in_=gate[b][:], func=AF.Silu)
            yield
            P.op("gpsimd", "tensor_tensor", [og[b][:]], [on[b][:], gs[b][:]], out=og[b][:], in0=on[b][:], in1=gs[b][:], op=ALU.mult)
            yield
            for j in range(4):
                P.op("tensor", "transpose", [tp_ps[:]], [og[b][:], ident_bf[:]], out=tp_ps[:, j, :], in_=og[b][:, j * 128:(j + 1) * 128],
                     identity=ident_bf[:], same_ok=True)
            P.op("vector", "tensor_copy", [oT[b][:]], [tp_ps[:]], out=oT[b][:], in_=tp_ps[:])
            yield
            P.dma("sync", s_oaT[h * 512:(h + 1) * 512, tok].re("(j p) t -> p j t", p=128), oT[b][:], semkey=oT[b])
            yield

        GI = 1 if samp else NB
        for h0_ in range(0, 4, GI):
            gens = [gla_gen(h, h - h0_) for h in range(h0_, min(4, h0_ + GI))]
            while gens:
                alive = []
                for g_ in gens:
                    try:
                        next(g_)
                        alive.append(g_)
                    except StopIteration:
                        pass
                gens = alive
    P.pop()


def _phase_gdn(P, C, L):
    d = L
    P.scope_name = "p4_gdn"
    s_qkv_fm, s_qkv_tm, s_bd, s_gb, s_obT = (d[k] for k in ["s_qkv_fm", "s_qkv_tm", "s_bd", "s_gb", "s_obT"])
    cwT_d, alog_d, dtb_d, gnb_d, sgdn_d, sconv_d, gdnp_d, gdns_d, convp_d, convs_d, ident_bf = (d[k] for k in
        ["cwT_d", "alog_d", "dtb_d", "gnb_d", "sgdn_d", "sconv_d", "gdnp_d", "gdns_d", "convp_d", "convs_d", "ident_bf"])
    P.push()

    def bl(name, src, n):
        t = P.sb(name, [128, n], F32)
        P.dma("sync", t[:], V(src.t.ap()[0:1, 0:n].partition_broadcast(128)[:, 0, :], [src]))
        return t
    gnB = bl("gnB", gnb_d, 128)
    alog_b = bl("alog_b", alog_d, 16)
    dtb_b = bl("dtb_b", dtb_d, 16)
    negA = P.sb("negA", [128, 16], F32)
    P.op("scalar", "activation", [negA[:]], [alog_b[:]], out=negA[:], in_=alog_b[:], func=AF.Exp)
    P.op("vector", "tensor_scalar", [negA[:]], [negA[:]], out=negA[:], in0=negA[:], scalar1=-1.0, scalar2=None, op0=ALU.mult)
    cw = P.sb("cw", [128, 48, 4], F32)
    P.dma("sync", cw[:], cwT_d[:])
    eps_t = P.sb("eps4", [128, 1], F32)
    P.op("vector", "memset", [eps_t[:]], [], eps_t[:], EPS)
    P.push()
    cvs = P.sb("cvs", [51, 6144], F32)
    P.dma("sync", cvs[48:51, :], s_qkv_tm[125:128, :])
    P.dma("sync", convp_d[:], cvs[48:51, :], semkey=cvs)
    for s in range(NSEQ):
        P.dma("sync", cvs[3 * s:3 * s + 3, :], s_qkv_tm[128 + 8 * s + 5:128 + 8 * s + 8, :])
    P.dma("sync", convs_d.ap().re("s r c -> (s r) c"), cvs[0:48, :], semkey=cvs)
    P.pop()
    sct = P.sb("sct", [48, 6144], F32)
    P.dma("sync", sct[:], sconv_d.ap().re("s r c -> (s r) c"))
    S = [P.sb("Sg%d" % i, [128, 128], F32) for i in range(16)]
    Sbf = [P.sb("Sgbf%d" % i, [128, 128], BF16) for i in range(16)]
    for i in range(16):
        P.op("gpsimd", "memset", [S[i][:]], [], S[i][:], 0.0)
        P.op("gpsimd", "memset", [Sbf[i][:]], [], Sbf[i][:], 0.0)
    prev3 = P.sb("prev3", [128, 16, 3, 3], F32)
    P.op("gpsimd", "memset", [prev3[:]], [], prev3[:], 0.0)
    bd = P.sb("bd", [128, 32], F32)
    beta = P.sb("beta", [128, 16], F32); tmp16 = P.sb("tmp16", [128, 16], F32); g = P.sb("g", [128, 16], F32)
    gh = P.sb("gh", [128, 16], F32); eg = P.sb("eg", [128, 16], F32); egsuf = P.sb("egsuf", [128, 16], F32)
    bq = P.sb("bq", [128, 16], F32)
    Rl = P.sb("Rl", [128, NSEQ, 16], F32); egl = P.sb("egl", [128, NSEQ, 16], F32)
    diag = P.sb("diag", [128, 16, 128], F32)
    M1 = diag
    Ds = P.sb("Ds", [128, 16, 128], F32); DT = P.sb("DT", [128, 16, 128], F32); egB = P.sb("egB", [128, 16, 128], F32)
    gbt = P.sb("gbt", [128, 2048], F32); gsb = gbt
    ogall = P.sb("ogall", [128, 2048], BF16)
    oTs = [P.sb("oTs%d" % i, [128, 8, 128], BF16) for i in range(2)]
    NB = int(os.environ.get("GDN_G", "4"))

    def mk(name, shape, dt):
        return [P.sb("%s%d" % (name, i), shape, dt) for i in range(NB)]
    ext = mk("ext", [128, 3, 176], F32)
    cacc = mk("cacc", [128, 3, 128], F32); act = mk("act", [128, 3, 128], F32)
    sq = mk("sq", [128, 2, 128], F32); rn = mk("rn", [128, 2, 128], F32)
    qn = mk("qn", [128, 128], BF16); qg = mk("qg", [128, 128], BF16); knf = mk("knf", [128, 128], F32); kn = mk("kn", [128, 128], BF16)
    rhsv = mk("rhsv", [128, 128], BF16); rhsw = mk("rhsw", [128, 128], BF16); kend = mk("kendg", [128, 128], BF16)
    mm = mk("mm", [128, 128], F32); Cl = mk("Cl", [128, NLEV, 128], F32)
    U = mk("U", [128, 128], F32); X = mk("X", [128, 128], F32); Psb = mk("Psb", [128, 128], F32); Ubf = mk("Ubf", [128, 128], BF16)
    nwT = mk("nwT", [128, 128], BF16); vnew = mk("vnew", [128, 128], BF16); qkT = mk("qkT", [128, 128], BF16)
    on = mk("ong", [128, 128], F32); ssq = mk("ssqg", [128, 1], F32); rstd = mk("rstdg", [128, 1], F32)
    junk = P.sb("junk4", [128, 128], BF16)
    nwTm = P.sb("nwTm", [128, NSEQ, 128], BF16); qgm = P.sb("qgm", [128, NSEQ, 128], BF16); kendm = P.sb("kendmg", [128, NSEQ, 128], BF16)
    S0h = [P.sb("S0h", [128, NSEQ, 128], F32)] * NB; S0hb = [P.sb("S0hb", [128, NSEQ, 128], BF16)] * NB
    S1h = [P.sb("S1h", [128, NSEQ, 128], F32)] * NB
    E1 = S1h[0]
    banks = [P.ps("gbank%d" % i, [128, 512], F32) for i in range(7)]

    def sub(bk, c0, n, name, pat=None, **kw):
        ap = banks[bk].t[:, c0:c0 + n]
        if pat:
            ap = ap.rearrange(pat, **kw)
        return SubBuf(banks[bk], ap, name)
    CB = NB
    ghB_half = [sub(CB, 0, 256, "ghB_ps0", "p (a b) -> p a b", a=2)] * 2
    sm_ps = sub(CB, 256, 32, "sm_ps")
    glB_ps = sub(CB, 288, 128, "glB_ps", "p (a b) -> p a b", b=16)
    SN = [NB + 1, NB + 2]
    sn_bank = [sub(SN[i], 0, 512, "sng_ps%d" % i, "p (a b) -> p a b", a=4) for i in range(2)]
    tp_ps = P.ps("tpg_ps", [128, 8, 128], BF16)
    slot_ps = []
    for i in range(NB):
        slot_ps.append(dict(
            ss=sub(i, 0, 256, "ss%d" % i), tk=sub(i, 256, 256, "tk%d" % i, "p (a b) -> p a b", a=2), tpc=sub(i, 0, 48, "tpc%d" % i),
            kk=sub(i, 0, 128, "kk%d" % i), lp=sub(i, 128, 128, "lp%d" % i), lu=sub(i, 256, 128, "lu%d" % i), lx=sub(i, 384, 128, "lx%d" % i),
            sn=sub(i, 0, 512, "sn%d" % i, "p (a b) -> p a b", a=4), wT=sub(i, 0, 128, "wT%d" % i), vn=sub(i, 128, 128, "vn%d" % i), qk=sub(i, 256, 128, "qk%d" % i), o=sub(i, 384, 128, "o%d" % i)))
    ident, ones = C("ident"), C("ones")
    it = 0
    G_T = [int(v) for v in os.environ.get("GDN_T", ",".join(str(i) for i in range(NT))).split(",") if v != ""]
    G_H = int(os.environ.get("GDN_H", "16"))
    G_A = int(os.environ.get("GDN_A", "1000"))
    for t in G_T:
        samp = (t == NT - 1)
        x = "s" if samp else "p"
        nseg = NSEQ if samp else 1
        nlev = 3 if samp else NLEV
        triT, sufT = C("triT_" + x), C("sufT_" + x)
        tok = slice(t * 128, (t + 1) * 128)
        _ac = [0]

        def _A():
            _ac[0] += 1
            return _ac[0] <= G_A
        if _A():
            P.dma("sync", bd[:], s_bd[tok, :])
        if _A():
            P.dma("sync", gbt[:], s_gb[tok, :])
        if _A():
            P.op("scalar", "activation", [gsb[:]], [gbt[:]], out=gsb[:], in_=gbt[:], func=AF.Silu)
        if _A():
            P.op("scalar", "activation", [beta[:]], [bd[:]], out=beta[:], in_=bd[:, 0:16], func=AF.Sigmoid)
        if _A():
            P.op("vector", "tensor_tensor", [tmp16[:]], [bd[:], dtb_b[:]], out=tmp16[:], in0=bd[:, 16:32], in1=dtb_b[:], op=ALU.add)
        if _A():
            P.op("scalar", "activation", [tmp16[:]], [tmp16[:]], out=tmp16[:], in_=tmp16[:], func=AF.Exp)
        if _A():
            P.op("scalar", "activation", [tmp16[:]], [tmp16[:]], out=tmp16[:], in_=tmp16[:], func=AF.Ln, bias=1.0, scale=1.0)
        if _A():
            P.op("vector", "tensor_tensor", [g[:]], [tmp16[:], negA[:]], out=g[:], in0=tmp16[:], in1=negA[:], op=ALU.mult)
        if _A():
            P.op("tensor", "matmul", [sm_ps[:]], [g[:], triT], sm_ps[:, 0:16], triT, g[:], start=True, stop=True)
        if _A():
            P.op("tensor", "matmul", [sm_ps[:]], [g[:], sufT], sm_ps[:, 16:32], sufT, g[:], start=True, stop=True, same_ok=True)
        if _A():
            P.op("vector", "tensor_copy", [gh[:]], [sm_ps[:]], out=gh[:], in_=sm_ps[:, 0:16])
        if _A():
            P.op("scalar", "activation", [eg[:]], [sm_ps[:]], out=eg[:], in_=sm_ps[:, 0:16], func=AF.Exp)
        if _A():
            P.op("scalar", "activation", [egsuf[:]], [sm_ps[:]], out=egsuf[:], in_=sm_ps[:, 16:32], func=AF.Exp)
        if _A():
            P.op("vector", "tensor_tensor", [bq[:]], [beta[:], eg[:]], out=bq[:], in0=beta[:], in1=eg[:], op=ALU.mult)
        selL = C("selL_" + x)
        if _A():
            P.op("vector", "tensor_tensor", [Rl[:]], [gh[:], selL], out=Rl[:, 0:nseg, :], in0=gh[:].un(1).bc([128, nseg, 16]),
                 in1=selL.un(2).bc([128, nseg, 16]), op=ALU.mult)
        for s0 in range(0, nseg, 8):
            sn_ = min(8, nseg - s0)
            P.op("tensor", "matmul", [glB_ps[:]], [Rl[:], ones], glB_ps[:, 0:sn_, :], ones, Rl[:, s0:s0 + sn_, :], start=True, stop=True)
            P.op("scalar", "activation", [egl[:]], [glB_ps[:]], out=egl[:, s0:s0 + sn_, :], in_=glB_ps[:, 0:sn_, :], func=AF.Exp)
        if _A():
            P.op("gpsimd", "tensor_tensor", [diag[:]], [gh[:], ident], out=diag[:], in0=ident.un(1).bc([128, 16, 128]),
                 in1=gh[:].un(2).bc([128, 16, 128]), op=ALU.mult)
        for q8 in range(8):
            hs = slice(q8 * 2, q8 * 2 + 2)
            gps = ghB_half[q8 % 2]
            if _A():
                P.op("tensor", "matmul", [gps[:]], [diag[:], ones], gps[:], ones, diag[:, hs, :], start=True, stop=True)
            if _A():
                P.op("vector", "tensor_tensor", [E1[:]], [gps[:], gh[:]], out=E1[:, hs, :], in0=gps[:],
                     in1=gh[:, hs].un(2).bc([128, 2, 128]), op=ALU.subtract)
            if _A():
                P.op("scalar", "activation", [egB[:]], [gps[:]], out=egB[:, hs, :], in_=gps[:], func=AF.Exp)
        if _A():
            P.op("gpsimd", "tensor_tensor", [M1[:]], [E1[:], C("bigUs_" + x)], out=M1[:], in0=E1[:],
                 in1=C("bigUs_" + x).un(1).bc([128, 16, 128]), op=ALU.add)
        if _A():
            P.op("scalar", "activation", [Ds[:]], [M1[:]], out=Ds[:], in_=M1[:], func=AF.Exp, scale=-1.0)
        if _A():
            P.op("gpsimd", "tensor_tensor", [M1[:]], [E1[:], C("nbigL_" + x)], out=M1[:], in0=E1[:],
                 in1=C("nbigL_" + x).un(1).bc([128, 16, 128]), op=ALU.add)
        if _A():
            P.op("scalar", "activation", [DT[:]], [M1[:]], out=DT[:], in_=M1[:], func=AF.Exp)
        def head_gen(h, b):
            pz = slot_ps[b]
            tpc_ps, ss_ps, tk_ps, kk_ps, lp_ps, lu_ps, lx_ps = pz['tpc'], pz['ss'], pz['tk'], pz['kk'], pz['lp'], pz['lu'], pz['lx']
            wT_ps, vn_ps, qk_ps, o_ps = pz['wT'], pz['vn'], pz['qk'], pz['o']
            sn_ps = sn_bank[h % 2] if samp else pz['sn']
            if not samp:
                ev = ext[b][:, :, 0:131]
                cur = ev[:, :, 3:131]
                P.op("gpsimd", "tensor_copy", [ext[b][:]], [prev3[:]], out=ev[:, :, 0:3], in_=prev3[:, h, :, :])
                yield
                for p3 in range(3):
                    r0 = p3 * 2048 + h * 128
                    P.dma("sync", ev[:, p3, 3:131], s_qkv_fm[r0:r0 + 128, tok])
                    yield
                P.op("gpsimd", "tensor_copy", [prev3[:]], [ext[b][:]], out=prev3[:, h, :, :], in_=ev[:, :, 128:131])
                yield
                taps = [ev[:, :, i:i + 128] for i in range(4)]
                accv = cacc[b][:]
            else:
                ev = ext[b][:].re("p a (s w) -> p a s w", w=11)
                for p3 in range(3):
                    r0 = p3 * 2048 + h * 128
                    P.dma("sync", ev[:, p3, :, 3:11], s_qkv_fm[r0:r0 + 128, tok].re("p (s w) -> p s w", w=8))
                    yield
                    P.op("tensor", "transpose", [tpc_ps[:]], [sct[:], ident], out=tpc_ps[:], in_=sct[:, r0:r0 + 128],
                         identity=ident[0:48, 0:48])
                    yield
                    P.op("vector", "tensor_copy", [ext[b][:]], [tpc_ps[:]], out=ev[:, p3, :, 0:3],
                         in_=tpc_ps[:].re("p (s r) -> p s r", r=3))
                    yield
                taps = None
                accv = cacc[b][:].re("p a (s w) -> p a s w", w=8)
            for p3 in range(3):
                blk = p3 * 16 + h
                if not samp:
                    tp = [ev[:, p3, i:i + 128] for i in range(4)]
                    av = cacc[b][:, p3, :]
                else:
                    tp = [ev[:, p3, :, i:i + 8] for i in range(4)]
                    av = accv[:, p3, :, :]
                P.op("gpsimd", "tensor_scalar", [cacc[b][:]], [ext[b][:], cw[:]], out=av, in0=tp[0], scalar1=cw[:, blk, 0:1],
                     scalar2=None, op0=ALU.mult)
                yield
                for i in range(1, 4):
                    P.op("vector", "scalar_tensor_tensor", [cacc[b][:]], [ext[b][:], cw[:], cacc[b][:]], out=av, in0=tp[i],
                         scalar=cw[:, blk, i:i + 1], in1=av, op0=ALU.mult, op1=ALU.add)
                    yield
            P.op("scalar", "activation", [act[b][:]], [cacc[b][:]], out=act[b][:], in_=cacc[b][:], func=AF.Silu)
            yield
            P.op("gpsimd", "tensor_tensor", [sq[b][:]], [act[b][:]], out=sq[b][:], in0=act[b][:, 0:2, :], in1=act[b][:, 0:2, :], op=ALU.mult)
            yield
            P.op("tensor", "matmul", [ss_ps[:]], [sq[b][:], ones], ss_ps[:], ones, sq[b][:].re("p a t -> p (a t)"), start=True, stop=True)
            yield
            P.op("scalar", "activation", [rn[b][:]], [ss_ps[:], eps_t[:]], out=rn[b][:].re("p a t -> p (a t)"), in_=ss_ps[:], func=AF.Sqrt,
                 bias=eps_t[:], scale=1.0)
            yield
            P.op("vector", "reciprocal", [rn[b][:]], [rn[b][:]], out=rn[b][:], in_=rn[b][:])
            yield
            P.op("vector", "scalar_tensor_tensor", [qn[b][:]], [act[b][:], rn[b][:]], out=qn[b][:], in0=act[b][:, 0, :], scalar=128.0 ** -0.5,
                 in1=rn[b][:, 0, :], op0=ALU.mult, op1=ALU.mult)
            yield
            P.op("vector", "tensor_tensor", [knf[b][:]], [act[b][:], rn[b][:]], out=knf[b][:], in0=act[b][:, 1, :], in1=rn[b][:, 1, :], op=ALU.mult)
            yield
            P.op("gpsimd", "tensor_copy", [kn[b][:]], [knf[b][:]], out=kn[b][:], in_=knf[b][:])
            yield
            P.op("gpsimd", "tensor_tensor", [qg[b][:]], [qn[b][:], egB[:]], out=qg[b][:], in0=qn[b][:], in1=egB[:, h, :], op=ALU.mult)
            yield
            P.op("tensor", "transpose", [tk_ps[:]], [knf[b][:], ident], out=tk_ps[:, 0, :], in_=knf[b][:], identity=ident)
            yield
            P.op("tensor", "transpose", [tk_ps[:]], [act[b][:], ident], out=tk_ps[:, 1, :], in_=act[b][:, 2, :], identity=ident, same_ok=True)
            yield
            P.op("vector", "tensor_scalar", [rhsw[b][:]], [tk_ps[:], bq[:]], out=rhsw[b][:], in0=tk_ps[:, 0, :], scalar1=bq[:, h:h + 1],
                 scalar2=None, op0=ALU.mult)
            yield
            P.op("vector", "tensor_scalar", [kend[b][:]], [tk_ps[:], egsuf[:]], out=kend[b][:], in0=tk_ps[:, 0, :], scalar1=egsuf[:, h:h + 1],
                 scalar2=None, op0=ALU.mult)
            yield
            P.op("vector", "tensor_scalar", [rhsv[b][:]], [tk_ps[:], beta[:]], out=rhsv[b][:], in0=tk_ps[:, 1, :], scalar1=beta[:, h:h + 1],
                 scalar2=None, op0=ALU.mult)
            yield
            P.op("tensor", "matmul", [kk_ps[:]], [kn[b][:]], kk_ps[:], kn[b][:], kn[b][:], start=True, stop=True)
            yield
            P.op("vector", "scalar_tensor_tensor", [mm[b][:]], [kk_ps[:], beta[:], Ds[:]], out=mm[b][:], in0=kk_ps[:], scalar=beta[:, h:h + 1],
                 in1=Ds[:, h, :], op0=ALU.mult, op1=ALU.mult)
            yield
            P.op("gpsimd", "tensor_tensor", [Cl[b][:]], [mm[b][:], C("lmask")], out=Cl[b][:, 0:nlev, :],
                 in0=mm[b][:].un(1).bc([128, nlev, 128]), in1=C("lmask").re("p (l f) -> p l f", l=NLEV)[:, 0:nlev, :], op=ALU.mult)
            yield
            P.op("gpsimd", "tensor_tensor", [X[b][:]], [Cl[b][:], ident], out=X[b][:], in0=ident, in1=Cl[b][:, 0, :], op=ALU.subtract)
            yield
            P.op("tensor", "transpose", [lx_ps[:]], [X[b][:], ident], out=lx_ps[:], in_=X[b][:], identity=ident)
            yield
            P.op("scalar", "copy", [U[b][:]], [lx_ps[:]], out=U[b][:], in_=lx_ps[:])
            yield
            for l in range(1, nlev):
                P.op("tensor", "matmul", [lp_ps[:]], [Cl[b][:], U[b][:]], lp_ps[:], Cl[b][:, l, :], U[b][:], start=True, stop=True)
                yield
                P.op("scalar", "copy", [Psb[b][:]], [lp_ps[:]], out=Psb[b][:], in_=lp_ps[:])
                yield
                P.op("tensor", "matmul", [lu_ps[:]], [X[b][:], Psb[b][:]], lu_ps[:], X[b][:], Psb[b][:], start=True, stop=True)
                yield
                P.op("vector", "tensor_tensor", [U[b][:]], [U[b][:], lu_ps[:]], out=U[b][:], in0=U[b][:], in1=lu_ps[:], op=ALU.subtract)
                yield
                if l < nlev - 1:
                    P.op("tensor", "transpose", [lx_ps[:]], [U[b][:], ident], out=lx_ps[:], in_=U[b][:], identity=ident)
                    yield
                    P.op("scalar", "copy", [X[b][:]], [lx_ps[:]], out=X[b][:], in_=lx_ps[:])
                    yield
            P.op("scalar", "copy", [Ubf[b][:]], [U[b][:]], out=Ubf[b][:], in_=U[b][:])
            yield
            P.op("tensor", "matmul", [wT_ps[:]], [rhsw[b][:], Ubf[b][:]], wT_ps[:], rhsw[b][:], Ubf[b][:], start=True, stop=True)
            yield
            P.op("scalar", "mul", [nwT[b][:]], [wT_ps[:]], out=nwT[b][:], in_=wT_ps[:], mul=-1.0)
            yield
            P.op("tensor", "matmul", [vn_ps[:]], [Ubf[b][:], rhsv[b][:]], vn_ps[:], Ubf[b][:], rhsv[b][:], start=True, stop=False)
            yield
            if not samp:
                P.op("tensor", "matmul", [vn_ps[:]], [nwT[b][:], Sbf[h][:]], vn_ps[:], nwT[b][:], Sbf[h][:], start=False, stop=True, same_ok=True)
                yield
            else:
                P.dma("sync", S0h[b][:], sgdn_d[:, h].re("s k v -> k s v"))
                yield
                P.op("gpsimd", "tensor_copy", [S0hb[b][:]], [S0h[b][:]], out=S0hb[b][:], in_=S0h[b][:])
                yield
                segF3 = C("segF").re("p (s t) -> p s t", s=NSEQ)
                P.op("vector", "tensor_tensor", [nwTm[:]], [nwT[b][:], C("segF")], out=nwTm[:], in0=nwT[b][:].un(1).bc([128, NSEQ, 128]),
                     in1=segF3, op=ALU.mult)
                yield
                P.op("gpsimd", "tensor_tensor", [qgm[:]], [qg[b][:], C("segF")], out=qgm[:], in0=qg[b][:].un(1).bc([128, NSEQ, 128]),
                     in1=segF3, op=ALU.mult)
                yield
                P.op("gpsimd", "tensor_tensor", [kendm[:]], [kend[b][:], C("segP")], out=kendm[:], in0=kend[b][:].un(1).bc([128, NSEQ, 128]),
                     in1=C("segP").un(2).bc([128, NSEQ, 128]), op=ALU.mult)
                yield
                for s in range(NSEQ):
                    P.op("tensor", "matmul", [vn_ps[:]], [nwTm[:], S0hb[b][:]], vn_ps[:], nwTm[:, s, :], S0hb[b][:, s, :],
                         start=False, stop=(s == NSEQ - 1), same_ok=True)
                    yield
            P.op("scalar", "copy", [vnew[b][:]], [vn_ps[:]], out=vnew[b][:], in_=vn_ps[:])
            yield
            P.op("tensor", "matmul", [qk_ps[:]], [kn[b][:], qn[b][:]], qk_ps[:], kn[b][:], qn[b][:], start=True, stop=True)
            yield
            P.op("vector", "tensor_tensor", [qkT[b][:]], [qk_ps[:], DT[:]], out=qkT[b][:], in0=qk_ps[:], in1=DT[:, h, :], op=ALU.mult)
            yield
            P.op("tensor", "matmul", [o_ps[:]], [qkT[b][:], vnew[b][:]], o_ps[:], qkT[b][:], vnew[b][:], start=True, stop=False)
            yield
            if not samp:
                P.op("tensor", "matmul", [o_ps[:]], [qg[b][:], Sbf[h][:]], o_ps[:], qg[b][:], Sbf[h][:], start=False, stop=True, same_ok=True)
                yield
            else:
                for s in range(NSEQ):
                    P.op("tensor", "matmul", [o_ps[:]], [qgm[:], S0hb[b][:]], o_ps[:], qgm[:, s, :], S0hb[b][:, s, :],
                         start=False, stop=(s == NSEQ - 1), same_ok=True)
                    yield
            if not samp:
                P.op("tensor", "matmul", [sn_ps[:]], [kend[b][:], vnew[b][:]], sn_ps[:, 0, :], kend[b][:], vnew[b][:], start=True, stop=True)
                yield
                P.op("vector", "scalar_tensor_tensor", [S[h][:]], [S[h][:], egl[:], sn_ps[:]], out=S[h][:], in0=S[h][:],
                     scalar=egl[:, 0, h:h + 1], in1=sn_ps[:, 0, :], op0=ALU.mult, op1=ALU.add)
                yield
                P.op("scalar", "copy", [Sbf[h][:]], [S[h][:]], out=Sbf[h][:], in_=S[h][:])
                yield
            else:
                for s4 in range(4):
                    for j in range(4):
                        s = s4 * 4 + j
                        P.op("tensor", "matmul", [sn_ps[:]], [kendm[:], vnew[b][:]], sn_ps[:, j, :], kendm[:, s, :], vnew[b][:],
                             start=True, stop=True, same_ok=(j > 0))
                        yield
                    for j in range(4):
                        s = s4 * 4 + j
                        P.op("vector", "scalar_tensor_tensor", [S1h[b][:]], [S0h[b][:], egl[:], sn_ps[:]], out=S1h[b][:, s, :],
                             in0=S0h[b][:, s, :], scalar=egl[:, s, h:h + 1], in1=sn_ps[:, j, :], op0=ALU.mult, op1=ALU.add)
                        yield
                P.dma("sync", gdns_d[:, h].re("s k v -> k s v"), S1h[b][:], semkey=S1h[b])
                yield
            P.op("scalar", "activation", [junk[:], ssq[b][:]], [o_ps[:]], out=junk[:], in_=o_ps[:], func=AF.Square, accum_out=ssq[b][:])
            yield
            P.op("scalar", "activation", [rstd[b][:]], [ssq[b][:], eps_t[:]], out=rstd[b][:], in_=ssq[b][:], func=AF.Sqrt,
                 bias=eps_t[:], scale=1.0 / 128)
            yield
            P.op("vector", "reciprocal", [rstd[b][:]], [rstd[b][:]], out=rstd[b][:], in_=rstd[b][:])
            yield
            P.op("vector", "scalar_tensor_tensor", [on[b][:]], [o_ps[:], rstd[b][:], gnB[:]], out=on[b][:], in0=o_ps[:],
                 scalar=rstd[b][:, 0:1], in1=gnB[:], op0=ALU.mult, op1=ALU.mult)
            yield
            P.op("gpsimd", "tensor_tensor", [ogall[:]], [on[b][:], gsb[:]], out=ogall[:, h * 128:(h + 1) * 128], in0=on[b][:],
                 in1=gsb[:, h * 128:(h + 1) * 128], op=ALU.mult)
            yield

        GI = 1 if samp else NB
        for h0_ in range(0, G_H, GI):
            gens = [head_gen(h, h - h0_) for h in range(h0_, min(G_H, h0_ + GI))]
            while gens:
                alive = []
                for g_ in gens:
                    try:
                        next(g_)
                        alive.append(g_)
                    except StopIteration:
                        pass
                gens = alive
        if t == NTP - 1:
            for hh in range(G_H):
                P.dma("sync", gdnp_d[hh], S[hh][:], semkey=S[hh])
        for g8 in (range(2) if G_H > 0 else []):
            for j in range(8):
                hh = g8 * 8 + j
                P.op("tensor", "transpose", [tp_ps[:]], [ogall[:], ident_bf[:]], out=tp_ps[:, j, :], in_=ogall[:, hh * 128:(hh + 1) * 128],
                     identity=ident_bf[:], same_ok=(j > 0))
            P.op("vector", "tensor_copy", [oTs[g8][:]], [tp_ps[:]], out=oTs[g8][:], in_=tp_ps[:])
            P.dma("sync", s_obT[g8 * 1024:(g8 + 1) * 1024, tok].re("(j p) t -> p j t", p=128), oTs[g8][:], semkey=oTs[g8])
    P.pop()


def _phase_post(P, C, L):
    d = L
    P.scope_name = "p5_post"
    s_oaT, s_obT, s_ma, s_mb, x_d, y_d, wba_d, wbb_d, wo_d, fng_d, ident_bf = (d[k] for k in
        ["s_oaT", "s_obT", "s_ma", "s_mb", "x_d", "y_d", "wba_d", "wbb_d", "wo_d", "fng_d", "ident_bf"])
    P.push()
    gfin = P.sb("gfin", [128, D], F32)
    P.dma("sync", gfin[:], V(fng_d.t.ap()[0:1, :].partition_broadcast(128)[:, 0, :], [fng_d]))
    eps_t = P.sb("eps5", [128, 1], F32)
    P.op("vector", "memset", [eps_t[:]], [], eps_t[:], EPS)
    pA = [P.ps("pp%d" % i, [128, 512], F32) for i in range(4)]
    tp_ps = P.ps("tp5", [128, 4, 128], BF16)
    halves = [list(range(0, 9)), list(range(9, NT))]
    wviews = {id(w): w.t.ap().rearrange("(ko ki) c -> ki ko c", ki=128) for w in (wba_d, wbb_d, wo_d)}

    def wload(dst, w, c0, n):
        for q4 in range(4):
            P.dma("gpsimd", dst[:, q4 * 4:(q4 + 1) * 4, 0:n], V(wviews[id(w)][:, q4 * 4:(q4 + 1) * 4, c0:c0 + n], [w]))
    for hi, tiles in enumerate(halves):
        nt = len(tiles)
        sfx = "_h%d" % hi
        t0 = tiles[0] * 128
        ntok = nt * 128
        P.push()
        mT = P.sb("mT" + sfx, [128, KT, 1152], BF16)
        P.push()
        oaT = P.sb("oaT" + sfx, [128, KT, 1152], BF16)
        obT = P.sb("obT" + sfx, [128, KT, 1152], BF16)
        for q4 in range(4):
            ks = slice(q4 * 4, q4 * 4 + 4)
            P.dma("sync", oaT[:, ks, 0:ntok], s_oaT[q4 * 512:(q4 + 1) * 512, t0:t0 + ntok].re("(k p) t -> p k t", p=128))
            P.dma("sync", obT[:, ks, 0:ntok], s_obT[q4 * 512:(q4 + 1) * 512, t0:t0 + ntok].re("(k p) t -> p k t", p=128))
        wa2 = [P.sb("wa%d" % i + sfx, [128, KT, 512], BF16) for i in range(2)]
        wb2 = [P.sb("wb%d" % i + sfx, [128, KT, 512], BF16) for i in range(2)]
        mat = [P.sb("mat%d" % i + sfx, [128, 512], F32) for i in range(2)]
        mbt = [P.sb("mbt%d" % i + sfx, [128, 512], F32) for i in range(2)]
        t1, t2 = mat, mbt
        mg = [P.sb("mg0" + sfx, [128, 512], BF16)] * 2
        it = 0
        wload(wa2[0], wba_d, 0, 512)
        wload(wb2[0], wbb_d, 0, 512)
        for j in range(4):
            wa, wb = wa2[j % 2], wb2[j % 2]
            if j + 1 < 4:
                wload(wa2[(j + 1) % 2], wba_d, (j + 1) * 512, 512)
                wload(wb2[(j + 1) % 2], wbb_d, (j + 1) * 512, 512)
            for ti, t in enumerate(tiles):
                b = it % 2
                it += 1
                ya, yb = pA[2 * b], pA[2 * b + 1]
                lt = slice(ti * 128, (ti + 1) * 128)
                for k in range(KT):
                    P.op("tensor", "matmul", [ya[:]], [oaT[:], wa[:]], ya[:], oaT[:, k, lt], wa[:, k, :], start=(k == 0), stop=(k == KT - 1), same_ok=True)
                for k in range(KT):
                    P.op("tensor", "matmul", [yb[:]], [obT[:], wb[:]], yb[:], obT[:, k, lt], wb[:, k, :], start=(k == 0), stop=(k == KT - 1), same_ok=True)
                P.dma("sync", mat[b][:], s_ma[t * 128:(t + 1) * 128, j * 512:(j + 1) * 512])
                P.dma("sync", mbt[b][:], s_mb[t * 128:(t + 1) * 128, j * 512:(j + 1) * 512])
                P.op("scalar", "activation", [mat[b][:]], [mat[b][:]], out=mat[b][:], in_=mat[b][:], func=AF.Sigmoid)
                P.op("scalar", "activation", [mbt[b][:]], [mbt[b][:]], out=mbt[b][:], in_=mbt[b][:], func=AF.Sigmoid)
                P.op("vector", "tensor_tensor", [t1[b][:]], [ya[:], mat[b][:]], out=t1[b][:], in0=ya[:], in1=mat[b][:], op=ALU.mult)
                P.op("vector", "tensor_tensor", [t2[b][:]], [yb[:], mbt[b][:]], out=t2[b][:], in0=yb[:], in1=mbt[b][:], op=ALU.mult)
                P.op("gpsimd", "tensor_tensor", [mg[b][:]], [t1[b][:], t2[b][:]], out=mg[b][:], in0=t1[b][:], in1=t2[b][:], op=ALU.add)
                for c in range(4):
                    P.op("tensor", "transpose", [tp_ps[:]], [mg[b][:], ident_bf[:]], out=tp_ps[:, c, :], in_=mg[b][:, c * 128:(c + 1) * 128],
                         identity=ident_bf[:], same_ok=(c > 0))
                P.op("scalar", "copy", [mT[:]], [tp_ps[:]], out=mT[:, 4 * j:4 * j + 4, lt], in_=tp_ps[:])
        P.pop()
        P.push()
        wo = P.sb("wo" + sfx, [128, KT, D], BF16)
        for j in range(4):
            for q4 in range(4):
                P.dma("gpsimd", wo[:, q4 * 4:(q4 + 1) * 4, j * 512:(j + 1) * 512],
                      V(wviews[id(wo_d)][:, q4 * 4:(q4 + 1) * 4, j * 512:(j + 1) * 512], [wo_d]))
        xt = [P.sb("x5_%d" % i + sfx, [128, D], F32) for i in range(2)]
        res = [P.sb("res%d" % i + sfx, [128, D], F32) for i in range(2)]
        yo = [P.sb("yo%d" % i + sfx, [128, D], F32) for i in range(2)]
        junk = P.sb("junk5" + sfx, [128, D], BF16)
        ssq = [P.sb("ssq5_%d" % i + sfx, [128, 1], F32) for i in range(2)]
        for ti, t in enumerate(tiles):
            b = ti % 2
            lt = slice(ti * 128, (ti + 1) * 128)
            P.dma("sync", xt[b][:], x_d[t * 128:(t + 1) * 128, :])
            for j in range(4):
                ps = pA[j]
                for k in range(KT):
                    P.op("tensor", "matmul", [ps[:]], [mT[:], wo[:]], ps[:], mT[:, k, lt], wo[:, k, j * 512:(j + 1) * 512],
                         start=(k == 0), stop=(k == KT - 1), same_ok=True)
                P.op("vector", "tensor_tensor", [res[b][:]], [ps[:], xt[b][:]], out=res[b][:, j * 512:(j + 1) * 512], in0=ps[:],
                     in1=xt[b][:, j * 512:(j + 1) * 512], op=ALU.add)
            P.op("scalar", "activation", [junk[:], ssq[b][:]], [res[b][:]], out=junk[:], in_=res[b][:], func=AF.Square, accum_out=ssq[b][:])
            P.op("scalar", "activation", [ssq[b][:]], [ssq[b][:], eps_t[:]], out=ssq[b][:], in_=ssq[b][:], func=AF.Sqrt, bias=eps_t[:], scale=1.0 / D)
            P.op("vector", "reciprocal", [ssq[b][:]], [ssq[b][:]], out=ssq[b][:], in_=ssq[b][:])
            P.op("vector", "scalar_tensor_tensor", [yo[b][:]], [res[b][:], ssq[b][:], gfin[:]], out=yo[b][:], in0=res[b][:], scalar=ssq[b][:, 0:1],
                 in1=gfin[:], op0=ALU.mult, op1=ALU.mult)
            P.dma("sync", y_d[t * 128:(t + 1) * 128, :], yo[b][:], semkey=yo[b])
        P.pop()
        P.pop()
    P.pop()

def _core_inputs(c, inp):
    b = c // 2
    xs = inp["x_sample"][16 * c:16 * c + 16].reshape(128, D)
    x = np.concatenate([inp["x_prompt"][b], xs], axis=0)
    w2ext = np.concatenate([inp["w_alpha2"][0], inp["b_alpha"][0][None, :]], axis=0)
    cwT = np.ascontiguousarray(inp["conv_w"][0].T.reshape(48, 128, 4).transpose(1, 0, 2))
    return {
        "x": np.ascontiguousarray(x, dtype=np.float32),
        "w_in": np.ascontiguousarray(inp["w_in"][0]),
        "w2ext": np.ascontiguousarray(w2ext),
        "cwT": cwT,
        "a_log": np.ascontiguousarray(inp["a_log"][0][None, :]),
        "dt_bias": np.ascontiguousarray(inp["dt_bias"][0][None, :]),
        "ln_in_g": np.ascontiguousarray(inp["ln_in_g"][0][None, :]),
        "final_norm_g": np.ascontiguousarray(inp["final_norm_g"][None, :]),
        "gla_norm_g": np.ascontiguousarray(inp["gla_norm_g"][0][None, :]),
        "gdn_norm_g": np.ascontiguousarray(inp["gdn_norm_g"][0][None, :]),
        "w_br_a": np.ascontiguousarray(inp["w_br_a"][0]),
        "w_br_b": np.ascontiguousarray(inp["w_br_b"][0]),
        "w_out": np.ascontiguousarray(inp["w_out"][0]),
        "state_gla": np.ascontiguousarray(inp["state_gla"][0, 16 * c:16 * c + 16]),
        "state_gdn": np.ascontiguousarray(inp["state_gdn"][0, 16 * c:16 * c + 16]),
        "state_conv": np.ascontiguousarray(inp["state_conv"][0, 16 * c:16 * c + 16]),
        "consts": _CONST,
    }


def kernel(**inputs):
    inp = {k: np.asarray(v) for k, v in inputs.items()}
    nc = build_nc()
    in_maps = [_core_inputs(c, inp) for c in range(8)]
    res = run_bass_kernel_spmd(nc, in_maps, core_ids=list(range(8)))
    r = res.results
    y_prompt = np.stack([r[2 * b]["y"][:NTP * 128] for b in range(4)], axis=0)
    y_sample = np.concatenate([r[c]["y"][NTP * 128:].reshape(16, 8, D) for c in range(8)], axis=0)
    gla_p = np.stack([r[2 * b]["gla_p"] for b in range(4)], axis=0)[None]
    gdn_p = np.stack([r[2 * b]["gdn_p"] for b in range(4)], axis=0)[None]
    conv_p = np.stack([r[2 * b]["conv_p"] for b in range(4)], axis=0)[None]
    gla_s = np.concatenate([r[c]["gla_s"] for c in range(8)], axis=0)[None]
    gdn_s = np.concatenate([r[c]["gdn_s"] for c in range(8)], axis=0)[None]
    conv_s = np.concatenate([r[c]["conv_s"] for c in range(8)], axis=0)[None]
    return (y_prompt.astype(np.float32), y_sample.astype(np.float32), gla_p.astype(np.float32),
            gdn_p.astype(np.float32), conv_p.astype(np.float32), gla_s.astype(np.float32),
            gdn_s.astype(np.float32), conv_s.astype(np.float32))
```
